# Optimizing a Trainium2 kernel written in Bass

```python
import math
import jax, jax.numpy as jnp
from jax import lax
import numpy as np

D_MODEL = 1024
BATCH = 2
SEQ = 16384
DEPTH = 2

N_META = 16
NORM_EPS = 1e-5
CONV_WIDTH = 4
LRU_WIDTH = D_MODEL // 2
LRU_HEADS = 8
LRU_HEAD_DIM = LRU_WIDTH // LRU_HEADS
LRU_C = 8.0
S5_WIDTH = D_MODEL // 2
S5_GROUP = 16
S5_GROUPS = S5_WIDTH // S5_GROUP
S5_STATE = 64
EVEN_IN = 2 * LRU_WIDTH + S5_WIDTH
EVEN_OUT = LRU_WIDTH + S5_WIDTH
SSD_INNER = 2 * D_MODEL
SSD_HEAD_DIM = 64
SSD_HEADS = SSD_INNER // SSD_HEAD_DIM
SSD_GROUPS = 8
SSD_HPG = SSD_HEADS // SSD_GROUPS
SSD_STATE = 128
SSD_CHUNK = 128
SSD_CONV_DIM = SSD_INNER + 2 * SSD_GROUPS * SSD_STATE
SSD_IN = SSD_INNER + SSD_CONV_DIM + SSD_HEADS
D_FF = 4 * D_MODEL
N_EVEN = (DEPTH + 1) // 2
N_ODD = DEPTH // 2

kernel_name = "hybrid_rglru_s5_ssd_trunk"


def rmsnorm(x, w):
    xf = x.astype(jnp.float32)
    y = xf * lax.rsqrt(jnp.mean(xf * xf, axis=-1, keepdims=True) + NORM_EPS)
    return (y * w.astype(jnp.float32)).astype(x.dtype)


def causal_dwconv(x, w, b):
    t = x.shape[1]
    xp = jnp.pad(x, ((0, 0), (CONV_WIDTH - 1, 0), (0, 0)))
    y = b
    for k in range(CONV_WIDTH):
        y = y + xp[:, k:k + t] * w[k]
    return y


def block_diag(x, w, n_blocks):
    b, t, _ = x.shape
    xb = x.reshape(b, t, n_blocks, -1)
    return jnp.einsum('btgi,gij->btgj', xb, w).reshape(b, t, -1)


def _real_combine(left, right):
    a1, b1 = left
    a2, b2 = right
    return a1 * a2, a2 * b1 + b2


def _complex_combine(left, right):
    ar1, ai1, br1, bi1 = left
    ar2, ai2, br2, bi2 = right
    return (ar2 * ar1 - ai2 * ai1, ar2 * ai1 + ai2 * ar1,
            ar2 * br1 - ai2 * bi1 + br2, ar2 * bi1 + ai2 * br1 + bi2)


def rglru(x, conv_w, conv_b, w_a, b_a, w_x, b_x, lam):
    dtype = x.dtype
    x = causal_dwconv(x, conv_w, conv_b).astype(jnp.float32)
    r = jax.nn.sigmoid(block_diag(x, w_a.astype(jnp.float32), LRU_HEADS) + b_a)
    i = jax.nn.sigmoid(block_diag(x, w_x.astype(jnp.float32), LRU_HEADS) + b_x)
    log_a = -LRU_C * r * jax.nn.softplus(-lam.astype(jnp.float32))
    a = jnp.exp(log_a)
    mult = jnp.sqrt(-jnp.expm1(2.0 * log_a))
    _, h = lax.associative_scan(_real_combine, (a, mult * (i * x)), axis=1)
    return h.astype(dtype)


def s5(u, a_re, a_im, b_re, b_im, c_re, c_im, d, log_dt, w_glu, b_glu):
    dtype = u.dtype
    bsz, t, _ = u.shape
    u = u.astype(jnp.float32)
    a_re, a_im = a_re.astype(jnp.float32), a_im.astype(jnp.float32)
    dt = jnp.exp(log_dt.astype(jnp.float32))[:, None]
    mag = jnp.exp(a_re * dt)
    ar, ai = mag * jnp.cos(a_im * dt), mag * jnp.sin(a_im * dt)
    den = a_re * a_re + a_im * a_im
    fr = ((ar - 1.0) * a_re + ai * a_im) / den
    fi = (ai * a_re - (ar - 1.0) * a_im) / den
    b_re, b_im = b_re.astype(jnp.float32), b_im.astype(jnp.float32)
    bbar_re = fr[..., None] * b_re - fi[..., None] * b_im
    bbar_im = fr[..., None] * b_im + fi[..., None] * b_re
    ug = u.reshape(bsz, t, S5_GROUPS, S5_GROUP)
    bu_re = jnp.einsum('btgi,gpi->btgp', ug, bbar_re)
    bu_im = jnp.einsum('btgi,gpi->btgp', ug, bbar_im)
    ar_t = jnp.broadcast_to(ar[None, None], (1, t, S5_GROUPS, S5_STATE))
    ai_t = jnp.broadcast_to(ai[None, None], (1, t, S5_GROUPS, S5_STATE))
    _, _, s_re, s_im = lax.associative_scan(_complex_combine, (ar_t, ai_t, bu_re, bu_im), axis=1)
    y = (jnp.einsum('btgp,gip->btgi', s_re, c_re.astype(jnp.float32))
         - jnp.einsum('btgp,gip->btgi', s_im, c_im.astype(jnp.float32)))
    y = y.reshape(bsz, t, S5_WIDTH) + d.astype(jnp.float32) * u
    y = jax.nn.gelu(y)
    y = y * jax.nn.sigmoid(block_diag(y, w_glu.astype(jnp.float32), S5_GROUPS) + b_glu)
    return y.astype(dtype)


def even_mixer(h, w_in, lru_conv_w, lru_conv_b, lru_w_a, lru_b_a, lru_w_x, lru_b_x,
               lru_lambda, s5_a_re, s5_a_im, s5_b_re, s5_b_im, s5_c_re, s5_c_im, s5_d,
               s5_log_dt, s5_w_glu, s5_b_glu, w_out):
    proj = h @ w_in
    x_lru = proj[..., :LRU_WIDTH]
    g_lru = proj[..., LRU_WIDTH:2 * LRU_WIDTH]
    u_s5 = proj[..., 2 * LRU_WIDTH:]
    y_lru = rglru(x_lru, lru_conv_w, lru_conv_b, lru_w_a, lru_b_a, lru_w_x, lru_b_x,
                  lru_lambda) * jax.nn.gelu(g_lru)
    y_s5 = s5(u_s5, s5_a_re, s5_a_im, s5_b_re, s5_b_im, s5_c_re, s5_c_im, s5_d,
              s5_log_dt, s5_w_glu, s5_b_glu)
    return jnp.concatenate([y_lru, y_s5], axis=-1) @ w_out


def ssd_chunked(x, dt, a, bm, cm):
    bsz, t = x.shape[:2]
    pad = (-t) % SSD_CHUNK

    def front_pad(v):
        return jnp.pad(v, [(0, 0), (pad, 0)] + [(0, 0)] * (v.ndim - 2))

    x, dt, bm, cm = front_pad(x), front_pad(dt), front_pad(bm), front_pad(cm)
    nc = (t + pad) // SSD_CHUNK
    L = SSD_CHUNK
    x = x.reshape(bsz, nc, L, SSD_GROUPS, SSD_HPG, SSD_HEAD_DIM)
    dt = dt.reshape(bsz, nc, L, SSD_HEADS)
    bm = bm.reshape(bsz, nc, L, SSD_GROUPS, SSD_STATE)
    cm = cm.reshape(bsz, nc, L, SSD_GROUPS, SSD_STATE)
    xdt = x * dt.reshape(bsz, nc, L, SSD_GROUPS, SSD_HPG)[..., None]
    cs = jnp.cumsum(jnp.moveaxis(dt * a, 2, 3), axis=-1)
    seg = cs[..., :, None] - cs[..., None, :]
    causal = jnp.tril(jnp.ones((L, L), dtype=bool))
    decay = jnp.where(causal, jnp.exp(jnp.where(causal, seg, 0.0)), 0.0)
    decay = decay.reshape(bsz, nc, SSD_GROUPS, SSD_HPG, L, L)
    scores = jnp.einsum('bclgn,bcsgn->bcgls', cm, bm)
    y_diag = jnp.einsum('bcgls,bcgrls,bcsgrp->bclgrp', scores, decay, xdt)
    to_end = jnp.exp(cs[..., -1:] - cs).reshape(bsz, nc, SSD_GROUPS, SSD_HPG, L)
    states = jnp.einsum('bcsgn,bcgrs,bcsgrp->bcgrpn', bm, to_end, xdt)
    chunk_decay = jnp.exp(cs[..., -1]).reshape(bsz, nc, SSD_GROUPS, SSD_HPG)

    def step(hc, inp):
        st, dec = inp
        return dec[..., None, None] * hc + st, hc

    h0 = jnp.zeros((bsz, SSD_GROUPS, SSD_HPG, SSD_HEAD_DIM, SSD_STATE), x.dtype)
    _, h_in = lax.scan(step, h0, (jnp.moveaxis(states, 1, 0), jnp.moveaxis(chunk_decay, 1, 0)))
    h_in = jnp.moveaxis(h_in, 0, 1)
    from_start = jnp.exp(cs).reshape(bsz, nc, SSD_GROUPS, SSD_HPG, L)
    y_off = jnp.einsum('bclgn,bcgrpn,bcgrl->bclgrp', cm, h_in, from_start)
    y = (y_diag + y_off).reshape(bsz, nc * L, SSD_HEADS, SSD_HEAD_DIM)
    return y[:, pad:]


def ssd_mixer(h, w_in, conv_w, conv_b, dt_bias, a_log, d, norm_w, w_out):
    dtype = h.dtype
    bsz, t, _ = h.shape
    proj = h @ w_in
    z = proj[..., :SSD_INNER]
    xbc = proj[..., SSD_INNER:SSD_INNER + SSD_CONV_DIM]
    dt_raw = proj[..., SSD_INNER + SSD_CONV_DIM:]
    xbc = jax.nn.silu(causal_dwconv(xbc, conv_w, conv_b)).astype(jnp.float32)
    xs = xbc[..., :SSD_INNER].reshape(bsz, t, SSD_HEADS, SSD_HEAD_DIM)
    bm = xbc[..., SSD_INNER:SSD_INNER + SSD_GROUPS * SSD_STATE].reshape(bsz, t, SSD_GROUPS, SSD_STATE)
    cm = xbc[..., SSD_INNER + SSD_GROUPS * SSD_STATE:].reshape(bsz, t, SSD_GROUPS, SSD_STATE)
    dt = jax.nn.softplus(dt_raw.astype(jnp.float32) + dt_bias.astype(jnp.float32))
    a = -jnp.exp(a_log.astype(jnp.float32))
    y = ssd_chunked(xs, dt, a, bm, cm) + d.astype(jnp.float32)[:, None] * xs
    g = y.reshape(bsz, t, SSD_INNER) * jax.nn.silu(z.astype(jnp.float32))
    g = g.reshape(bsz, t, SSD_GROUPS, -1)
    g = g * lax.rsqrt(jnp.mean(g * g, axis=-1, keepdims=True) + NORM_EPS)
    g = g.reshape(bsz, t, SSD_INNER) * norm_w.astype(jnp.float32)
    return g.astype(dtype) @ w_out


def sq_relu_mlp(h, w_up, w_down):
    return jnp.square(jax.nn.relu(h @ w_up)) @ w_down


def setup_inputs(seed: int = 0) -> dict:
    key = jax.random.key(seed)
    ks = jax.random.split(key, 40)
    nrm = lambda k, shape, s: jax.random.normal(k, shape, jnp.float32) * s
    NE, NO = N_EVEN, N_ODD
    u = jax.random.uniform(ks[12], (NE, LRU_WIDTH), jnp.float32, 0.9, 0.999)
    sg = u ** (1.0 / LRU_C)
    lru_lambda = jnp.log(sg) - jnp.log1p(-sg)
    n_idx = jnp.arange(S5_STATE, dtype=jnp.float32)
    s5_a_re = -0.5 + nrm(ks[13], (NE, S5_GROUPS, S5_STATE), 0.01)
    s5_a_im = math.pi * n_idx + nrm(ks[14], (NE, S5_GROUPS, S5_STATE), 0.01)
    s5_log_dt = jax.random.uniform(ks[20], (NE, S5_GROUPS), jnp.float32,
                                   math.log(0.001), math.log(0.1))
    dt0 = jnp.exp(jax.random.uniform(ks[27], (NO, SSD_HEADS), jnp.float32,
                                     math.log(0.001), math.log(0.1)))
    ssd_dt_bias = dt0 + jnp.log(-jnp.expm1(-dt0))
    ssd_a_log = jnp.log(jax.random.uniform(ks[28], (NO, SSD_HEADS), jnp.float32, 1.0, 16.0))
    return {
        "x": nrm(ks[0], (BATCH, SEQ, D_MODEL), 1.0),
        "meta_tokens": nrm(ks[1], (N_META, D_MODEL), 1.0),
        "norm_mix": 1.0 + nrm(ks[2], (DEPTH, D_MODEL), 0.01),
        "norm_mlp": 1.0 + nrm(ks[3], (DEPTH, D_MODEL), 0.01),
        "norm_final": 1.0 + nrm(ks[4], (D_MODEL,), 0.01),
        "ev_w_in": nrm(ks[5], (NE, D_MODEL, EVEN_IN), D_MODEL ** -0.5),
        "lru_conv_w": nrm(ks[6], (NE, CONV_WIDTH, LRU_WIDTH), CONV_WIDTH ** -0.5),
        "lru_conv_b": nrm(ks[7], (NE, LRU_WIDTH), 0.01),
        "lru_w_a": nrm(ks[8], (NE, LRU_HEADS, LRU_HEAD_DIM, LRU_HEAD_DIM), LRU_HEAD_DIM ** -0.5),
        "lru_b_a": nrm(ks[9], (NE, LRU_WIDTH), 0.01),
        "lru_w_x": nrm(ks[10], (NE, LRU_HEADS, LRU_HEAD_DIM, LRU_HEAD_DIM), LRU_HEAD_DIM ** -0.5),
        "lru_b_x": nrm(ks[11], (NE, LRU_WIDTH), 0.01),
        "lru_lambda": lru_lambda,
        "s5_a_re": s5_a_re,
        "s5_a_im": s5_a_im,
        "s5_b_re": nrm(ks[15], (NE, S5_GROUPS, S5_STATE, S5_GROUP), (2 * S5_GROUP) ** -0.5),
        "s5_b_im": nrm(ks[16], (NE, S5_GROUPS, S5_STATE, S5_GROUP), (2 * S5_GROUP) ** -0.5),
        "s5_c_re": nrm(ks[17], (NE, S5_GROUPS, S5_GROUP, S5_STATE), (2 * S5_STATE) ** -0.5),
        "s5_c_im": nrm(ks[18], (NE, S5_GROUPS, S5_GROUP, S5_STATE), (2 * S5_STATE) ** -0.5),
        "s5_d": nrm(ks[19], (NE, S5_WIDTH), 1.0),
        "s5_log_dt": s5_log_dt,
        "s5_w_glu": nrm(ks[21], (NE, S5_GROUPS, S5_GROUP, S5_GROUP), S5_GROUP ** -0.5),
        "s5_b_glu": nrm(ks[22], (NE, S5_WIDTH), 0.01),
        "ev_w_out": nrm(ks[23], (NE, EVEN_OUT, D_MODEL), EVEN_OUT ** -0.5),
        "ssd_w_in": nrm(ks[24], (NO, D_MODEL, SSD_IN), D_MODEL ** -0.5),
        "ssd_conv_w": nrm(ks[25], (NO, CONV_WIDTH, SSD_CONV_DIM), CONV_WIDTH ** -0.5),
        "ssd_conv_b": nrm(ks[26], (NO, SSD_CONV_DIM), 0.01),
        "ssd_dt_bias": ssd_dt_bias,
        "ssd_a_log": ssd_a_log,
        "ssd_d": 1.0 + nrm(ks[29], (NO, SSD_HEADS), 0.01),
        "ssd_norm": 1.0 + nrm(ks[30], (NO, SSD_INNER), 0.01),
        "ssd_w_out": nrm(ks[31], (NO, SSD_INNER, D_MODEL), SSD_INNER ** -0.5),
        "mlp_w_up": nrm(ks[32], (DEPTH, D_MODEL, D_FF), D_MODEL ** -0.5),
        "mlp_w_down": nrm(ks[33], (DEPTH, D_FF, D_MODEL), D_FF ** -0.5),
    }


def reference(x, meta_tokens, norm_mix, norm_mlp, norm_final, ev_w_in, lru_conv_w,
              lru_conv_b, lru_w_a, lru_b_a, lru_w_x, lru_b_x, lru_lambda, s5_a_re, s5_a_im,
              s5_b_re, s5_b_im, s5_c_re, s5_c_im, s5_d, s5_log_dt, s5_w_glu, s5_b_glu,
              ev_w_out, ssd_w_in, ssd_conv_w, ssd_conv_b, ssd_dt_bias, ssd_a_log, ssd_d,
              ssd_norm, ssd_w_out, mlp_w_up, mlp_w_down):
    bsz = x.shape[0]
    meta = jnp.broadcast_to(meta_tokens[None].astype(x.dtype), (bsz, N_META, x.shape[-1]))
    h = jnp.concatenate([meta, x], axis=1)
    for layer in range(DEPTH):
        i = layer // 2
        hn = rmsnorm(h, norm_mix[layer])
        if layer % 2 == 0:
            mix = even_mixer(hn, ev_w_in[i], lru_conv_w[i], lru_conv_b[i], lru_w_a[i],
                             lru_b_a[i], lru_w_x[i], lru_b_x[i], lru_lambda[i], s5_a_re[i],
                             s5_a_im[i], s5_b_re[i], s5_b_im[i], s5_c_re[i], s5_c_im[i],
                             s5_d[i], s5_log_dt[i], s5_w_glu[i], s5_b_glu[i], ev_w_out[i])
        else:
            mix = ssd_mixer(hn, ssd_w_in[i], ssd_conv_w[i], ssd_conv_b[i], ssd_dt_bias[i],
                            ssd_a_log[i], ssd_d[i], ssd_norm[i], ssd_w_out[i])
        h = h + mix
        h = h + sq_relu_mlp(rmsnorm(h, norm_mlp[layer]), mlp_w_up[layer], mlp_w_down[layer])
    h = rmsnorm(h, norm_final)
    return h[:, N_META:]
```

```python
from concourse.bass_utils import run_bass_kernel_spmd


import numpy as np
import concourse.bass as bass
import concourse.mybir as mybir

F32 = mybir.dt.float32
BF16 = mybir.dt.bfloat16
AF = mybir.ActivationFunctionType
ALU = mybir.AluOpType
AX = mybir.AxisListType

ENGS = ["tensor", "vector", "scalar", "gpsimd", "sync"]


class Prog:
    def __init__(self, nc):
        self.nc = nc
        self.ins = {e: [] for e in ENGS}
        self.res = {}
        self.dma_sem = {}
        self.waited = {e: {} for e in ENGS}
        self.esems = [{e: nc.alloc_semaphore("prg0_" + e) for e in ENGS}]
        self.gen_start = {e: [0] for e in ENGS}
        self.pdi = 0

    def _deps(self, eng, reads, writes):
        deps = []
        for r in reads:
            st = self.res.get(r)
            if st and st["w"] is not None:
                deps.append(st["w"])
        for w in writes:
            st = self.res.get(w)
            if st:
                if st["w"] is not None:
                    deps.append(st["w"])
                deps.extend(st["r"])
        out = []
        wd = self.waited[eng]
        for d in deps:
            if d[0] == "e":
                if d[1] == eng and eng in ("tensor", "sync"):
                    continue
                key = ("e", d[1])
                if wd.get(key, -1) >= d[2]:
                    continue
                wd[key] = d[2]
                out.append(d)
            else:
                key = ("d", d[1])
                if wd.get(key, -1) >= d[2]:
                    continue
                wd[key] = d[2]
                out.append(d)
        return out

    def _commit(self, me, reads, writes):
        for r in reads:
            st = self.res.setdefault(r, {"w": None, "r": []})
            st["r"].append(me)
        for w in writes:
            self.res[w] = {"w": me, "r": []}

    def op(self, eng, name, reads, writes, *args, **kw):
        fn = (lambda e, name=name, args=args, kw=kw: getattr(e, name)(*args, **kw))
        ex = [r for r in reads if isinstance(r, tuple) and r[0] == "ps"]
        if ex:
            reads = [r for r in reads if r not in ex]
            writes = list(writes) + [r for r in ex if r not in writes]
        waits = self._deps(eng, reads, writes)
        idx = len(self.ins[eng])
        self.ins[eng].append({"fn": fn, "waits": waits, "dma": None})
        self._commit(("e", eng, idx), reads, writes)

    def mm(self, out, lhsT, rhs, start, stop, reads, writes, **kw):
        self.op("tensor", "matmul", reads, writes, out, lhsT=lhsT, rhs=rhs, start=start, stop=stop, **kw)

    def act(self, out, in_, func, reads, writes, **kw):
        self.op("scalar", "activation", reads, writes, out=out, in_=in_, func=func, **kw)

    def dma(self, eng, out, in_, reads=(), writes=(), sem=None):
        if sem is None:
            sem = ("pd", self.pdi)
            self.pdi += 1
        if sem not in self.dma_sem:
            self.dma_sem[sem] = [self.nc.alloc_semaphore("d%d" % len(self.dma_sem)), 0]
        waits = self._deps(eng, reads, writes)
        self.dma_sem[sem][1] += 16
        val = self.dma_sem[sem][1]
        h = self.dma_sem[sem][0]
        self.ins[eng].append({"fn": (lambda e, out=out, in_=in_: e.dma_start(out=out, in_=in_)),
                              "waits": waits, "dma": h})
        self._commit(("d", sem, val), reads, writes)

    def barrier(self):
        last = {}
        for e in ENGS:
            for idx in range(len(self.ins[e]) - 1, -1, -1):
                if self.ins[e][idx]["dma"] is None and self.ins[e][idx]["fn"] is not None:
                    last[e] = idx
                    break
        for eng in ENGS:
            waits = []
            for key, (h, tot) in self.dma_sem.items():
                if self.waited[eng].get(("d", key), -1) < tot:
                    waits.append(("d", key, tot))
                    self.waited[eng][("d", key)] = tot
            for e, idx in last.items():
                if e != eng and self.waited[eng].get(("e", e), -1) < idx:
                    waits.append(("e", e, idx))
                    self.waited[eng][("e", e)] = idx
            self.ins[eng].append({"fn": None, "waits": waits, "dma": None})

    def phase(self):
        self.barrier()
        g = len(self.esems)
        self.esems.append({e: self.nc.alloc_semaphore("prg%d_%s" % (g, e)) for e in ENGS})
        for e in ENGS:
            self.gen_start[e].append(len(self.ins[e]))
        self.pdi = 0

    def finish(self, eng="sync"):
        waits = []
        for key, (h, tot) in self.dma_sem.items():
            if self.waited[eng].get(("d", key), -1) < tot:
                waits.append(("d", key, tot))
        for e in ENGS:
            if e != eng and self.ins[e]:
                for idx in range(len(self.ins[e]) - 1, -1, -1):
                    if self.ins[e][idx]["dma"] is None and self.ins[e][idx]["fn"] is not None:
                        waits.append(("e", e, idx))
                        break
        self.ins[eng].append({"fn": None, "waits": waits, "dma": None})

    def emit(self):
        nc = self.nc
        needed = {e: set() for e in ENGS}
        for e in ENGS:
            for ins in self.ins[e]:
                for d in ins["waits"]:
                    if d[0] == "e":
                        needed[d[1]].add(d[2])
        import bisect
        val = {}
        gen_of = {}
        for e in ENGS:
            run = 0
            g = 0
            starts = self.gen_start[e]
            for idx in range(len(self.ins[e])):
                while g + 1 < len(starts) and idx >= starts[g + 1]:
                    g += 1
                    run = 0
                if idx in needed[e]:
                    run += 1
                    val[(e, idx)] = run
                    gen_of[(e, idx)] = g
        stats = {e: (len(self.ins[e]), len(needed[e])) for e in ENGS}
        ecount = {}
        dval = {k: 0 for k in self.dma_sem}
        hmap = {id(v[0]): k for k, v in self.dma_sem.items()}
        ptr = {e: 0 for e in ENGS}
        progress = True
        while progress:
            progress = False
            for e in ENGS:
                while ptr[e] < len(self.ins[e]):
                    ins = self.ins[e][ptr[e]]
                    ok = True
                    for d in ins["waits"]:
                        if d[0] == "e":
                            if ecount.get((d[1], gen_of[(d[1], d[2])]), 0) < val[(d[1], d[2])]:
                                ok = False
                                break
                        else:
                            if dval[d[1]] < d[2]:
                                ok = False
                                break
                    if not ok:
                        break
                    if ins["dma"] is not None:
                        dval[hmap[id(ins["dma"])]] += 16
                    elif ins["fn"] is not None and ptr[e] in needed[e]:
                        kk = (e, gen_of[(e, ptr[e])])
                        ecount[kk] = ecount.get(kk, 0) + 1
                    ptr[e] += 1
                    progress = True
        for e in ENGS:
            if ptr[e] < len(self.ins[e]):
                raise RuntimeError("DEADLOCK in dry run: engine %s stuck at %d/%d waits=%s" % (
                    e, ptr[e], len(self.ins[e]), self.ins[e][ptr[e]]["waits"]))

        def mk(e):
            def body(engobj):
                for idx, ins in enumerate(self.ins[e]):
                    for d in ins["waits"]:
                        if d[0] == "e":
                            engobj.wait_ge(self.esems[gen_of[(d[1], d[2])]][d[1]], val[(d[1], d[2])])
                        else:
                            engobj.wait_ge(self.dma_sem[d[1]][0], d[2])
                    if ins["fn"] is None:
                        continue
                    r = ins["fn"](engobj)
                    if ins["dma"] is not None:
                        r.then_inc(ins["dma"], 16)
                    elif idx in needed[e]:
                        r.then_inc(self.esems[gen_of[(e, idx)]][e], 1)
            return body

        with nc.Block() as block:
            block.tensor(mk("tensor"))
            block.vector(mk("vector"))
            block.scalar(mk("scalar"))
            block.gpsimd(mk("gpsimd"))
            block.sync(mk("sync"))
        return stats


import math
import numpy as np
import concourse.bass as bass
import concourse.mybir as mybir

D = 1024
DFF = 4096
EPS = 1e-5
BS = 512
PI = math.pi
NCOL = 1544


class Ctx:
    def __init__(self, nc, pg):
        self.nc = nc
        self.pg = pg
        self.words = 53000
        self.arena = nc.alloc_sbuf_tensor("arena", [128, self.words], F32)
        self.off = 0
        self.ps = [nc.alloc_psum_tensor("ps%d" % i, [128, 512], F32) for i in range(8)]
        self.psb = self.ps[4][:, :].bitcast(BF16)

    def reset(self):
        self.off = 0

    def sb(self, name, shape, dt=F32):
        n = 1
        for d_ in shape[1:]:
            n *= d_
        nw = n if dt == F32 else (n + 1) // 2
        nw = (nw + 7) // 8 * 8
        assert self.off + nw <= self.words, ("arena overflow", name, self.off, nw)
        base = self.arena[0:shape[0], self.off:self.off + nw]
        self.off += nw
        if dt != F32:
            base = base.bitcast(dt)
        ap = base[:, 0:n]
        if len(shape) == 3:
            ap = ap.rearrange("p (a b) -> p a b", a=shape[1])
        elif len(shape) == 4:
            ap = ap.rearrange("p (a b c) -> p a b c", a=shape[1], b=shape[2])
        return ap


def blocks_of(nt_total, pre=16, bs=512):
    bl = [(0, pre)] if pre else []
    t = pre
    while t < nt_total:
        n = min(bs, nt_total - t)
        bl.append((t, n))
        t += n
    return bl

def rms_rstd(pg, x_ap, sq_ap_fn, ones_bf, ps_ap, rstd_ap, D_, rkeys, sqkey, pskey, rstdkey, kt, eps_ap):
    for k in range(kt):
        pg.act(sq_ap_fn(k), x_ap[:, k, :], AF.Square, rkeys, [sqkey])
    for k in range(kt):
        pg.mm(ps_ap, ones_bf, sq_ap_fn(k), k == 0, k == kt - 1, [sqkey, "ones"], [pskey])
    pg.act(rstd_ap, ps_ap, AF.Sqrt, [pskey, "epsc"], [rstdkey], scale=1.0 / D_, bias=eps_ap)
    pg.op("vector", "reciprocal", [rstdkey], [rstdkey], out=rstd_ap, in_=rstd_ap)


def gelu_tanh(pg, eng, out_ap, x_ap, t1, t2, rk, wk, tk):
    pg.op(eng, "tensor_tensor", rk, [tk + "1"], out=t1, in0=x_ap, in1=x_ap, op=ALU.mult)
    pg.op(eng, "tensor_scalar", [tk + "1"], [tk + "1"], out=t1, in0=t1, scalar1=0.044715, scalar2=1.0,
          op0=ALU.mult, op1=ALU.add)
    pg.op(eng, "tensor_tensor", rk + [tk + "1"], [tk + "2"], out=t2, in0=t1, in1=x_ap, op=ALU.mult)
    pg.act(t1, t2, AF.Sigmoid, [tk + "2"], [tk + "1"], scale=1.5957691216057308)
    pg.op(eng, "tensor_tensor", rk + [tk + "1"], wk, out=out_ap, in0=x_ap, in1=t1, op=ALU.mult)


def sincos_tables(pg, nc, arg_ap, sin_out, cos_out, tmp_ap, key, negpi, tmp2_ap=None):
    MAGIC = 12582912.0
    SC = 2 * PI * (1.0 - 1e-6)
    if tmp2_ap is None:
        tmp2_ap = cos_out
    for off, outp, nm in ((0.0, sin_out, "sin"), (0.25, cos_out, "cos")):
        pg.op("vector", "tensor_scalar", [key + "arg", key + "sin"], [key + "tmp"], out=tmp_ap, in0=arg_ap, scalar1=1.0 / (2 * PI),
              scalar2=off, op0=ALU.mult, op1=ALU.add)
        pg.op("vector", "tensor_scalar", [key + "tmp"], [key + nm], out=outp, in0=tmp_ap, scalar1=MAGIC, scalar2=None, op0=ALU.add)
        pg.op("vector", "tensor_scalar", [key + nm], [key + nm], out=outp, in0=outp, scalar1=-MAGIC, scalar2=None, op0=ALU.add)
        pg.op("vector", "tensor_tensor", [key + "tmp", key + nm], [key + "tmp"], out=tmp_ap, in0=tmp_ap, in1=outp, op=ALU.subtract)
        pg.act(outp, tmp_ap, AF.Sin, [key + "tmp"], [key + nm], scale=SC)


def lam_parts(pg, nc, are, aim, ldt, shape, key, negpi, pool=None, sbf=None):
    al = lambda n: sbf(key + n, shape, F32)
    names = ["dt", "x", "mag", "th", "sn", "cs", "tmp", "ar1", "ai", "den", "fr", "fi", "t2"]
    if pool is None:
        tiles = [al(n) for n in names]
    else:
        tiles = pool[:len(names)]
    dt, x, mag, th, sn, cs_, tmp, ar1, ai, den, fr, fi, t2 = tiles
    sl = tuple(slice(None) for _ in shape)
    V = "vector"
    pg.act(dt[sl], ldt[sl], AF.Exp, [key + "in"], [key + "dt"])
    pg.op(V, "tensor_tensor", [key + "in", key + "dt"], [key + "x"], out=x[sl], in0=are[sl], in1=dt[sl], op=ALU.mult)
    pg.op(V, "tensor_scalar", [key + "x"], [key + "mag"], out=mag[sl], in0=x[sl], scalar1=1.0 / 5, scalar2=1.0, op0=ALU.mult, op1=ALU.add)
    for dv in (4.0, 3.0, 2.0, 1.0):
        pg.op(V, "tensor_tensor", [key + "x", key + "mag"], [key + "mag"], out=mag[sl], in0=mag[sl], in1=x[sl], op=ALU.mult)
        pg.op(V, "tensor_scalar", [key + "mag"], [key + "mag"], out=mag[sl], in0=mag[sl], scalar1=1.0 / dv, scalar2=1.0, op0=ALU.mult, op1=ALU.add)
    pg.op(V, "tensor_tensor", [key + "in", key + "dt"], [key + "arg"], out=th[sl], in0=aim[sl], in1=dt[sl], op=ALU.mult)
    sincos_tables(pg, nc, th[sl], sn[sl], cs_[sl], tmp[sl], key, negpi)
    pg.op(V, "tensor_tensor", [key + "mag", key + "cos"], [key + "ar1"], out=ar1[sl], in0=mag[sl], in1=cs_[sl], op=ALU.mult)
    pg.op(V, "tensor_scalar", [key + "ar1"], [key + "ar1"], out=ar1[sl], in0=ar1[sl], scalar1=-1.0, scalar2=None, op0=ALU.add)
    pg.op(V, "tensor_tensor", [key + "mag", key + "sin"], [key + "ai"], out=ai[sl], in0=mag[sl], in1=sn[sl], op=ALU.mult)
    pg.op(V, "tensor_tensor", [key + "in"], [key + "den"], out=den[sl], in0=are[sl], in1=are[sl], op=ALU.mult)
    pg.op(V, "tensor_tensor", [key + "in", key + "tmp"], [key + "tmp"], out=tmp[sl], in0=aim[sl], in1=aim[sl], op=ALU.mult)
    pg.op(V, "tensor_tensor", [key + "den", key + "tmp"], [key + "den"], out=den[sl], in0=den[sl], in1=tmp[sl], op=ALU.add)
    pg.op(V, "reciprocal", [key + "den"], [key + "den"], out=den[sl], in_=den[sl])
    pg.op(V, "tensor_tensor", [key + "ar1", key + "in"], [key + "fr"], out=fr[sl], in0=ar1[sl], in1=are[sl], op=ALU.mult)
    pg.op(V, "tensor_tensor", [key + "ai", key + "in", key + "tmp"], [key + "tmp"], out=tmp[sl], in0=ai[sl], in1=aim[sl], op=ALU.mult)
    pg.op(V, "tensor_tensor", [key + "fr", key + "tmp"], [key + "fr"], out=fr[sl], in0=fr[sl], in1=tmp[sl], op=ALU.add)
    pg.op(V, "tensor_tensor", [key + "fr", key + "den"], [key + "fr"], out=fr[sl], in0=fr[sl], in1=den[sl], op=ALU.mult)
    pg.op(V, "tensor_tensor", [key + "ai", key + "in"], [key + "fi"], out=fi[sl], in0=ai[sl], in1=are[sl], op=ALU.mult)
    pg.op(V, "tensor_tensor", [key + "ar1", key + "in"], [key + "t2"], out=t2[sl], in0=ar1[sl], in1=aim[sl], op=ALU.mult)
    pg.op(V, "tensor_tensor", [key + "fi", key + "t2"], [key + "fi"], out=fi[sl], in0=fi[sl], in1=t2[sl], op=ALU.subtract)
    pg.op(V, "tensor_tensor", [key + "fi", key + "den"], [key + "fi"], out=fi[sl], in0=fi[sl], in1=den[sl], op=ALU.mult)
    return mag, th, fr, fi


def phase_mlp(ctx, h_in, parts, w_up, w_dn, nw, nf, h_out, NT, final_norm, BS):
    nc, pg, sb = ctx.nc, ctx.pg, ctx.sb
    nparts = len(parts)
    blocks = None
    hin_v = h_in.rearrange("(k p) t -> p k t", p=128)
    hout_v = h_out.rearrange("(k p) t -> p k t", p=128)
    parts_v = [p_.rearrange("(k p) t -> p k t", p=128) for p_ in parts]
    wup = sb("wup", [128, 8, DFF], BF16)
    wdn = sb("wdn", [128, 32, D], BF16)
    nw_t = sb("nw_t", [128, 8], F32)
    nf_t = sb("nf_t", [128, 8], F32)
    ones = sb("ones", [128, 128], BF16)
    epsc = sb("epsc", [128, 1], F32)
    xb = [sb("xb%d" % i, [128, 8, BS], F32) for i in range(2)]
    hn = sb("hn", [128, 8, BS], BF16)
    hid = sb("hid", [128, 32, BS], BF16)
    rstd = sb("rstd", [128, BS], F32)
    tmp = [sb("tmp%d" % i, [128, BS], BF16) for i in range(4)]
    ps = ctx.ps
    pb = sb("pb", [128, 8, BS], F32) if nparts else None

    pg.op("vector", "memset", [], ["ones"], ones[:, :], 1.0)
    pg.op("vector", "memset", [], ["epsc"], epsc[:, :], EPS)
    pg.dma("sync", nw_t[:, :], nw, writes=["nw"])
    pg.dma("sync", nf_t[:, :], nf, writes=["nf"])
    wup_v = w_up.rearrange("(k p) n -> p k n", p=128)
    wdn_v = w_dn.rearrange("(m p) d -> p m d", p=128)
    for k in range(8):
        pg.dma("gpsimd", wup[:, k, :], wup_v[:, k, :], writes=[("wup", k)])
    for j in range(8):
        pg.dma("gpsimd", wdn[:, 4 * j:4 * j + 4, :], wdn_v[:, 4 * j:4 * j + 4, :], writes=[("wdn", j)])

    if blocks is None:
        blocks = blocks_of(NT, pre=0, bs=BS)
    psi = 0
    for bi, (t0, nt) in enumerate(blocks):
        s = bi % 2
        X = xb[s]
        xk = ("xb", s)
        pg.dma("sync", X[:, :, 0:nt], hin_v[:, :, t0:t0 + nt], writes=[xk], sem=("xsem", s))
        for i in range(nparts):
            pg.dma("sync", pb[:, :, 0:nt], parts_v[i][:, :, t0:t0 + nt], writes=["pb"], sem=("pbsem",))
            pg.op("gpsimd" if i % 2 else "vector", "tensor_tensor", [xk, "pb"], [xk], out=X[:, :, 0:nt], in0=X[:, :, 0:nt],
                  in1=pb[:, :, 0:nt], op=ALU.add)
        rms_rstd(pg, X[:, :, 0:nt], lambda k: hn[:, k, 0:nt], ones[:, :], ps[7][:, 0:nt], rstd[:, 0:nt], D,
                 [xk], "hn", ("ps", 7), "rstd", 8, epsc[:, 0:1])
        for k in range(8):
            pg.op("vector", "scalar_tensor_tensor", [xk, "rstd", "nw"], ["hn"],
                  out=hn[:, k, 0:nt], in0=X[:, k, 0:nt], scalar=nw_t[:, k:k + 1], in1=rstd[:, 0:nt],
                  op0=ALU.mult, op1=ALU.mult)
        for m in range(32):
            p = psi % 6
            psi += 1
            for k in range(8):
                pg.mm(ps[p][:, 0:nt], wup[:, k, m * 128:(m + 1) * 128], hn[:, k, 0:nt], k == 0, k == 7,
                      ["hn", ("wup", k)], [("ps", p)])
            tq = m % 4
            pg.act(tmp[tq][:, 0:nt], ps[p][:, 0:nt], AF.Relu, [("ps", p)], [("tmp", tq)])
            pg.op("vector" if m % 2 == 0 else "gpsimd", "tensor_tensor", [("tmp", tq)], [("hid", m)],
                  out=hid[:, m, 0:nt], in0=tmp[tq][:, 0:nt], in1=tmp[tq][:, 0:nt], op=ALU.mult)
        for d in range(8):
            p = psi % 6
            psi += 1
            for m in range(32):
                pg.mm(ps[p][:, 0:nt], wdn[:, m, d * 128:(d + 1) * 128], hid[:, m, 0:nt], m == 0, m == 31,
                      [("hid", m), ("wdn", m // 4)], [("ps", p)])
            pg.op("vector", "tensor_tensor", [("ps", p), xk], [xk],
                  out=X[:, d, 0:nt], in0=X[:, d, 0:nt], in1=ps[p][:, 0:nt], op=ALU.add)
        if final_norm:
            rms_rstd(pg, X[:, :, 0:nt], lambda k: hn[:, k, 0:nt], ones[:, :], ps[7][:, 0:nt], rstd[:, 0:nt], D,
                     [xk], "hn", ("ps", 7), "rstd", 8, epsc[:, 0:1])
            for k in range(8):
                pg.op("vector", "scalar_tensor_tensor", [xk, "rstd", "nf"], [xk],
                      out=X[:, k, 0:nt], in0=X[:, k, 0:nt], scalar=nf_t[:, k:k + 1], in1=rstd[:, 0:nt],
                      op0=ALU.mult, op1=ALU.mult)
        pg.dma("sync", hout_v[:, :, t0:t0 + nt], X[:, :, 0:nt], reads=[xk], sem=("osem", s))


def phase_mix0(ctx, x_in, P_, part, T, blocks=None, dbg=9):
    nc, pg, sb = ctx.nc, ctx.pg, ctx.sb
    win_d, nw_d, vecs_d, wa_d, wx_d, wglu_d, wout_d, scol_d, srow_d, bT_d, cT_d, ident_d, kidx_d = [
        P_[k] for k in ["win", "nw", "vecs", "wa", "wx", "wglu", "wout", "scol", "srow", "bT", "cT", "ident", "kidx"]]
    xin_v = x_in.rearrange("(k p) t -> p k t", p=128)
    part_v = part.rearrange("(k p) t -> p k t", p=128)
    win = sb("win_s", [128, 8, 384], BF16)
    wout = sb("wout_s", [128, 2, D], BF16)
    nw_t = sb("nw_s", [128, 8]); vecs = sb("vecs_s", [128, 16])
    wa32 = sb("wa32", [128, 128]); wx32 = sb("wx32", [128, 128]); wglu32 = sb("wglu32", [128, 128])
    wa = sb("wa_s", [128, 128], BF16); wx = sb("wx_s", [128, 128], BF16); wglu = sb("wglu_s", [128, 128], BF16)
    scol = sb("scol_s", [128, 3, 4]); srow = sb("srow_s", [128, 3, 512])
    bT = sb("bT_s", [128, 2, 512]); cT = sb("cT_s", [128, 2, 512])
    ident = sb("ident_s", [128, 128]); kidx = sb("kidx_s", [128, BS])
    ones = sb("ones", [128, 128], BF16); epsc = sb("epsc", [128, 1]); negpi = sb("negpi", [128, 1]); onec = sb("onec", [128, 1])
    V = "vector"
    pg.op(V, "memset", [], ["ones"], ones[:, :], 1.0)
    pg.op(V, "memset", [], ["epsc"], epsc[:, :], EPS)
    pg.op(V, "memset", [], ["negpi"], negpi[:, :], -PI)
    pg.op(V, "memset", [], ["onec"], onec[:, :], 1.0)
    for t_, d_, k_ in [(nw_t, nw_d, "nw"), (vecs, vecs_d, "vecs"), (wa32, wa_d, "wa32"), (wx32, wx_d, "wx32"),
                       (wglu32, wglu_d, "wglu32"), (scol, scol_d, "colin"), (srow, srow_d, "rowin"),
                       (bT, bT_d, "bT"), (cT, cT_d, "cT"), (ident, ident_d, "ident"), (kidx, kidx_d, "kidx")]:
        sl = tuple(slice(None) for _ in t_.shape)
        pg.dma("sync", t_[sl], d_, writes=[k_])
    pg.dma("gpsimd", win[:, :, :], win_d.rearrange("(k p) n -> p k n", p=128), writes=["win"])
    pg.dma("gpsimd", wout[:, :, :], wout_d.rearrange("(k p) n -> p k n", p=128), writes=["wout"])
    pg.op(V, "tensor_copy", ["wa32"], ["wa"], out=wa[:, :], in_=wa32[:, :])
    pg.op(V, "tensor_copy", ["wx32"], ["wx"], out=wx[:, :], in_=wx32[:, :])
    pg.op(V, "tensor_copy", ["wglu32"], ["wglu"], out=wglu[:, :], in_=wglu32[:, :])
    cdiag = sb("cdiag", [128, 4, 128], BF16)
    for k in range(4):
        pg.op(V, "tensor_scalar", ["ident", "vecs"], ["cdiag"], out=cdiag[:, k, :], in0=ident[:, :],
              scalar1=vecs[:, k:k + 1], scalar2=None, op0=ALU.mult)
    lc = sb("lc", [128, 1])
    pg.act(lc[:, :], vecs[:, 7:8], AF.Exp, ["vecs"], ["lc"], scale=-1.0)
    pg.act(lc[:, :], lc[:, :], AF.Ln, ["lc", "onec"], ["lc"], bias=onec[:, 0:1])
    pg.op(V, "tensor_scalar", ["lc"], ["lc"], out=lc[:, :], in0=lc[:, :], scalar1=-8.0, scalar2=None, op0=ALU.mult)
    NB = 32
    bufs = [sb("w%d" % i, [128, BS]) for i in range(NB)]
    c_are = sb("c_are", [128, 4]); c_aim = sb("c_aim", [128, 4]); c_ldt = sb("c_ldt", [128, 4])
    for i, t_ in enumerate([c_are, c_aim, c_ldt]):
        pg.op(V, "tensor_copy", ["colin"], ["colin2"], out=t_[:, :], in_=scol[:, i, :])
    pg.op(V, "tensor_copy", ["colin2"], ["colin"], out=c_ldt[:, :], in_=c_ldt[:, :])
    magc, thc, _, _ = lam_parts(pg, nc, c_are, c_aim, c_ldt, [128, 4], "col", negpi[:, 0:1], sbf=sb)
    r_are, r_aim, r_ldt = bufs[13], bufs[14], bufs[15]
    for i, t_ in enumerate([r_are, r_aim, r_ldt]):
        pg.op(V, "tensor_copy", ["rowin"], ["rowin2"], out=t_[:, :], in_=srow[:, i, :])
    pg.op(V, "tensor_copy", ["rowin2"], ["rowin"], out=r_ldt[:, :], in_=r_ldt[:, :])
    _, _, frr, fir = lam_parts(pg, nc, r_are, r_aim, r_ldt, [128, 512], "row", negpi[:, 0:1], pool=bufs)
    bbT = sb("bbT", [128, 2, 512], BF16)
    tA, tB = bufs[16], bufs[17]
    pg.op(V, "tensor_tensor", ["bT", "rowfr"], ["tA"], out=tA[:, :], in0=bT[:, 0, :], in1=frr[:, :], op=ALU.mult)
    pg.op(V, "tensor_tensor", ["bT", "rowfi"], ["tB"], out=tB[:, :], in0=bT[:, 1, :], in1=fir[:, :], op=ALU.mult)
    pg.op(V, "tensor_tensor", ["tA", "tB"], ["bbT"], out=bbT[:, 0, :], in0=tA[:, :], in1=tB[:, :], op=ALU.subtract)
    pg.op(V, "tensor_tensor", ["bT", "rowfi", "tA"], ["tA"], out=tA[:, :], in0=bT[:, 0, :], in1=fir[:, :], op=ALU.mult)
    pg.op(V, "tensor_tensor", ["bT", "rowfr", "tB"], ["tB"], out=tB[:, :], in0=bT[:, 1, :], in1=frr[:, :], op=ALU.mult)
    pg.op(V, "tensor_tensor", ["tA", "tB"], ["bbT"], out=bbT[:, 1, :], in0=tA[:, :], in1=tB[:, :], op=ALU.add)
    ccT = sb("ccT", [128, 2, 512], BF16)
    pg.op(V, "tensor_copy", ["cT"], ["ccT"], out=ccT[:, 0, :], in_=cT[:, 0, :])
    pg.op(V, "tensor_scalar", ["cT", "ccT"], ["ccT"], out=ccT[:, 1, :], in0=cT[:, 1, :], scalar1=-1.0, scalar2=None, op0=ALU.mult)
    cosT = sb("cosT", [128, 4, BS]); sinT = sb("sinT", [128, 4, BS]); decT = sb("decT", [128, 4, BS])
    for j in range(4):
        key = "tb%d" % j
        ta = bufs[18 + 2 * j]
        pg.op(V, "tensor_scalar", ["kidx", "colarg"], [key + "arg"], out=ta[:, :], in0=kidx[:, :],
              scalar1=thc[:, j:j + 1], scalar2=None, op0=ALU.mult)
        tt = bufs[19 + 2 * j]
        sincos_tables(pg, nc, ta[:, :], sinT[:, j, :], cosT[:, j, :], tt[:, :], key, negpi[:, 0:1])
        pg.op(V, "tensor_scalar", ["kidx", "colmag"], ["decT"], out=decT[:, j, :], in0=kidx[:, :], scalar1=0.0,
              scalar2=magc[:, j:j + 1], op0=ALU.mult, op1=ALU.add)
    tabk = ["decT"] + ["tb%dsin" % j for j in range(4)] + ["tb%dcos" % j for j in range(4)]

    xb = [sb("xb%d" % i, [128, 8, BS]) for i in range(2)]
    hn = sb("hn", [128, 8, BS], BF16)
    rstd = sb("rstd", [128, BS])
    xlp = sb("xlp", [128, 4 + BS], BF16)
    bb16 = [sb("h%d" % i, [128, BS], BF16) for i in range(12)]
    Y = sb("Y", [128, 2, BS], BF16)
    ob = [sb("ob0", [128, 8, BS])] * 2
    hst = sb("hst", [128, 2])
    sst = sb("sst", [128, 4, 2, 2])
    stmp = sb("stmp", [128, 4, 4])
    ps = ctx.ps
    pg.op(V, "memset", [], ["xlp"], xlp[:, :], 0.0)
    pg.op(V, "memset", [], ["hst"], hst[:, :], 0.0)
    pg.op(V, "memset", [], ["sst"], sst[:, :, :, :], 0.0)
    pg.op(V, "memset", [], ["Y0"], Y[:, 0, :], 0.0)
    pg.op(V, "memset", [], ["Y1"], Y[:, 1, :], 0.0)

    if blocks is None:
        blocks = blocks_of(T)
    pg.barrier()
    G = "gpsimd"
    for bi, (t0, nt) in enumerate(blocks):
        s = bi % 2
        X = xb[s]; xk = ("xb", s)
        W = lambda i: bufs[i][:, 0:nt]
        Hh = lambda i: bb16[i][:, 0:nt]
        wk = lambda i: ("w", i)
        hk = lambda i: ("h", i)
        P = lambda i: ps[i][:, 0:nt]
        pk = lambda i: ("ps", i)
        pg.dma("sync", X[:, :, 0:nt], xin_v[:, :, t0:t0 + nt], writes=[xk], sem=("xsem", s))
        FL = "norm,hn,inproj,ev1,ev2,ev3,ev4".split(",")
        if "norm" in FL:
            rms_rstd(pg, X[:, :, 0:nt], lambda k: hn[:, k, 0:nt], ones[:, :], P(0), rstd[:, 0:nt], D,
                     [xk], "hn", pk(0), "rstd", 8, epsc[:, 0:1])
        if "hn" in FL:
            for k in range(8):
                pg.op(V, "scalar_tensor_tensor", [xk, "rstd", "nw"], ["hn"], out=hn[:, k, 0:nt], in0=X[:, k, 0:nt],
                      scalar=nw_t[:, k:k + 1], in1=rstd[:, 0:nt], op0=ALU.mult, op1=ALU.mult)
        if "inproj" in FL:
            for m in range(3):
                for k in range(8):
                    pg.mm(P(1 + m), win[:, k, m * 128:(m + 1) * 128], hn[:, k, 0:nt], k == 0, k == 7, ["hn", "win"], [pk(1 + m)])
        if "ev1" in FL:
            pg.act(xlp[:, 4:4 + nt], P(1), AF.Identity, [pk(1)], ["xlp"])
        if "ev2" in FL:
            pg.act(W(0), P(2), AF.Identity, [pk(2)], [wk(0)])
        if "ev3" in FL:
            pg.act(W(1), P(3), AF.Identity, [pk(3)], [wk(1)])
        if "ev4" in FL:
            pg.op(V, "tensor_copy", [pk(3)], [hk(0)], out=Hh(0), in_=P(3))
        if dbg >= 2:
            for k in range(4):
                pg.mm(P(4), cdiag[:, k, :], xlp[:, 1 + k:1 + k + nt], k == 0, k == 3, ["xlp", "cdiag"], [pk(4)])
            pg.act(W(2), P(4), AF.Identity, [pk(4), "vecs"], [wk(2)], bias=vecs[:, 4:5])
            pg.op(V, "tensor_copy", [wk(2)], [hk(1)], out=Hh(1), in_=W(2))
            pg.op(V, "tensor_copy", ["xlp"], ["xlp"], out=xlp[:, 0:4], in_=xlp[:, nt:nt + 4])
            pg.mm(P(5), wa[:, :], Hh(1), True, True, [hk(1), "wa"], [pk(5)])
            pg.mm(P(6), wx[:, :], Hh(1), True, True, [hk(1), "wx"], [pk(6)])
            pg.act(W(3), P(5), AF.Sigmoid, [pk(5), "vecs"], [wk(3)], bias=vecs[:, 5:6])
            pg.act(W(4), P(6), AF.Sigmoid, [pk(6), "vecs"], [wk(4)], bias=vecs[:, 6:7])
            pg.act(W(5), W(3), AF.Exp, [wk(3), "lc"], [wk(5)], scale=lc[:, 0:1])
            pg.op(G, "tensor_tensor", [wk(5)], [wk(6)], out=W(6), in0=W(5), in1=W(5), op=ALU.mult)
            pg.act(W(6), W(6), AF.Sqrt, [wk(6), "onec"], [wk(6)], scale=-1.0, bias=onec[:, 0:1])
            pg.op(G, "tensor_tensor", [wk(4), wk(2)], [wk(7)], out=W(7), in0=W(4), in1=W(2), op=ALU.mult)
            pg.op(G, "tensor_tensor", [wk(7), wk(6)], [wk(7)], out=W(7), in0=W(7), in1=W(6), op=ALU.mult)
            hin = hst[:, (bi % 2):(bi % 2) + 1]; hout = hst[:, ((bi + 1) % 2):((bi + 1) % 2) + 1]
            pg.op(V, "tensor_tensor_scan", [wk(5), wk(7), "hst"], [wk(8)], out=W(8), data0=W(5), data1=W(7),
                  initial=hin, op0=ALU.mult, op1=ALU.add)
            pg.op(V, "tensor_copy", [wk(8)], ["hst"], out=hout, in_=bufs[8][:, nt - 1:nt])
            gelu_tanh(pg, G, W(9), W(0), W(10), W(11), [wk(0)], [wk(9)], "ga")
            pg.op(V, "tensor_tensor", [wk(8), wk(9)], ["Y0"], out=Y[:, 0, 0:nt], in0=W(8), in1=W(9), op=ALU.mult)
        if dbg >= 3:
            for j in range(4):
                pa, pb = (4, 5) if j % 2 == 0 else (6, 7)
                pg.mm(P(pa), bbT[:, 0, j * 128:(j + 1) * 128], Hh(0), True, True, [hk(0), "bbT"], [pk(pa)])
                pg.mm(P(pb), bbT[:, 1, j * 128:(j + 1) * 128], Hh(0), True, True, [hk(0), "bbT"], [pk(pb)])
                o = 12 + (j % 2) * 8
                cs_ = cosT[:, j, 0:nt]; sn_ = sinT[:, j, 0:nt]
                pg.op(V, "tensor_tensor", [pk(pa)] + tabk, [wk(o)], out=W(o), in0=P(pa), in1=cs_, op=ALU.mult)
                pg.op(V, "tensor_tensor", [pk(pb)] + tabk, [wk(o + 1)], out=W(o + 1), in0=P(pb), in1=sn_, op=ALU.mult)
                pg.op(G, "tensor_tensor", [wk(o), wk(o + 1)], [wk(o)], out=W(o), in0=W(o), in1=W(o + 1), op=ALU.add)
                pg.op(V, "tensor_tensor", [pk(pb)] + tabk, [wk(o + 2)], out=W(o + 2), in0=P(pb), in1=cs_, op=ALU.mult)
                pg.op(V, "tensor_tensor", [pk(pa)] + tabk, [wk(o + 3)], out=W(o + 3), in0=P(pa), in1=sn_, op=ALU.mult)
                pg.op(G, "tensor_tensor", [wk(o + 2), wk(o + 3)], [wk(o + 2)], out=W(o + 2), in0=W(o + 2), in1=W(o + 3), op=ALU.subtract)
                pi_, po_ = bi % 2, (bi + 1) % 2
                pg.op(V, "tensor_tensor_scan", [wk(o), "sst"] + tabk, [wk(o + 4)], out=W(o + 4), data0=decT[:, j, 0:nt], data1=W(o),
                      initial=sst[:, j, 0, pi_:pi_ + 1], op0=ALU.mult, op1=ALU.add)
                pg.op(V, "tensor_tensor_scan", [wk(o + 2), "sst"] + tabk, [wk(o + 5)], out=W(o + 5), data0=decT[:, j, 0:nt], data1=W(o + 2),
                      initial=sst[:, j, 1, pi_:pi_ + 1], op0=ALU.mult, op1=ALU.add)
                cN = cosT[:, j, nt - 1:nt]; sN = sinT[:, j, nt - 1:nt]
                lre = bufs[o + 4][:, nt - 1:nt]; lim = bufs[o + 5][:, nt - 1:nt]
                pg.op(V, "tensor_tensor", [wk(o + 5)] + tabk, ["stmp"], out=stmp[:, j, 0:1], in0=lim, in1=sN, op=ALU.mult)
                pg.op(V, "scalar_tensor_tensor", [wk(o + 4), "stmp", "sst"] + tabk, ["sst"], out=sst[:, j, 0, po_:po_ + 1], in0=lre, scalar=cN,
                      in1=stmp[:, j, 0:1], op0=ALU.mult, op1=ALU.subtract)
                pg.op(V, "tensor_tensor", [wk(o + 4)] + tabk, ["stmp"], out=stmp[:, j, 1:2], in0=lre, in1=sN, op=ALU.mult)
                pg.op(V, "scalar_tensor_tensor", [wk(o + 5), "stmp", "sst"] + tabk, ["sst"], out=sst[:, j, 1, po_:po_ + 1], in0=lim, scalar=cN,
                      in1=stmp[:, j, 1:2], op0=ALU.mult, op1=ALU.add)
                pg.op(G, "tensor_tensor", [wk(o + 4)] + tabk, [wk(o + 6)], out=W(o + 6), in0=W(o + 4), in1=cs_, op=ALU.mult)
                pg.op(G, "tensor_tensor", [wk(o + 5)] + tabk, [wk(o + 7)], out=W(o + 7), in0=W(o + 5), in1=sn_, op=ALU.mult)
                pg.op(G, "tensor_tensor", [wk(o + 6), wk(o + 7)], [hk(2 + 2 * j)], out=Hh(2 + 2 * j), in0=W(o + 6), in1=W(o + 7), op=ALU.subtract)
                pg.op(G, "tensor_tensor", [wk(o + 4)] + tabk, [wk(o + 6)], out=W(o + 6), in0=W(o + 4), in1=sn_, op=ALU.mult)
                pg.op(G, "tensor_tensor", [wk(o + 5)] + tabk, [wk(o + 7)], out=W(o + 7), in0=W(o + 5), in1=cs_, op=ALU.mult)
                pg.op(G, "tensor_tensor", [wk(o + 6), wk(o + 7)], [hk(3 + 2 * j)], out=Hh(3 + 2 * j), in0=W(o + 6), in1=W(o + 7), op=ALU.add)
            for j in range(4):
                pg.mm(P(1), ccT[:, 0, j * 128:(j + 1) * 128], Hh(2 + 2 * j), j == 0, False, [hk(2 + 2 * j), "ccT"], [pk(1)])
                pg.mm(P(1), ccT[:, 1, j * 128:(j + 1) * 128], Hh(3 + 2 * j), False, j == 3, [hk(3 + 2 * j), "ccT"], [pk(1)])
            pg.op(V, "scalar_tensor_tensor", [wk(1), "vecs", pk(1)], [wk(28)], out=W(28), in0=W(1), scalar=vecs[:, 8:9],
                  in1=P(1), op0=ALU.mult, op1=ALU.add)
            gelu_tanh(pg, V, W(29), W(28), W(30), W(31), [wk(28)], [wk(29)], "gb")
            pg.op(V, "tensor_copy", [wk(29)], [hk(10)], out=Hh(10), in_=W(29))
            pg.mm(P(2), wglu[:, :], Hh(10), True, True, [hk(10), "wglu"], [pk(2)])
            pg.act(W(30), P(2), AF.Sigmoid, [pk(2), "vecs"], [wk(30)], bias=vecs[:, 9:10])
            pg.op(V, "tensor_tensor", [wk(29), wk(30)], ["Y1"], out=Y[:, 1, 0:nt], in0=W(29), in1=W(30), op=ALU.mult)
        O = ob[0]; okk = ("ob", 0)
        for d in range(8):
            p = 3 if d % 2 == 0 else 0
            pg.mm(P(p), wout[:, 0, d * 128:(d + 1) * 128], Y[:, 0, 0:nt], True, False, ["Y0", "wout"], [pk(p)])
            pg.mm(P(p), wout[:, 1, d * 128:(d + 1) * 128], Y[:, 1, 0:nt], False, True, ["Y1", "wout"], [pk(p)])
            if d % 2 == 0:
                pg.act(O[:, d, 0:nt], P(p), AF.Identity, [pk(p)], [okk])
            else:
                pg.op(V, "tensor_copy", [pk(p)], [okk], out=O[:, d, 0:nt], in_=P(p))
        pg.dma("sync", part_v[:, :, t0:t0 + nt], O[:, :, 0:nt], reads=[okk], sem=("osem", 0))


def phase_mix1(ctx, x_in, P_, part, T, blocks=None):
    nc, pg, sb = ctx.nc, ctx.pg, ctx.sb
    win_d, nw_d, cv_d, hv_d, cvec_d, wout_d, ident_d, mneg_d, seg_d, sel_d = [
        P_[k] for k in ["win", "nw", "cv", "hv", "cvec", "wout", "ident", "mneg", "seg", "sel"]]
    xin_v = x_in.rearrange("(k p) t -> p k t", p=128)
    part_v = part.rearrange("(k p) t -> p k t", p=128)
    V, G = "vector", "gpsimd"
    win = sb("win_s", [128, 8, NCOL], BF16)
    wout = sb("wout_s", [128, 4, D], BF16)
    nw_t = sb("nw_s", [128, 8]); cv = sb("cv_s", [128, 8, 5]); hv = sb("hv_s", [8, 2]); cvec = sb("cvec_s", [128, 8])
    ident = sb("ident_s", [128, 128]); identb = sb("identb", [128, 128], BF16)
    mneg = sb("mneg_s", [128, BS]); mnegb = sb("mnegb", [128, BS], BF16)
    seg = sb("seg_s", [8, BS]); sel = sb("sel_s", [8, 8 * 128])
    ones = sb("ones", [128, 128], BF16); ones32 = sb("ones32", [8, 128])
    epsc = sb("epsc", [128, 1]); onec = sb("onec", [128, 1])
    pg.op(V, "memset", [], ["ones"], ones[:, :], 1.0)
    pg.op(V, "memset", [], ["ones32"], ones32[:, :], 1.0)
    pg.op(V, "memset", [], ["epsc"], epsc[:, :], EPS)
    pg.op(V, "memset", [], ["onec"], onec[:, :], 1.0)
    for t_, d_, k_ in [(nw_t, nw_d, "nw"), (cv, cv_d, "cv"), (hv, hv_d, "hv"), (cvec, cvec_d, "cvec"),
                       (ident, ident_d, "ident"), (mneg, mneg_d, "mneg"), (seg, seg_d, "seg"), (sel, sel_d, "sel")]:
        sl = tuple(slice(None) for _ in t_.shape)
        pg.dma("sync", t_[sl], d_, writes=[k_])
    win_v = win_d.rearrange("(k p) n -> p k n", p=128)
    for k in range(8):
        pg.dma("gpsimd", win[:, k, :], win_v[:, k, :], writes=[("win", k)])
    pg.dma("gpsimd", wout[:, :, :], wout_d.rearrange("(k p) n -> p k n", p=128), writes=["wout"])
    pg.op(V, "tensor_copy", ["ident"], ["identb"], out=identb[:, :], in_=ident[:, :])
    pg.op(V, "tensor_copy", ["mneg"], ["mnegb"], out=mnegb[:, :], in_=mneg[:, :])
    cdiag = sb("cdiag", [128, 8, 4, 128], BF16)
    for m in range(8):
        for k in range(4):
            pg.op(V, "tensor_scalar", ["ident", "cv"], ["cdiag"], out=cdiag[:, m, k, :], in0=ident[:, :],
                  scalar1=cv[:, m, k:k + 1], scalar2=None, op0=ALU.mult)
    ah = sb("ah", [8, 1])
    pg.act(ah[:, :], hv[:, 1:2], AF.Exp, ["hv"], ["ah"])
    pg.op(V, "tensor_scalar", ["ah"], ["ah"], out=ah[:, :], in0=ah[:, :], scalar1=-1.0, scalar2=None, op0=ALU.mult)

    xb = [sb("xb%d" % i, [128, 8, BS]) for i in range(2)]
    hn = sb("hn", [128, 8, BS], BF16)
    rstd = sb("rstd", [128, BS])
    zs = sb("zs", [128, 4, BS])
    xbc = sb("xbc", [128, 8, 4 + BS], BF16)
    xc = sb("xc", [128, 8, BS], BF16)
    dtf = [sb("dtf%d" % i, [8, BS]) for i in range(6)]
    tm = sb("tm", [128, 32])
    diag8 = sb("diag8", [8, 8])
    decb = sb("decb", [128, 8])
    Hs = sb("Hs", [128, 8, 64]); Hbf = sb("Hbf", [128, 512], BF16)
    E = [sb("E%d" % i, [128, 128], BF16) for i in range(8)]
    Gh = [sb("Gh%d" % i, [128, 128], BF16) for i in range(8)]
    xdt = sb("xdt", [128, 8, 64], BF16); xw = sb("xw", [128, 8, 64], BF16)
    btm = sb("btm", [128, 256], BF16)
    tyo = sb("tyo", [128, 8, 64]); ytm = sb("ytm", [128, 512], BF16)
    yfm = sb("yfm", [128, 4, BS])
    gg = sb("gg", [128, 4, BS]); sq = sb("sq", [128, 4, BS], BF16); rs2 = sb("rs2", [128, 2, BS])
    GN = sb("GN", [128, 4, BS], BF16)
    ob = sb("ob", [128, 8, BS])
    ps = ctx.ps
    psb = ctx.psb
    pg.op(V, "memset", [], ["xbc"], xbc[:, :, :], 0.0)
    pg.op(V, "memset", [], ["H"], Hs[:, :, :], 0.0)
    pg.op(V, "memset", [], ["Hbf"], Hbf[:, :], 0.0)
    pg.barrier()

    if blocks is None:
        blocks = blocks_of(T)
    pr = 0
    for bi, (t0, nt) in enumerate(blocks):
        s = bi % 2
        X = xb[s]; xk = ("xb", s)
        P = lambda i: ps[i][:, 0:nt]
        pk = lambda i: ("ps", i)
        pg.dma("sync", X[:, :, 0:nt], xin_v[:, :, t0:t0 + nt], writes=[xk], sem=("xsem", s))
        rms_rstd(pg, X[:, :, 0:nt], lambda k: hn[:, k, 0:nt], ones[:, :], P(0), rstd[:, 0:nt], D,
                 [xk], "hn", pk(0), "rstd", 8, epsc[:, 0:1])
        for k in range(8):
            pg.op(V, "scalar_tensor_tensor", [xk, "rstd", "nw"], ["hn"], out=hn[:, k, 0:nt], in0=X[:, k, 0:nt],
                  scalar=nw_t[:, k:k + 1], in1=rstd[:, 0:nt], op0=ALU.mult, op1=ALU.mult)
        wk_all = [("win", k) for k in range(8)]
        for m in range(12):
            p = 1 + (pr % 2); pr += 1
            for k in range(8):
                pg.mm(P(p), win[:, k, m * 128:(m + 1) * 128], hn[:, k, 0:nt], k == 0, k == 7, ["hn"] + wk_all, [pk(p)])
            if m < 4:
                pg.act(zs[:, m, 0:nt], P(p), AF.Silu, [pk(p)], [("zs", m)])
            else:
                pg.act(xbc[:, m - 4, 4:4 + nt], P(p), AF.Identity, [pk(p)], [("xbc", m - 4)])
        p = 1 + (pr % 2); pr += 1
        for k in range(8):
            pg.mm(ps[p][0:8, 0:nt], win[:, k, 1536:1544], hn[:, k, 0:nt], k == 0, k == 7, ["hn"] + wk_all, [pk(p)])
        e1, dtA, cs, ncs, tend, fs = [t_[:, 0:nt] for t_ in dtf]
        pg.act(e1, ps[p][0:8, 0:nt], AF.Exp, [pk(p), "hv"], ["e1"], bias=hv[:, 0:1])
        pg.act(e1, e1, AF.Ln, ["e1", "onec"], ["e1"], bias=onec[0:8, 0:1])
        pg.op(V, "tensor_scalar", ["e1", "ah"], ["dtA"], out=dtA, in0=e1, scalar1=ah[:, 0:1], scalar2=None, op0=ALU.mult)
        pg.op(V, "tensor_tensor_scan", ["dtA", "seg"], ["cs"], out=cs, data0=seg[:, 0:nt], data1=dtA, initial=0.0,
              op0=ALU.mult, op1=ALU.add)
        pg.op(V, "tensor_scalar", ["cs"], ["ncs"], out=ncs, in0=cs, scalar1=-1.0, scalar2=None, op0=ALU.mult)
        chunks = [(c0, min(128, nt - c0)) for c0 in range(0, nt, 128)]
        for (c0, L) in chunks:
            pg.op(V, "tensor_scalar", ["cs"], ["tend"], out=dtf[4][:, c0:c0 + L], in0=dtf[2][:, c0:c0 + L], scalar1=-1.0,
                  scalar2=dtf[2][:, c0 + L - 1:c0 + L], op0=ALU.mult, op1=ALU.add)
        pg.act(tend, tend, AF.Exp, ["tend"], ["tend"])
        pg.act(fs, cs, AF.Exp, ["cs"], ["fs"])
        for m in range(8):
            p = 1 + (pr % 2); pr += 1
            for k in range(4):
                pg.mm(P(p), cdiag[:, m, k, :], xbc[:, m, 1 + k:1 + k + nt], k == 0, k == 3, [("xbc", m), "cdiag"], [pk(p)])
            pg.act(xc[:, m, 0:nt], P(p), AF.Silu, [pk(p), "cv"], [("xc", m)], bias=cv[:, m, 4:5])
            pg.op(V, "tensor_copy", [("xbc", m)], [("xbc", m)], out=xbc[:, m, 0:4], in_=xbc[:, m, nt:nt + 4])
        for (c0, L) in chunks:
            for qi, (src, key) in enumerate([(dtf[0], "e1"), (dtf[3], "ncs"), (dtf[4], "tend"), (dtf[5], "fs")]):
                pg.mm(ps[5][0:L, 256 + 8 * qi:256 + 8 * qi + 8], src[0:8, c0:c0 + L], ident[0:8, 0:8], True, True,
                      [key, "ident"], [pk(5)])
            pg.act(tm[0:L, :], ps[5][0:L, 256:288], AF.Identity, [pk(5)], ["tm"])
            pg.op(V, "tensor_scalar", ["ident", "cs"], ["diag8"], out=diag8[:, :], in0=ident[0:8, 0:8],
                  scalar1=dtf[2][:, c0 + L - 1:c0 + L], scalar2=None, op0=ALU.mult)
            pg.mm(ps[5][:, 288:296], ones32[:, :], diag8[:, :], True, True, ["ones32", "diag8"], [pk(5)])
            pg.act(decb[:, :], ps[5][:, 288:296], AF.Exp, [pk(5)], ["decb"])
            for g in range(2):
                pg.mm(ps[5][0:L, g * 128:g * 128 + L], xc[:, 4 + g, c0:c0 + L], xc[:, 6 + g, c0:c0 + L], True, True,
                      [("xc", 4 + g), ("xc", 6 + g)], [pk(5)])
            for h in range(8):
                pg.mm(ps[3][:, 0:L], sel[:, h * 128:(h + 1) * 128], dtf[2][:, c0:c0 + L], True, False, ["sel", "cs"], [pk(3)])
                pg.mm(ps[3][:, 0:L], identb[:, :], mnegb[:, 0:L], False, True, ["identb", "mnegb"], [pk(3)])
                pg.act(E[h][0:L, 0:L], ps[3][0:L, 0:L], AF.Exp, [pk(3), "tm"], [("E", h)], bias=tm[0:L, 8 + h:9 + h])
                g = h // 4
                pg.op(V, "tensor_tensor", [("E", h), pk(5)], [("G", h)], out=Gh[h][0:L, 0:L], in0=ps[5][0:L, g * 128:g * 128 + L],
                      in1=E[h][0:L, 0:L], op=ALU.mult)
            for m in range(4):
                pg.op("tensor", "transpose", [("xc", m), "identb"], [("ps", "b")], psb[0:L, m * 128:(m + 1) * 128], xc[:, m, c0:c0 + L], identb[:, :])
            for g in range(2):
                pg.op("tensor", "transpose", [("xc", 4 + g), "identb"], [("ps", "b")], psb[0:L, 512 + g * 128:512 + (g + 1) * 128],
                      xc[:, 4 + g, c0:c0 + L], identb[:, :])
            psx = psb[0:L, 0:512].rearrange("p (h d) -> p h d", h=8)
            pg.op(V, "tensor_tensor", [("ps", "b"), "tm"], ["xdt"], out=xdt[0:L, :, :], in0=psx,
                  in1=tm[0:L, 0:8].unsqueeze(2).broadcast_to([L, 8, 64]), op=ALU.mult)
            pg.op(V, "tensor_copy", [("ps", "b")], ["btm"], out=btm[0:L, :], in_=psb[0:L, 512:768])
            pg.op(G, "tensor_tensor", ["xdt", "tm"], ["xw"], out=xw[0:L, :, :], in0=xdt[0:L, :, :],
                  in1=tm[0:L, 16:24].unsqueeze(2).broadcast_to([L, 8, 64]), op=ALU.mult)
            for h in range(8):
                pg.mm(ps[6][0:L, h * 64:(h + 1) * 64], Gh[h][0:L, 0:L], xdt[0:L, h, :], True, True, [("G", h), "xdt"], [pk(6)])
            for g in range(2):
                pg.mm(ps[7][0:L, g * 256:(g + 1) * 256], xc[:, 6 + g, c0:c0 + L], Hbf[:, g * 256:(g + 1) * 256], True, True,
                      [("xc", 6 + g), "Hbf"], [pk(7)])
            ps7v = ps[7][0:L, :].rearrange("p (h d) -> p h d", h=8)
            pg.op(V, "tensor_tensor", [pk(7), "tm"], ["tyo"], out=tyo[0:L, :, :], in0=ps7v,
                  in1=tm[0:L, 24:32].unsqueeze(2).broadcast_to([L, 8, 64]), op=ALU.mult)
            pg.op(V, "tensor_tensor", [pk(6), "tyo"], ["ytm"], out=ytm[0:L, :], in0=ps[6][0:L, :],
                  in1=tyo[0:L, :, :].rearrange("p h d -> p (h d)"), op=ALU.add)
            for g in range(2):
                pg.mm(ps[7][:, g * 256:(g + 1) * 256], btm[0:L, g * 128:(g + 1) * 128],
                      xw[0:L, 4 * g:4 * g + 4, :].rearrange("p h d -> p (h d)"), True, True, ["btm", "xw"], [pk(7)])
            pg.op(V, "tensor_tensor", ["H", "decb"], ["H"], out=Hs[:, :, :], in0=Hs[:, :, :],
                  in1=decb[:, :].unsqueeze(2).broadcast_to([128, 8, 64]), op=ALU.mult)
            pg.op(V, "tensor_tensor", ["H", pk(7)], ["H"], out=Hs[:, :, :], in0=Hs[:, :, :],
                  in1=ps[7][:, :].rearrange("p (h d) -> p h d", h=8), op=ALU.add)
            pg.op(G, "tensor_copy", ["H"], ["Hbf"], out=Hbf[:, :], in_=Hs[:, :, :].rearrange("p h d -> p (h d)"))
            for m in range(4):
                pg.op("tensor", "transpose", ["ytm", "identb"], [("ps", "b")], psb[:, m * 128:m * 128 + L], ytm[0:L, m * 128:(m + 1) * 128], identb[0:L, 0:L])
            for m in range(4):
                pg.act(yfm[:, m, c0:c0 + L], psb[:, m * 128:m * 128 + L], AF.Identity, [("ps", "b")], [("yfm", m)])
        for m in range(4):
            pg.op(V, "scalar_tensor_tensor", [("xc", m), "cvec", ("yfm", m)], [("yfm", m)], out=yfm[:, m, 0:nt], in0=xc[:, m, 0:nt],
                  scalar=cvec[:, m:m + 1], in1=yfm[:, m, 0:nt], op0=ALU.mult, op1=ALU.add)
            pg.op(G, "tensor_tensor", [("yfm", m), ("zs", m)], [("gg", m)], out=gg[:, m, 0:nt], in0=yfm[:, m, 0:nt], in1=zs[:, m, 0:nt], op=ALU.mult)
            pg.act(sq[:, m, 0:nt], gg[:, m, 0:nt], AF.Square, [("gg", m)], [("sq", m)])
        for g in range(2):
            pg.mm(P(0), ones[:, :], sq[:, 2 * g, 0:nt], True, False, [("sq", 2 * g), "ones"], [pk(0)])
            pg.mm(P(0), ones[:, :], sq[:, 2 * g + 1, 0:nt], False, True, [("sq", 2 * g + 1), "ones"], [pk(0)])
            pg.act(rs2[:, g, 0:nt], P(0), AF.Sqrt, [pk(0), "epsc"], [("rs2", g)], scale=1.0 / 256, bias=epsc[:, 0:1])
            pg.op(V, "reciprocal", [("rs2", g)], [("rs2", g)], out=rs2[:, g, 0:nt], in_=rs2[:, g, 0:nt])
        for m in range(4):
            pg.op(V, "scalar_tensor_tensor", [("gg", m), "cvec", ("rs2", m // 2)], [("GN", m)], out=GN[:, m, 0:nt], in0=gg[:, m, 0:nt],
                  scalar=cvec[:, 4 + m:5 + m], in1=rs2[:, m // 2, 0:nt], op0=ALU.mult, op1=ALU.mult)
        for d in range(8):
            p = 1 + (pr % 2); pr += 1
            for m in range(4):
                pg.mm(P(p), wout[:, m, d * 128:(d + 1) * 128], GN[:, m, 0:nt], m == 0, m == 3, [("GN", m), "wout"], [pk(p)])
            if d % 2 == 0:
                pg.act(ob[:, d, 0:nt], P(p), AF.Identity, [pk(p)], ["ob"])
            else:
                pg.op(V, "tensor_copy", [pk(p)], ["ob"], out=ob[:, d, 0:nt], in_=P(p))
        pg.dma("sync", part_v[:, :, t0:t0 + nt], ob[:, :, 0:nt], reads=["ob"], sem=("osem", 0))


M0_SHAPES = {'win': [1024, 384], 'nw': [128, 8], 'vecs': [128, 16], 'wa': [128, 128], 'wx': [128, 128], 'wglu': [128, 128], 'wout': [256, 1024], 'scol': [128, 3, 4], 'srow': [128, 3, 512], 'bT': [128, 2, 512], 'cT': [128, 2, 512], 'ident': [128, 128], 'kidx': [128, 512]}
M1_SHAPES = {'win': [1024, 1544], 'nw': [128, 8], 'cv': [128, 8, 5], 'hv': [8, 2], 'cvec': [128, 8], 'wout': [512, 1024], 'ident': [128, 128], 'mneg': [128, 512], 'seg': [8, 512], 'sel': [8, 1024]}


def build_fused(T, mlp_bs=410):
    nc = bass.Bass("TRN2", target_bir_lowering=False)
    din = lambda n, s: nc.dram_tensor(n, list(s), F32, kind="ExternalInput").ap()
    x_in = din("x_in", [D, T])
    out = nc.dram_tensor("out", [D, T], F32, kind="ExternalOutput").ap()
    parts = [nc.dram_tensor("part%d" % q, [D, T], F32).ap() for q in range(4)]
    h1 = nc.dram_tensor("h1", [D, T], F32).ap()
    PA = [{k: din("a%d_%s" % (q, k), s) for k, s in M0_SHAPES.items()} for q in range(4)]
    PB = [{k: din("b%d_%s" % (q, k), s) for k, s in M1_SHAPES.items()} for q in range(4)]
    mw = [{"w_up": din("m%d_w_up" % l, [D, DFF]), "w_dn": din("m%d_w_dn" % l, [DFF, D]), "nw": din("m%d_nw" % l, [128, 8])}
          for l in range(2)]
    nf = din("nf", [128, 8])
    pg = Prog(nc)
    ctx = Ctx(nc, pg)
    first = True
    for q in range(4):
        if not first:
            pg.phase(); ctx.reset()
        first = False
        phase_mix0(ctx, x_in, PA[q], parts[q], T)
    pg.phase(); ctx.reset()
    phase_mlp(ctx, x_in, parts, mw[0]["w_up"], mw[0]["w_dn"], mw[0]["nw"], nf, h1, T, False, mlp_bs)
    for q in range(4):
        pg.phase(); ctx.reset()
        phase_mix1(ctx, h1, PB[q], parts[q], T)
    pg.phase(); ctx.reset()
    phase_mlp(ctx, h1, parts, mw[1]["w_up"], mw[1]["w_dn"], mw[1]["nw"], nf, out, T, True, mlp_bs)
    pg.finish()
    stats = pg.emit()
    return nc, stats


import numpy as np

def lay8(v):
    return np.ascontiguousarray(v.reshape(-1, 128).T.astype(np.float32))

def prep_mix0(inp, q):
    f = np.float32
    w_in = inp["ev_w_in"][0]
    sl = slice(128 * q, 128 * q + 128)
    win = np.concatenate([w_in[:, 0:512][:, sl], w_in[:, 512:1024][:, sl], w_in[:, 1024:1536][:, sl]], axis=1)
    vecs = np.zeros((128, 16), f)
    vecs[:, 0:4] = inp["lru_conv_w"][0][:, sl].T
    vecs[:, 4] = inp["lru_conv_b"][0][sl]
    vecs[:, 5] = inp["lru_b_a"][0][sl]
    vecs[:, 6] = inp["lru_b_x"][0][sl]
    vecs[:, 7] = inp["lru_lambda"][0][sl]
    vecs[:, 8] = inp["s5_d"][0][sl]
    vecs[:, 9] = inp["s5_b_glu"][0][sl]
    def bd(blocks):
        n = blocks[0].shape[0]
        m = np.zeros((n * len(blocks), n * len(blocks)), f)
        for i, b in enumerate(blocks):
            m[i * n:(i + 1) * n, i * n:(i + 1) * n] = b
        return m
    wa = bd([inp["lru_w_a"][0][2 * q], inp["lru_w_a"][0][2 * q + 1]])
    wx = bd([inp["lru_w_x"][0][2 * q], inp["lru_w_x"][0][2 * q + 1]])
    wglu = bd([inp["s5_w_glu"][0][8 * q + g] for g in range(8)])
    w_out = inp["ev_w_out"][0]
    wout = np.concatenate([w_out[0:512][sl], w_out[512:1024][sl]], axis=0)
    scol = np.zeros((128, 3, 4), f)
    srow = np.zeros((128, 3, 512), f)
    bT = np.zeros((128, 2, 512), f)
    cT = np.zeros((128, 2, 512), f)
    for j in range(4):
        for slot in range(2):
            gl = 2 * j + slot
            g = 8 * q + gl
            sp = slice(slot * 64, slot * 64 + 64)
            scol[sp, 0, j] = inp["s5_a_re"][0][g]
            scol[sp, 1, j] = inp["s5_a_im"][0][g]
            scol[sp, 2, j] = inp["s5_log_dt"][0][g]
            cs = slice(j * 128 + slot * 64, j * 128 + slot * 64 + 64)
            srow[:, 0, cs] = inp["s5_a_re"][0][g][None, :]
            srow[:, 1, cs] = inp["s5_a_im"][0][g][None, :]
            srow[:, 2, cs] = inp["s5_log_dt"][0][g]
            ch = slice(gl * 16, gl * 16 + 16)
            bT[ch, 0, cs] = inp["s5_b_re"][0][g].T
            bT[ch, 1, cs] = inp["s5_b_im"][0][g].T
            cc = slice(j * 128 + gl * 16, j * 128 + gl * 16 + 16)
            cT[sp, 0, cc] = inp["s5_c_re"][0][g].T
            cT[sp, 1, cc] = inp["s5_c_im"][0][g].T
    return {
        "win": np.ascontiguousarray(win), "nw": lay8(inp["norm_mix"][0]), "vecs": vecs, "wa": wa, "wx": wx,
        "wglu": wglu, "wout": np.ascontiguousarray(wout), "scol": scol, "srow": srow, "bT": bT, "cT": cT,
        "ident": np.eye(128, dtype=f), "kidx": np.tile(np.arange(1, 513, dtype=f)[None, :], (128, 1)),
    }

def seq_fm(inp, b):
    return np.ascontiguousarray(np.concatenate([inp["meta_tokens"], inp["x"][b]], axis=0).T.astype(np.float32))


def prep_mix1(inp, q):
    f = np.float32
    w = inp["ssd_w_in"][0]
    cols = np.concatenate([np.arange(512 * q, 512 * q + 512), 2048 + np.arange(512 * q, 512 * q + 512),
                           4096 + np.arange(256 * q, 256 * q + 256), 5120 + np.arange(256 * q, 256 * q + 256),
                           6144 + np.arange(8 * q, 8 * q + 8)])
    win = np.ascontiguousarray(w[:, cols])
    cch = np.concatenate([np.arange(512 * q, 512 * q + 512), 2048 + np.arange(256 * q, 256 * q + 256),
                          3072 + np.arange(256 * q, 256 * q + 256)])
    cw = inp["ssd_conv_w"][0][:, cch]
    cb = inp["ssd_conv_b"][0][cch]
    cv = np.zeros((128, 8, 5), f)
    for m in range(8):
        cv[:, m, 0:4] = cw[:, m * 128:(m + 1) * 128].T
        cv[:, m, 4] = cb[m * 128:(m + 1) * 128]
    hv = np.stack([inp["ssd_dt_bias"][0][8 * q:8 * q + 8], inp["ssd_a_log"][0][8 * q:8 * q + 8]], axis=1).astype(f)
    cvec = np.zeros((128, 8), f)
    dd = np.repeat(inp["ssd_d"][0][8 * q:8 * q + 8], 64)
    nn = inp["ssd_norm"][0][512 * q:512 * q + 512]
    for m in range(4):
        cvec[:, m] = dd[m * 128:(m + 1) * 128]
        cvec[:, 4 + m] = nn[m * 128:(m + 1) * 128]
    wout = np.ascontiguousarray(inp["ssd_w_out"][0][512 * q:512 * q + 512])
    s_idx = np.arange(128)[:, None]
    l_idx = (np.arange(512) % 128)[None, :]
    mneg = np.where(l_idx >= s_idx, 0.0, -30000.0).astype(f)
    seg = np.ones((8, 512), f)
    seg[:, ::128] = 0.0
    sel = np.zeros((8, 8, 128), f)
    for h in range(8):
        sel[h, h, :] = 1.0
    return {"win": win, "nw": lay8(inp["norm_mix"][1]), "cv": cv, "hv": np.ascontiguousarray(hv), "cvec": cvec,
            "wout": wout, "ident": np.eye(128, dtype=f), "mneg": mneg, "seg": seg, "sel": sel.reshape(8, 1024)}


_CACHE = {}
T_SEQ = 16400


def kernel(**inputs):
    inp = {k: np.asarray(v) for k, v in inputs.items()}
    if "nc" not in _CACHE:
        _CACHE["nc"] = build_fused(T_SEQ, mlp_bs=410)[0]
    nc = _CACHE["nc"]
    base = {}
    for q in range(4):
        for k, v in prep_mix0(inp, q).items():
            base["a%d_%s" % (q, k)] = v
        for k, v in prep_mix1(inp, q).items():
            base["b%d_%s" % (q, k)] = v
    for l in range(2):
        base["m%d_w_up" % l] = np.ascontiguousarray(inp["mlp_w_up"][l])
        base["m%d_w_dn" % l] = np.ascontiguousarray(inp["mlp_w_down"][l])
        base["m%d_nw" % l] = lay8(inp["norm_mlp"][l])
    base["nf"] = lay8(inp["norm_final"])
    seqs = [seq_fm(inp, b) for b in range(2)]
    in_maps = []
    for c in range(8):
        m = dict(base)
        m["x_in"] = seqs[c % 2]
        in_maps.append(m)
    res = run_bass_kernel_spmd(nc, in_maps, core_ids=list(range(8))).results
    out = np.stack([np.ascontiguousarray(res[b]["out"][:, 16:].T) for b in range(2)], axis=0)
    return out.astype(np.float32)
```

```python
from concourse.bass_utils import run_bass_kernel_spmd


import numpy as np
import concourse.bass as bass
import concourse.mybir as mybir

F32 = mybir.dt.float32
BF16 = mybir.dt.bfloat16
AF = mybir.ActivationFunctionType
ALU = mybir.AluOpType
AX = mybir.AxisListType

ENGS = ["tensor", "vector", "scalar", "gpsimd", "sync"]


class Prog:
    def __init__(self, nc):
        self.nc = nc
        self.ins = {e: [] for e in ENGS}
        self.res = {}
        self.dma_sem = {}
        self.waited = {e: {} for e in ENGS}
        self.esems = [{e: nc.alloc_semaphore("prg0_" + e) for e in ENGS}]
        self.gen_start = {e: [0] for e in ENGS}
        self.pdi = 0

    def _deps(self, eng, reads, writes):
        deps = []
        for r in reads:
            st = self.res.get(r)
            if st and st["w"] is not None:
                deps.append(st["w"])
        for w in writes:
            st = self.res.get(w)
            if st:
                if st["w"] is not None:
                    deps.append(st["w"])
                deps.extend(st["r"])
        out = []
        wd = self.waited[eng]
        for d in deps:
            if d[0] == "e":
                if d[1] == eng and eng in ("tensor", "sync"):
                    continue
                key = ("e", d[1])
                if wd.get(key, -1) >= d[2]:
                    continue
                wd[key] = d[2]
                out.append(d)
            else:
                key = ("d", d[1])
                if wd.get(key, -1) >= d[2]:
                    continue
                wd[key] = d[2]
                out.append(d)
        return out

    def _commit(self, me, reads, writes):
        for r in reads:
            st = self.res.setdefault(r, {"w": None, "r": []})
            st["r"].append(me)
        for w in writes:
            self.res[w] = {"w": me, "r": []}

    def op(self, eng, name, reads, writes, *args, **kw):
        fn = (lambda e, name=name, args=args, kw=kw: getattr(e, name)(*args, **kw))
        ex = [r for r in reads if isinstance(r, tuple) and r[0] == "ps"]
        if ex:
            reads = [r for r in reads if r not in ex]
            writes = list(writes) + [r for r in ex if r not in writes]
        waits = self._deps(eng, reads, writes)
        idx = len(self.ins[eng])
        self.ins[eng].append({"fn": fn, "waits": waits, "dma": None})
        self._commit(("e", eng, idx), reads, writes)

    def mm(self, out, lhsT, rhs, start, stop, reads, writes, **kw):
        self.op("tensor", "matmul", reads, writes, out, lhsT=lhsT, rhs=rhs, start=start, stop=stop, **kw)

    def act(self, out, in_, func, reads, writes, **kw):
        self.op("scalar", "activation", reads, writes, out=out, in_=in_, func=func, **kw)

    def dma(self, eng, out, in_, reads=(), writes=(), sem=None, **dkw):
        if sem is None:
            sem = ("pd", self.pdi)
            self.pdi += 1
        if sem not in self.dma_sem:
            self.dma_sem[sem] = [self.nc.alloc_semaphore("d%d" % len(self.dma_sem)), 0]
        waits = self._deps(eng, reads, writes)
        self.dma_sem[sem][1] += 16
        val = self.dma_sem[sem][1]
        h = self.dma_sem[sem][0]
        self.ins[eng].append({"fn": (lambda e, out=out, in_=in_, dkw=dkw: e.dma_start(out=out, in_=in_, **dkw)),
                              "waits": waits, "dma": h})
        self._commit(("d", sem, val), reads, writes)

    def barrier(self):
        last = {}
        for e in ENGS:
            for idx in range(len(self.ins[e]) - 1, -1, -1):
                if self.ins[e][idx]["dma"] is None and self.ins[e][idx]["fn"] is not None:
                    last[e] = idx
                    break
        for eng in ENGS:
            waits = []
            for key, (h, tot) in self.dma_sem.items():
                if self.waited[eng].get(("d", key), -1) < tot:
                    waits.append(("d", key, tot))
                    self.waited[eng][("d", key)] = tot
            for e, idx in last.items():
                if e != eng and self.waited[eng].get(("e", e), -1) < idx:
                    waits.append(("e", e, idx))
                    self.waited[eng][("e", e)] = idx
            self.ins[eng].append({"fn": None, "waits": waits, "dma": None})

    def phase(self):
        self.barrier()
        g = len(self.esems)
        self.esems.append({e: self.nc.alloc_semaphore("prg%d_%s" % (g, e)) for e in ENGS})
        for e in ENGS:
            self.gen_start[e].append(len(self.ins[e]))
        self.pdi = 0

    def finish(self, eng="sync"):
        waits = []
        for key, (h, tot) in self.dma_sem.items():
            if self.waited[eng].get(("d", key), -1) < tot:
                waits.append(("d", key, tot))
        for e in ENGS:
            if e != eng and self.ins[e]:
                for idx in range(len(self.ins[e]) - 1, -1, -1):
                    if self.ins[e][idx]["dma"] is None and self.ins[e][idx]["fn"] is not None:
                        waits.append(("e", e, idx))
                        break
        self.ins[eng].append({"fn": None, "waits": waits, "dma": None})

    def emit(self):
        nc = self.nc
        needed = {e: set() for e in ENGS}
        for e in ENGS:
            for ins in self.ins[e]:
                for d in ins["waits"]:
                    if d[0] == "e":
                        needed[d[1]].add(d[2])
        import bisect
        val = {}
        gen_of = {}
        for e in ENGS:
            run = 0
            g = 0
            starts = self.gen_start[e]
            for idx in range(len(self.ins[e])):
                while g + 1 < len(starts) and idx >= starts[g + 1]:
                    g += 1
                    run = 0
                if idx in needed[e]:
                    run += 1
                    val[(e, idx)] = run
                    gen_of[(e, idx)] = g
        stats = {e: (len(self.ins[e]), len(needed[e])) for e in ENGS}
        ecount = {}
        dval = {k: 0 for k in self.dma_sem}
        hmap = {id(v[0]): k for k, v in self.dma_sem.items()}
        ptr = {e: 0 for e in ENGS}
        progress = True
        while progress:
            progress = False
            for e in ENGS:
                while ptr[e] < len(self.ins[e]):
                    ins = self.ins[e][ptr[e]]
                    ok = True
                    for d in ins["waits"]:
                        if d[0] == "e":
                            if ecount.get((d[1], gen_of[(d[1], d[2])]), 0) < val[(d[1], d[2])]:
                                ok = False
                                break
                        else:
                            if dval[d[1]] < d[2]:
                                ok = False
                                break
                    if not ok:
                        break
                    if ins["dma"] is not None:
                        dval[hmap[id(ins["dma"])]] += 16
                    elif ins["fn"] is not None and ptr[e] in needed[e]:
                        kk = (e, gen_of[(e, ptr[e])])
                        ecount[kk] = ecount.get(kk, 0) + 1
                    ptr[e] += 1
                    progress = True
        for e in ENGS:
            if ptr[e] < len(self.ins[e]):
                raise RuntimeError("DEADLOCK in dry run: engine %s stuck at %d/%d waits=%s" % (
                    e, ptr[e], len(self.ins[e]), self.ins[e][ptr[e]]["waits"]))

        def mk(e):
            def body(engobj):
                for idx, ins in enumerate(self.ins[e]):
                    for d in ins["waits"]:
                        if d[0] == "e":
                            engobj.wait_ge(self.esems[gen_of[(d[1], d[2])]][d[1]], val[(d[1], d[2])])
                        else:
                            engobj.wait_ge(self.dma_sem[d[1]][0], d[2])
                    if ins["fn"] is None:
                        continue
                    r = ins["fn"](engobj)
                    if ins["dma"] is not None:
                        r.then_inc(ins["dma"], 16)
                    elif idx in needed[e]:
                        r.then_inc(self.esems[gen_of[(e, idx)]][e], 1)
            return body

        with nc.Block() as block:
            block.tensor(mk("tensor"))
            block.vector(mk("vector"))
            block.scalar(mk("scalar"))
            block.gpsimd(mk("gpsimd"))
            block.sync(mk("sync"))
        return stats


import math
import numpy as np
import concourse.bass as bass
import concourse.mybir as mybir

D = 1024
DFF = 4096
EPS = 1e-5
BS = 512
PI = math.pi
NCOL = 1544


class Ctx:
    def __init__(self, nc, pg):
        self.nc = nc
        self.pg = pg
        self.words = 53000
        self.arena = nc.alloc_sbuf_tensor("arena", [128, self.words], F32)
        self.off = 0
        self.ps = [nc.alloc_psum_tensor("ps%d" % i, [128, 512], F32) for i in range(8)]
        self.psb = self.ps[4][:, :].bitcast(BF16)

    def reset(self):
        self.off = 0

    def sb(self, name, shape, dt=F32):
        n = 1
        for d_ in shape[1:]:
            n *= d_
        nw = n if dt == F32 else (n + 1) // 2
        nw = (nw + 7) // 8 * 8
        assert self.off + nw <= self.words, ("arena overflow", name, self.off, nw)
        base = self.arena[0:shape[0], self.off:self.off + nw]
        self.off += nw
        if dt != F32:
            base = base.bitcast(dt)
        ap = base[:, 0:n]
        if len(shape) == 3:
            ap = ap.rearrange("p (a b) -> p a b", a=shape[1])
        elif len(shape) == 4:
            ap = ap.rearrange("p (a b c) -> p a b c", a=shape[1], b=shape[2])
        return ap


def blocks_of(nt_total, pre=16, bs=512):
    bl = [(0, pre)] if pre else []
    t = pre
    while t < nt_total:
        n = min(bs, nt_total - t)
        bl.append((t, n))
        t += n
    return bl

def rms_rstd(pg, x_ap, sq_ap_fn, ones_bf, ps_ap, rstd_ap, D_, rkeys, sqkey, pskey, rstdkey, kt, eps_ap):
    for k in range(kt):
        pg.act(sq_ap_fn(k), x_ap[:, k, :], AF.Square, rkeys, [sqkey])
    for k in range(kt):
        pg.mm(ps_ap, ones_bf, sq_ap_fn(k), k == 0, k == kt - 1, [sqkey, "ones"], [pskey])
    pg.act(rstd_ap, ps_ap, AF.Sqrt, [pskey, "epsc"], [rstdkey], scale=1.0 / D_, bias=eps_ap)
    pg.op("vector", "reciprocal", [rstdkey], [rstdkey], out=rstd_ap, in_=rstd_ap)


def gelu_tanh(pg, eng, out_ap, x_ap, t1, t2, rk, wk, tk):
    pg.op(eng, "tensor_tensor", rk, [tk + "1"], out=t1, in0=x_ap, in1=x_ap, op=ALU.mult)
    pg.op(eng, "tensor_scalar", [tk + "1"], [tk + "1"], out=t1, in0=t1, scalar1=0.044715, scalar2=1.0,
          op0=ALU.mult, op1=ALU.add)
    pg.op(eng, "tensor_tensor", rk + [tk + "1"], [tk + "2"], out=t2, in0=t1, in1=x_ap, op=ALU.mult)
    pg.act(t1, t2, AF.Sigmoid, [tk + "2"], [tk + "1"], scale=1.5957691216057308)
    pg.op(eng, "tensor_tensor", rk + [tk + "1"], wk, out=out_ap, in0=x_ap, in1=t1, op=ALU.mult)


def sincos_tables(pg, nc, arg_ap, sin_out, cos_out, tmp_ap, key, negpi, tmp2_ap=None):
    MAGIC = 12582912.0
    SC = 2 * PI * (1.0 - 1e-6)
    if tmp2_ap is None:
        tmp2_ap = cos_out
    for off, outp, nm in ((0.0, sin_out, "sin"), (0.25, cos_out, "cos")):
        pg.op("vector", "tensor_scalar", [key + "arg", key + "sin"], [key + "tmp"], out=tmp_ap, in0=arg_ap, scalar1=1.0 / (2 * PI),
              scalar2=off, op0=ALU.mult, op1=ALU.add)
        pg.op("vector", "tensor_scalar", [key + "tmp"], [key + nm], out=outp, in0=tmp_ap, scalar1=MAGIC, scalar2=None, op0=ALU.add)
        pg.op("vector", "tensor_scalar", [key + nm], [key + nm], out=outp, in0=outp, scalar1=-MAGIC, scalar2=None, op0=ALU.add)
        pg.op("vector", "tensor_tensor", [key + "tmp", key + nm], [key + "tmp"], out=tmp_ap, in0=tmp_ap, in1=outp, op=ALU.subtract)
        pg.act(outp, tmp_ap, AF.Sin, [key + "tmp"], [key + nm], scale=SC)


def lam_parts(pg, nc, are, aim, ldt, shape, key, negpi, pool=None, sbf=None):
    al = lambda n: sbf(key + n, shape, F32)
    names = ["dt", "x", "mag", "th", "sn", "cs", "tmp", "ar1", "ai", "den", "fr", "fi", "t2"]
    if pool is None:
        tiles = [al(n) for n in names]
    else:
        tiles = pool[:len(names)]
    dt, x, mag, th, sn, cs_, tmp, ar1, ai, den, fr, fi, t2 = tiles
    sl = tuple(slice(None) for _ in shape)
    V = "vector"
    pg.act(dt[sl], ldt[sl], AF.Exp, [key + "in"], [key + "dt"])
    pg.op(V, "tensor_tensor", [key + "in", key + "dt"], [key + "x"], out=x[sl], in0=are[sl], in1=dt[sl], op=ALU.mult)
    pg.op(V, "tensor_scalar", [key + "x"], [key + "mag"], out=mag[sl], in0=x[sl], scalar1=1.0 / 5, scalar2=1.0, op0=ALU.mult, op1=ALU.add)
    for dv in (4.0, 3.0, 2.0, 1.0):
        pg.op(V, "tensor_tensor", [key + "x", key + "mag"], [key + "mag"], out=mag[sl], in0=mag[sl], in1=x[sl], op=ALU.mult)
        pg.op(V, "tensor_scalar", [key + "mag"], [key + "mag"], out=mag[sl], in0=mag[sl], scalar1=1.0 / dv, scalar2=1.0, op0=ALU.mult, op1=ALU.add)
    pg.op(V, "tensor_tensor", [key + "in", key + "dt"], [key + "arg"], out=th[sl], in0=aim[sl], in1=dt[sl], op=ALU.mult)
    sincos_tables(pg, nc, th[sl], sn[sl], cs_[sl], tmp[sl], key, negpi)
    pg.op(V, "tensor_tensor", [key + "mag", key + "cos"], [key + "ar1"], out=ar1[sl], in0=mag[sl], in1=cs_[sl], op=ALU.mult)
    pg.op(V, "tensor_scalar", [key + "ar1"], [key + "ar1"], out=ar1[sl], in0=ar1[sl], scalar1=-1.0, scalar2=None, op0=ALU.add)
    pg.op(V, "tensor_tensor", [key + "mag", key + "sin"], [key + "ai"], out=ai[sl], in0=mag[sl], in1=sn[sl], op=ALU.mult)
    pg.op(V, "tensor_tensor", [key + "in"], [key + "den"], out=den[sl], in0=are[sl], in1=are[sl], op=ALU.mult)
    pg.op(V, "tensor_tensor", [key + "in", key + "tmp"], [key + "tmp"], out=tmp[sl], in0=aim[sl], in1=aim[sl], op=ALU.mult)
    pg.op(V, "tensor_tensor", [key + "den", key + "tmp"], [key + "den"], out=den[sl], in0=den[sl], in1=tmp[sl], op=ALU.add)
    pg.op(V, "reciprocal", [key + "den"], [key + "den"], out=den[sl], in_=den[sl])
    pg.op(V, "tensor_tensor", [key + "ar1", key + "in"], [key + "fr"], out=fr[sl], in0=ar1[sl], in1=are[sl], op=ALU.mult)
    pg.op(V, "tensor_tensor", [key + "ai", key + "in", key + "tmp"], [key + "tmp"], out=tmp[sl], in0=ai[sl], in1=aim[sl], op=ALU.mult)
    pg.op(V, "tensor_tensor", [key + "fr", key + "tmp"], [key + "fr"], out=fr[sl], in0=fr[sl], in1=tmp[sl], op=ALU.add)
    pg.op(V, "tensor_tensor", [key + "fr", key + "den"], [key + "fr"], out=fr[sl], in0=fr[sl], in1=den[sl], op=ALU.mult)
    pg.op(V, "tensor_tensor", [key + "ai", key + "in"], [key + "fi"], out=fi[sl], in0=ai[sl], in1=are[sl], op=ALU.mult)
    pg.op(V, "tensor_tensor", [key + "ar1", key + "in"], [key + "t2"], out=t2[sl], in0=ar1[sl], in1=aim[sl], op=ALU.mult)
    pg.op(V, "tensor_tensor", [key + "fi", key + "t2"], [key + "fi"], out=fi[sl], in0=fi[sl], in1=t2[sl], op=ALU.subtract)
    pg.op(V, "tensor_tensor", [key + "fi", key + "den"], [key + "fi"], out=fi[sl], in0=fi[sl], in1=den[sl], op=ALU.mult)
    return mag, th, fr, fi


def phase_mlp(ctx, h_in, parts, w_up, w_dn, nw, nf, h_out, NT, final_norm, BS):
    nc, pg, sb = ctx.nc, ctx.pg, ctx.sb
    nparts = len(parts)
    blocks = None
    hin_v = h_in.rearrange("(k p) t -> p k t", p=128)
    hout_v = h_out.rearrange("(k p) t -> p k t", p=128)
    parts_v = [p_.rearrange("(k p) t -> p k t", p=128) for p_ in parts]
    wup = sb("wup", [128, 8, DFF], BF16)
    wdn = sb("wdn", [128, 32, D], BF16)
    nw_t = sb("nw_t", [128, 8], F32)
    nf_t = sb("nf_t", [128, 8], F32)
    ones = sb("ones", [128, 128], BF16)
    epsc = sb("epsc", [128, 1], F32)
    xb = [sb("xb%d" % i, [128, 8, BS], F32) for i in range(2)]
    hn = sb("hn", [128, 8, BS], BF16)
    hid = sb("hid", [128, 32, BS], BF16)
    rstd = sb("rstd", [128, BS], F32)
    tmp = [sb("tmp%d" % i, [128, BS], BF16) for i in range(4)]
    ps = ctx.ps

    pg.op("vector", "memset", [], ["ones"], ones[:, :], 1.0)
    pg.op("vector", "memset", [], ["epsc"], epsc[:, :], EPS)
    pg.dma("sync", nw_t[:, :], nw, writes=["nw"])
    pg.dma("sync", nf_t[:, :], nf, writes=["nf"])
    wup_v = w_up.rearrange("(k p) n -> p k n", p=128)
    wdn_v = w_dn.rearrange("(m p) d -> p m d", p=128)
    for k in range(8):
        pg.dma("gpsimd", wup[:, k, :], wup_v[:, k, :], writes=[("wup", k)])
    for j in range(8):
        pg.dma("gpsimd", wdn[:, 4 * j:4 * j + 4, :], wdn_v[:, 4 * j:4 * j + 4, :], writes=[("wdn", j)])

    if blocks is None:
        blocks = blocks_of(NT, pre=0, bs=BS)
    psi = 0
    def issue_loads(bi):
        t0_, nt_ = blocks[bi]
        s_ = bi % 2
        pg.dma("sync", xb[s_][:, :, 0:nt_], hin_v[:, :, t0_:t0_ + nt_], writes=[("xb", s_)], sem=("xsem", s_))
        for i in range(nparts):
            pg.dma("gpsimd", xb[s_][:, :, 0:nt_], parts_v[i][:, :, t0_:t0_ + nt_], writes=[("xb", s_)], sem=("xsem", s_),
                   accum_op=ALU.add)

    issue_loads(0)
    for bi, (t0, nt) in enumerate(blocks):
        s = bi % 2
        X = xb[s]
        xk = ("xb", s)
        if bi + 1 < len(blocks):
            issue_loads(bi + 1)
        rms_rstd(pg, X[:, :, 0:nt], lambda k: hn[:, k, 0:nt], ones[:, :], ps[7][:, 0:nt], rstd[:, 0:nt], D,
                 [xk], "hn", ("ps", 7), "rstd", 8, epsc[:, 0:1])
        for k in range(8):
            pg.op("vector", "scalar_tensor_tensor", [xk, "rstd", "nw"], ["hn"],
                  out=hn[:, k, 0:nt], in0=X[:, k, 0:nt], scalar=nw_t[:, k:k + 1], in1=rstd[:, 0:nt],
                  op0=ALU.mult, op1=ALU.mult)
        for m in range(32):
            p = psi % 6
            psi += 1
            for k in range(8):
                pg.mm(ps[p][:, 0:nt], wup[:, k, m * 128:(m + 1) * 128], hn[:, k, 0:nt], k == 0, k == 7,
                      ["hn", ("wup", k)], [("ps", p)])
            tq = m % 4
            pg.act(tmp[tq][:, 0:nt], ps[p][:, 0:nt], AF.Relu, [("ps", p)], [("tmp", tq)])
            pg.op("vector", "tensor_tensor", [("tmp", tq)], [("hid", m)],
                  out=hid[:, m, 0:nt], in0=tmp[tq][:, 0:nt], in1=tmp[tq][:, 0:nt], op=ALU.mult)
        for d in range(8):
            p = psi % 6
            psi += 1
            for m in range(32):
                pg.mm(ps[p][:, 0:nt], wdn[:, m, d * 128:(d + 1) * 128], hid[:, m, 0:nt], m == 0, m == 31,
                      [("hid", m), ("wdn", m // 4)], [("ps", p)])
            pg.op("vector", "tensor_tensor", [("ps", p), xk], [xk],
                  out=X[:, d, 0:nt], in0=X[:, d, 0:nt], in1=ps[p][:, 0:nt], op=ALU.add)
        if final_norm:
            rms_rstd(pg, X[:, :, 0:nt], lambda k: hn[:, k, 0:nt], ones[:, :], ps[7][:, 0:nt], rstd[:, 0:nt], D,
                     [xk], "hn", ("ps", 7), "rstd", 8, epsc[:, 0:1])
            for k in range(8):
                pg.op("vector", "scalar_tensor_tensor", [xk, "rstd", "nf"], [xk],
                      out=X[:, k, 0:nt], in0=X[:, k, 0:nt], scalar=nf_t[:, k:k + 1], in1=rstd[:, 0:nt],
                      op0=ALU.mult, op1=ALU.mult)
        pg.dma("sync", hout_v[:, :, t0:t0 + nt], X[:, :, 0:nt], reads=[xk], sem=("osem", s))


def phase_mix0(ctx, x_in, P_, part, T, blocks=None, dbg=9):
    nc, pg, sb = ctx.nc, ctx.pg, ctx.sb
    win_d, nw_d, vecs_d, wa_d, wx_d, wglu_d, wout_d, scol_d, srow_d, bT_d, cT_d, ident_d, kidx_d = [
        P_[k] for k in ["win", "nw", "vecs", "wa", "wx", "wglu", "wout", "scol", "srow", "bT", "cT", "ident", "kidx"]]
    xin_v = x_in.rearrange("(k p) t -> p k t", p=128)
    part_v = part.rearrange("(k p) t -> p k t", p=128)
    win = sb("win_s", [128, 8, 384], BF16)
    wout = sb("wout_s", [128, 2, D], BF16)
    nw_t = sb("nw_s", [128, 8]); vecs = sb("vecs_s", [128, 16])
    wa32 = sb("wa32", [128, 128]); wx32 = sb("wx32", [128, 128]); wglu32 = sb("wglu32", [128, 128])
    wa = sb("wa_s", [128, 128], BF16); wx = sb("wx_s", [128, 128], BF16); wglu = sb("wglu_s", [128, 128], BF16)
    scol = sb("scol_s", [128, 3, 4]); srow = sb("srow_s", [128, 3, 512])
    bT = sb("bT_s", [128, 2, 512]); cT = sb("cT_s", [128, 2, 512])
    ident = sb("ident_s", [128, 128]); kidx = sb("kidx_s", [128, BS])
    ones = sb("ones", [128, 128], BF16); epsc = sb("epsc", [128, 1]); negpi = sb("negpi", [128, 1]); onec = sb("onec", [128, 1])
    V = "vector"
    pg.op(V, "memset", [], ["ones"], ones[:, :], 1.0)
    pg.op(V, "memset", [], ["epsc"], epsc[:, :], EPS)
    pg.op(V, "memset", [], ["negpi"], negpi[:, :], -PI)
    pg.op(V, "memset", [], ["onec"], onec[:, :], 1.0)
    for t_, d_, k_ in [(nw_t, nw_d, "nw"), (vecs, vecs_d, "vecs"), (wa32, wa_d, "wa32"), (wx32, wx_d, "wx32"),
                       (wglu32, wglu_d, "wglu32"), (scol, scol_d, "colin"), (srow, srow_d, "rowin"),
                       (bT, bT_d, "bT"), (cT, cT_d, "cT"), (ident, ident_d, "ident"), (kidx, kidx_d, "kidx")]:
        sl = tuple(slice(None) for _ in t_.shape)
        pg.dma("sync", t_[sl], d_, writes=[k_])
    pg.dma("gpsimd", win[:, :, :], win_d.rearrange("(k p) n -> p k n", p=128), writes=["win"])
    pg.dma("gpsimd", wout[:, :, :], wout_d.rearrange("(k p) n -> p k n", p=128), writes=["wout"])
    pg.op(V, "tensor_copy", ["wa32"], ["wa"], out=wa[:, :], in_=wa32[:, :])
    pg.op(V, "tensor_copy", ["wx32"], ["wx"], out=wx[:, :], in_=wx32[:, :])
    pg.op(V, "tensor_copy", ["wglu32"], ["wglu"], out=wglu[:, :], in_=wglu32[:, :])
    cdiag = sb("cdiag", [128, 4, 128], BF16)
    for k in range(4):
        pg.op(V, "tensor_scalar", ["ident", "vecs"], ["cdiag"], out=cdiag[:, k, :], in0=ident[:, :],
              scalar1=vecs[:, k:k + 1], scalar2=None, op0=ALU.mult)
    lc = sb("lc", [128, 1])
    pg.act(lc[:, :], vecs[:, 7:8], AF.Exp, ["vecs"], ["lc"], scale=-1.0)
    pg.act(lc[:, :], lc[:, :], AF.Ln, ["lc", "onec"], ["lc"], bias=onec[:, 0:1])
    pg.op(V, "tensor_scalar", ["lc"], ["lc"], out=lc[:, :], in0=lc[:, :], scalar1=-8.0, scalar2=None, op0=ALU.mult)
    NB = 32
    bufs = [sb("w%d" % i, [128, BS]) for i in range(NB)]
    c_are = sb("c_are", [128, 4]); c_aim = sb("c_aim", [128, 4]); c_ldt = sb("c_ldt", [128, 4])
    for i, t_ in enumerate([c_are, c_aim, c_ldt]):
        pg.op(V, "tensor_copy", ["colin"], ["colin2"], out=t_[:, :], in_=scol[:, i, :])
    pg.op(V, "tensor_copy", ["colin2"], ["colin"], out=c_ldt[:, :], in_=c_ldt[:, :])
    magc, thc, _, _ = lam_parts(pg, nc, c_are, c_aim, c_ldt, [128, 4], "col", negpi[:, 0:1], sbf=sb)
    r_are, r_aim, r_ldt = bufs[13], bufs[14], bufs[15]
    for i, t_ in enumerate([r_are, r_aim, r_ldt]):
        pg.op(V, "tensor_copy", ["rowin"], ["rowin2"], out=t_[:, :], in_=srow[:, i, :])
    pg.op(V, "tensor_copy", ["rowin2"], ["rowin"], out=r_ldt[:, :], in_=r_ldt[:, :])
    _, _, frr, fir = lam_parts(pg, nc, r_are, r_aim, r_ldt, [128, 512], "row", negpi[:, 0:1], pool=bufs)
    bbT = sb("bbT", [128, 2, 512], BF16)
    tA, tB = bufs[16], bufs[17]
    pg.op(V, "tensor_tensor", ["bT", "rowfr"], ["tA"], out=tA[:, :], in0=bT[:, 0, :], in1=frr[:, :], op=ALU.mult)
    pg.op(V, "tensor_tensor", ["bT", "rowfi"], ["tB"], out=tB[:, :], in0=bT[:, 1, :], in1=fir[:, :], op=ALU.mult)
    pg.op(V, "tensor_tensor", ["tA", "tB"], ["bbT"], out=bbT[:, 0, :], in0=tA[:, :], in1=tB[:, :], op=ALU.subtract)
    pg.op(V, "tensor_tensor", ["bT", "rowfi", "tA"], ["tA"], out=tA[:, :], in0=bT[:, 0, :], in1=fir[:, :], op=ALU.mult)
    pg.op(V, "tensor_tensor", ["bT", "rowfr", "tB"], ["tB"], out=tB[:, :], in0=bT[:, 1, :], in1=frr[:, :], op=ALU.mult)
    pg.op(V, "tensor_tensor", ["tA", "tB"], ["bbT"], out=bbT[:, 1, :], in0=tA[:, :], in1=tB[:, :], op=ALU.add)
    ccT = sb("ccT", [128, 2, 512], BF16)
    pg.op(V, "tensor_copy", ["cT"], ["ccT"], out=ccT[:, 0, :], in_=cT[:, 0, :])
    pg.op(V, "tensor_scalar", ["cT", "ccT"], ["ccT"], out=ccT[:, 1, :], in0=cT[:, 1, :], scalar1=-1.0, scalar2=None, op0=ALU.mult)
    cosT = sb("cosT", [128, 4, BS]); sinT = sb("sinT", [128, 4, BS]); decT = sb("decT", [128, 4, BS])
    for j in range(4):
        key = "tb%d" % j
        ta = bufs[18 + 2 * j]
        pg.op(V, "tensor_scalar", ["kidx", "colarg"], [key + "arg"], out=ta[:, :], in0=kidx[:, :],
              scalar1=thc[:, j:j + 1], scalar2=None, op0=ALU.mult)
        tt = bufs[19 + 2 * j]
        sincos_tables(pg, nc, ta[:, :], sinT[:, j, :], cosT[:, j, :], tt[:, :], key, negpi[:, 0:1])
        pg.op(V, "tensor_scalar", ["kidx", "colmag"], ["decT"], out=decT[:, j, :], in0=kidx[:, :], scalar1=0.0,
              scalar2=magc[:, j:j + 1], op0=ALU.mult, op1=ALU.add)
    tabk = ["decT"] + ["tb%dsin" % j for j in range(4)] + ["tb%dcos" % j for j in range(4)]

    xb = [sb("xb%d" % i, [128, 8, BS]) for i in range(2)]
    hn = sb("hn", [128, 8, BS], BF16)
    rstd = sb("rstd", [128, BS])
    xlp = sb("xlp", [128, 4 + BS], BF16)
    bb16 = [sb("h%d" % i, [128, BS], BF16) for i in range(12)]
    Y = sb("Y", [128, 2, BS], BF16)
    ob = [sb("ob0", [128, 8, BS])] * 2
    hst = sb("hst", [128, 2])
    sst = sb("sst", [128, 4, 2, 2])
    stmp = sb("stmp", [128, 4, 4])
    ps = ctx.ps
    pg.op(V, "memset", [], ["xlp"], xlp[:, :], 0.0)
    pg.op(V, "memset", [], ["hst"], hst[:, :], 0.0)
    pg.op(V, "memset", [], ["sst"], sst[:, :, :, :], 0.0)
    pg.op(V, "memset", [], ["Y0"], Y[:, 0, :], 0.0)
    pg.op(V, "memset", [], ["Y1"], Y[:, 1, :], 0.0)

    if blocks is None:
        blocks = blocks_of(T)
    pg.barrier()
    G = "gpsimd"
    for bi, (t0, nt) in enumerate(blocks):
        s = bi % 2
        X = xb[s]; xk = ("xb", s)
        W = lambda i: bufs[i][:, 0:nt]
        Hh = lambda i: bb16[i][:, 0:nt]
        wk = lambda i: ("w", i)
        hk = lambda i: ("h", i)
        P = lambda i: ps[i][:, 0:nt]
        pk = lambda i: ("ps", i)
        pg.dma("sync", X[:, :, 0:nt], xin_v[:, :, t0:t0 + nt], writes=[xk], sem=("xsem", s))
        FL = "norm,hn,inproj,ev1,ev2,ev3,ev4".split(",")
        if "norm" in FL:
            rms_rstd(pg, X[:, :, 0:nt], lambda k: hn[:, k, 0:nt], ones[:, :], P(0), rstd[:, 0:nt], D,
                     [xk], "hn", pk(0), "rstd", 8, epsc[:, 0:1])
        if "hn" in FL:
            for k in range(8):
                pg.op(V, "scalar_tensor_tensor", [xk, "rstd", "nw"], ["hn"], out=hn[:, k, 0:nt], in0=X[:, k, 0:nt],
                      scalar=nw_t[:, k:k + 1], in1=rstd[:, 0:nt], op0=ALU.mult, op1=ALU.mult)
        if "inproj" in FL:
            for m in range(3):
                for k in range(8):
                    pg.mm(P(1 + m), win[:, k, m * 128:(m + 1) * 128], hn[:, k, 0:nt], k == 0, k == 7, ["hn", "win"], [pk(1 + m)])
        if "ev1" in FL:
            pg.act(xlp[:, 4:4 + nt], P(1), AF.Identity, [pk(1)], ["xlp"])
        if "ev2" in FL:
            pg.act(W(0), P(2), AF.Identity, [pk(2)], [wk(0)])
        if "ev3" in FL:
            pg.act(W(1), P(3), AF.Identity, [pk(3)], [wk(1)])
        if "ev4" in FL:
            pg.op(V, "tensor_copy", [pk(3)], [hk(0)], out=Hh(0), in_=P(3))
        if dbg >= 2:
            for k in range(4):
                pg.mm(P(4), cdiag[:, k, :], xlp[:, 1 + k:1 + k + nt], k == 0, k == 3, ["xlp", "cdiag"], [pk(4)])
            pg.act(W(2), P(4), AF.Identity, [pk(4), "vecs"], [wk(2)], bias=vecs[:, 4:5])
            pg.op(V, "tensor_copy", [wk(2)], [hk(1)], out=Hh(1), in_=W(2))
            pg.op(V, "tensor_copy", ["xlp"], ["xlp"], out=xlp[:, 0:4], in_=xlp[:, nt:nt + 4])
            pg.mm(P(5), wa[:, :], Hh(1), True, True, [hk(1), "wa"], [pk(5)])
            pg.mm(P(6), wx[:, :], Hh(1), True, True, [hk(1), "wx"], [pk(6)])
            pg.act(W(3), P(5), AF.Sigmoid, [pk(5), "vecs"], [wk(3)], bias=vecs[:, 5:6])
            pg.act(W(4), P(6), AF.Sigmoid, [pk(6), "vecs"], [wk(4)], bias=vecs[:, 6:7])
            pg.act(W(5), W(3), AF.Exp, [wk(3), "lc"], [wk(5)], scale=lc[:, 0:1])
            pg.op(G, "tensor_tensor", [wk(5)], [wk(6)], out=W(6), in0=W(5), in1=W(5), op=ALU.mult)
            pg.act(W(6), W(6), AF.Sqrt, [wk(6), "onec"], [wk(6)], scale=-1.0, bias=onec[:, 0:1])
            pg.op(G, "tensor_tensor", [wk(4), wk(2)], [wk(7)], out=W(7), in0=W(4), in1=W(2), op=ALU.mult)
            pg.op(G, "tensor_tensor", [wk(7), wk(6)], [wk(7)], out=W(7), in0=W(7), in1=W(6), op=ALU.mult)
            hin = hst[:, (bi % 2):(bi % 2) + 1]; hout = hst[:, ((bi + 1) % 2):((bi + 1) % 2) + 1]
            pg.op(V, "tensor_tensor_scan", [wk(5), wk(7), "hst"], [wk(8)], out=W(8), data0=W(5), data1=W(7),
                  initial=hin, op0=ALU.mult, op1=ALU.add)
            pg.op(V, "tensor_copy", [wk(8)], ["hst"], out=hout, in_=bufs[8][:, nt - 1:nt])
            gelu_tanh(pg, G, W(9), W(0), W(10), W(11), [wk(0)], [wk(9)], "ga")
            pg.op(V, "tensor_tensor", [wk(8), wk(9)], ["Y0"], out=Y[:, 0, 0:nt], in0=W(8), in1=W(9), op=ALU.mult)
        if dbg >= 3:
            for j in range(4):
                pa, pb = (4, 5) if j % 2 == 0 else (6, 7)
                pg.mm(P(pa), bbT[:, 0, j * 128:(j + 1) * 128], Hh(0), True, True, [hk(0), "bbT"], [pk(pa)])
                pg.mm(P(pb), bbT[:, 1, j * 128:(j + 1) * 128], Hh(0), True, True, [hk(0), "bbT"], [pk(pb)])
                o = 12 + (j % 2) * 8
                cs_ = cosT[:, j, 0:nt]; sn_ = sinT[:, j, 0:nt]
                pg.op(V, "tensor_tensor", [pk(pa)] + tabk, [wk(o)], out=W(o), in0=P(pa), in1=cs_, op=ALU.mult)
                pg.op(V, "tensor_tensor", [pk(pb)] + tabk, [wk(o + 1)], out=W(o + 1), in0=P(pb), in1=sn_, op=ALU.mult)
                pg.op(G, "tensor_tensor", [wk(o), wk(o + 1)], [wk(o)], out=W(o), in0=W(o), in1=W(o + 1), op=ALU.add)
                pg.op(V, "tensor_tensor", [pk(pb)] + tabk, [wk(o + 2)], out=W(o + 2), in0=P(pb), in1=cs_, op=ALU.mult)
                pg.op(V, "tensor_tensor", [pk(pa)] + tabk, [wk(o + 3)], out=W(o + 3), in0=P(pa), in1=sn_, op=ALU.mult)
                pg.op(G, "tensor_tensor", [wk(o + 2), wk(o + 3)], [wk(o + 2)], out=W(o + 2), in0=W(o + 2), in1=W(o + 3), op=ALU.subtract)
                pi_, po_ = bi % 2, (bi + 1) % 2
                pg.op(V, "tensor_tensor_scan", [wk(o), "sst"] + tabk, [wk(o + 4)], out=W(o + 4), data0=decT[:, j, 0:nt], data1=W(o),
                      initial=sst[:, j, 0, pi_:pi_ + 1], op0=ALU.mult, op1=ALU.add)
                pg.op(V, "tensor_tensor_scan", [wk(o + 2), "sst"] + tabk, [wk(o + 5)], out=W(o + 5), data0=decT[:, j, 0:nt], data1=W(o + 2),
                      initial=sst[:, j, 1, pi_:pi_ + 1], op0=ALU.mult, op1=ALU.add)
                cN = cosT[:, j, nt - 1:nt]; sN = sinT[:, j, nt - 1:nt]
                lre = bufs[o + 4][:, nt - 1:nt]; lim = bufs[o + 5][:, nt - 1:nt]
                pg.op(V, "tensor_tensor", [wk(o + 5)] + tabk, ["stmp"], out=stmp[:, j, 0:1], in0=lim, in1=sN, op=ALU.mult)
                pg.op(V, "scalar_tensor_tensor", [wk(o + 4), "stmp", "sst"] + tabk, ["sst"], out=sst[:, j, 0, po_:po_ + 1], in0=lre, scalar=cN,
                      in1=stmp[:, j, 0:1], op0=ALU.mult, op1=ALU.subtract)
                pg.op(V, "tensor_tensor", [wk(o + 4)] + tabk, ["stmp"], out=stmp[:, j, 1:2], in0=lre, in1=sN, op=ALU.mult)
                pg.op(V, "scalar_tensor_tensor", [wk(o + 5), "stmp", "sst"] + tabk, ["sst"], out=sst[:, j, 1, po_:po_ + 1], in0=lim, scalar=cN,
                      in1=stmp[:, j, 1:2], op0=ALU.mult, op1=ALU.add)
                pg.op(G, "tensor_tensor", [wk(o + 4)] + tabk, [wk(o + 6)], out=W(o + 6), in0=W(o + 4), in1=cs_, op=ALU.mult)
                pg.op(G, "tensor_tensor", [wk(o + 5)] + tabk, [wk(o + 7)], out=W(o + 7), in0=W(o + 5), in1=sn_, op=ALU.mult)
                pg.op(G, "tensor_tensor", [wk(o + 6), wk(o + 7)], [hk(2 + 2 * j)], out=Hh(2 + 2 * j), in0=W(o + 6), in1=W(o + 7), op=ALU.subtract)
                pg.op(G, "tensor_tensor", [wk(o + 4)] + tabk, [wk(o + 6)], out=W(o + 6), in0=W(o + 4), in1=sn_, op=ALU.mult)
                pg.op(G, "tensor_tensor", [wk(o + 5)] + tabk, [wk(o + 7)], out=W(o + 7), in0=W(o + 5), in1=cs_, op=ALU.mult)
                pg.op(G, "tensor_tensor", [wk(o + 6), wk(o + 7)], [hk(3 + 2 * j)], out=Hh(3 + 2 * j), in0=W(o + 6), in1=W(o + 7), op=ALU.add)
            for j in range(4):
                pg.mm(P(1), ccT[:, 0, j * 128:(j + 1) * 128], Hh(2 + 2 * j), j == 0, False, [hk(2 + 2 * j), "ccT"], [pk(1)])
                pg.mm(P(1), ccT[:, 1, j * 128:(j + 1) * 128], Hh(3 + 2 * j), False, j == 3, [hk(3 + 2 * j), "ccT"], [pk(1)])
            pg.op(V, "scalar_tensor_tensor", [wk(1), "vecs", pk(1)], [wk(28)], out=W(28), in0=W(1), scalar=vecs[:, 8:9],
                  in1=P(1), op0=ALU.mult, op1=ALU.add)
            gelu_tanh(pg, V, W(29), W(28), W(30), W(31), [wk(28)], [wk(29)], "gb")
            pg.op(V, "tensor_copy", [wk(29)], [hk(10)], out=Hh(10), in_=W(29))
            pg.mm(P(2), wglu[:, :], Hh(10), True, True, [hk(10), "wglu"], [pk(2)])
            pg.act(W(30), P(2), AF.Sigmoid, [pk(2), "vecs"], [wk(30)], bias=vecs[:, 9:10])
            pg.op(V, "tensor_tensor", [wk(29), wk(30)], ["Y1"], out=Y[:, 1, 0:nt], in0=W(29), in1=W(30), op=ALU.mult)
        O = ob[0]; okk = ("ob", 0)
        for d in range(8):
            p = 3 if d % 2 == 0 else 0
            pg.mm(P(p), wout[:, 0, d * 128:(d + 1) * 128], Y[:, 0, 0:nt], True, False, ["Y0", "wout"], [pk(p)])
            pg.mm(P(p), wout[:, 1, d * 128:(d + 1) * 128], Y[:, 1, 0:nt], False, True, ["Y1", "wout"], [pk(p)])
            if d % 2 == 0:
                pg.act(O[:, d, 0:nt], P(p), AF.Identity, [pk(p)], [okk])
            else:
                pg.op(V, "tensor_copy", [pk(p)], [okk], out=O[:, d, 0:nt], in_=P(p))
        pg.dma("sync", part_v[:, :, t0:t0 + nt], O[:, :, 0:nt], reads=[okk], sem=("osem", 0))


def phase_mix1(ctx, x_in, P_, part, T, blocks=None):
    nc, pg, sb = ctx.nc, ctx.pg, ctx.sb
    win_d, nw_d, cv_d, hv_d, cvec_d, wout_d, ident_d, mneg_d, seg_d, sel_d = [
        P_[k] for k in ["win", "nw", "cv", "hv", "cvec", "wout", "ident", "mneg", "seg", "sel"]]
    xin_v = x_in.rearrange("(k p) t -> p k t", p=128)
    part_v = part.rearrange("(k p) t -> p k t", p=128)
    V, G = "vector", "gpsimd"
    win = sb("win_s", [128, 8, NCOL], BF16)
    wout = sb("wout_s", [128, 4, D], BF16)
    nw_t = sb("nw_s", [128, 8]); cv = sb("cv_s", [128, 8, 5]); hv = sb("hv_s", [8, 2]); cvec = sb("cvec_s", [128, 8])
    ident = sb("ident_s", [128, 128]); identb = sb("identb", [128, 128], BF16)
    mneg = sb("mneg_s", [128, BS]); mnegb = sb("mnegb", [128, BS], BF16)
    seg = sb("seg_s", [8, BS]); sel = sb("sel_s", [8, 8 * 128])
    ones = sb("ones", [128, 128], BF16); ones32 = sb("ones32", [8, 128])
    epsc = sb("epsc", [128, 1]); onec = sb("onec", [128, 1])
    pg.op(V, "memset", [], ["ones"], ones[:, :], 1.0)
    pg.op(V, "memset", [], ["ones32"], ones32[:, :], 1.0)
    pg.op(V, "memset", [], ["epsc"], epsc[:, :], EPS)
    pg.op(V, "memset", [], ["onec"], onec[:, :], 1.0)
    for t_, d_, k_ in [(nw_t, nw_d, "nw"), (cv, cv_d, "cv"), (hv, hv_d, "hv"), (cvec, cvec_d, "cvec"),
                       (ident, ident_d, "ident"), (mneg, mneg_d, "mneg"), (seg, seg_d, "seg"), (sel, sel_d, "sel")]:
        sl = tuple(slice(None) for _ in t_.shape)
        pg.dma("sync", t_[sl], d_, writes=[k_])
    win_v = win_d.rearrange("(k p) n -> p k n", p=128)
    for k in range(8):
        pg.dma("gpsimd", win[:, k, :], win_v[:, k, :], writes=[("win", k)])
    pg.dma("gpsimd", wout[:, :, :], wout_d.rearrange("(k p) n -> p k n", p=128), writes=["wout"])
    pg.op(V, "tensor_copy", ["ident"], ["identb"], out=identb[:, :], in_=ident[:, :])
    pg.op(V, "tensor_copy", ["mneg"], ["mnegb"], out=mnegb[:, :], in_=mneg[:, :])
    cdiag = sb("cdiag", [128, 8, 4, 128], BF16)
    for m in range(8):
        for k in range(4):
            pg.op(V, "tensor_scalar", ["ident", "cv"], ["cdiag"], out=cdiag[:, m, k, :], in0=ident[:, :],
                  scalar1=cv[:, m, k:k + 1], scalar2=None, op0=ALU.mult)
    ah = sb("ah", [8, 1])
    pg.act(ah[:, :], hv[:, 1:2], AF.Exp, ["hv"], ["ah"])
    pg.op(V, "tensor_scalar", ["ah"], ["ah"], out=ah[:, :], in0=ah[:, :], scalar1=-1.0, scalar2=None, op0=ALU.mult)

    xb = [sb("xb%d" % i, [128, 8, BS]) for i in range(2)]
    hn = sb("hn", [128, 8, BS], BF16)
    rstd = sb("rstd", [128, BS])
    zs = sb("zs", [128, 4, BS])
    xbc = sb("xbc", [128, 8, 4 + BS], BF16)
    xc = sb("xc", [128, 8, BS], BF16)
    dtf = [sb("dtf%d" % i, [8, BS]) for i in range(6)]
    tm2 = [sb("tm%d" % i, [128, 32]) for i in range(2)]
    diag82 = [sb("diag8%d" % i, [8, 8]) for i in range(2)]
    decb2 = [sb("decb%d" % i, [128, 8]) for i in range(2)]
    Hs = sb("Hs", [128, 8, 64]); Hbf = sb("Hbf", [128, 512], BF16)
    E = [sb("E%d" % i, [128, 128], BF16) for i in range(8)]
    Gh = [sb("Gh%d" % i, [128, 128], BF16) for i in range(8)]
    xdt2 = [sb("xdt%d" % i, [128, 8, 64], BF16) for i in range(2)]; xw2 = [sb("xw%d" % i, [128, 8, 64], BF16) for i in range(2)]
    btm2 = [sb("btm%d" % i, [128, 256], BF16) for i in range(2)]
    tyo2 = [sb("tyo%d" % i, [128, 8, 64]) for i in range(2)]; ytm2 = [sb("ytm%d" % i, [128, 512], BF16) for i in range(2)]
    yfm = sb("yfm", [128, 4, BS])
    gg = sb("gg", [128, 4, BS]); sq = sb("sq", [128, 4, BS], BF16); rs2 = sb("rs2", [128, 2, BS])
    GN = sb("GN", [128, 4, BS], BF16)
    ob = sb("ob", [128, 8, BS])
    ps = ctx.ps
    psb = ctx.psb
    pg.op(V, "memset", [], ["xbc"], xbc[:, :, :], 0.0)
    pg.op(V, "memset", [], ["H"], Hs[:, :, :], 0.0)
    pg.op(V, "memset", [], ["Hbf"], Hbf[:, :], 0.0)
    pg.barrier()

    if blocks is None:
        blocks = blocks_of(T)
    pr = 0
    cpar = 0
    for bi, (t0, nt) in enumerate(blocks):
        s = bi % 2
        X = xb[s]; xk = ("xb", s)
        P = lambda i: ps[i][:, 0:nt]
        pk = lambda i: ("ps", i)
        pg.dma("sync", X[:, :, 0:nt], xin_v[:, :, t0:t0 + nt], writes=[xk], sem=("xsem", s))
        rms_rstd(pg, X[:, :, 0:nt], lambda k: hn[:, k, 0:nt], ones[:, :], P(0), rstd[:, 0:nt], D,
                 [xk], "hn", pk(0), "rstd", 8, epsc[:, 0:1])
        for k in range(8):
            pg.op(V, "scalar_tensor_tensor", [xk, "rstd", "nw"], ["hn"], out=hn[:, k, 0:nt], in0=X[:, k, 0:nt],
                  scalar=nw_t[:, k:k + 1], in1=rstd[:, 0:nt], op0=ALU.mult, op1=ALU.mult)
        wk_all = [("win", k) for k in range(8)]
        for m in range(12):
            p = 1 + (pr % 2); pr += 1
            for k in range(8):
                pg.mm(P(p), win[:, k, m * 128:(m + 1) * 128], hn[:, k, 0:nt], k == 0, k == 7, ["hn"] + wk_all, [pk(p)])
            if m < 4:
                pg.act(zs[:, m, 0:nt], P(p), AF.Silu, [pk(p)], [("zs", m)])
            else:
                pg.act(xbc[:, m - 4, 4:4 + nt], P(p), AF.Identity, [pk(p)], [("xbc", m - 4)])
        p = 1 + (pr % 2); pr += 1
        for k in range(8):
            pg.mm(ps[p][0:8, 0:nt], win[:, k, 1536:1544], hn[:, k, 0:nt], k == 0, k == 7, ["hn"] + wk_all, [pk(p)])
        e1, dtA, cs, ncs, tend, fs = [t_[:, 0:nt] for t_ in dtf]
        pg.act(e1, ps[p][0:8, 0:nt], AF.Exp, [pk(p), "hv"], ["e1"], bias=hv[:, 0:1])
        pg.act(e1, e1, AF.Ln, ["e1", "onec"], ["e1"], bias=onec[0:8, 0:1])
        pg.op(V, "tensor_scalar", ["e1", "ah"], ["dtA"], out=dtA, in0=e1, scalar1=ah[:, 0:1], scalar2=None, op0=ALU.mult)
        pg.op(V, "tensor_tensor_scan", ["dtA", "seg"], ["cs"], out=cs, data0=seg[:, 0:nt], data1=dtA, initial=0.0,
              op0=ALU.mult, op1=ALU.add)
        pg.op(V, "tensor_scalar", ["cs"], ["ncs"], out=ncs, in0=cs, scalar1=-1.0, scalar2=None, op0=ALU.mult)
        chunks = [(c0, min(128, nt - c0)) for c0 in range(0, nt, 128)]
        for (c0, L) in chunks:
            pg.op(V, "tensor_scalar", ["cs"], ["tend"], out=dtf[4][:, c0:c0 + L], in0=dtf[2][:, c0:c0 + L], scalar1=-1.0,
                  scalar2=dtf[2][:, c0 + L - 1:c0 + L], op0=ALU.mult, op1=ALU.add)
        pg.act(tend, tend, AF.Exp, ["tend"], ["tend"])
        pg.act(fs, cs, AF.Exp, ["cs"], ["fs"])
        for m in range(8):
            p = 1 + (pr % 2); pr += 1
            for k in range(4):
                pg.mm(P(p), cdiag[:, m, k, :], xbc[:, m, 1 + k:1 + k + nt], k == 0, k == 3, [("xbc", m), "cdiag"], [pk(p)])
            pg.act(xc[:, m, 0:nt], P(p), AF.Silu, [pk(p), "cv"], [("xc", m)], bias=cv[:, m, 4:5])
            pg.op(V, "tensor_copy", [("xbc", m)], [("xbc", m)], out=xbc[:, m, 0:4], in_=xbc[:, m, nt:nt + 4])
        for (c0, L) in chunks:
            par = cpar % 2; cpar += 1
            tm, diag8, decb, xdt, xw, btm, tyo, ytm = tm2[par], diag82[par], decb2[par], xdt2[par], xw2[par], btm2[par], tyo2[par], ytm2[par]
            TM, DG, DC, XD, XW, BT, TY, YT = [(n_, par) for n_ in ["tm", "diag8", "decb", "xdt", "xw", "btm", "tyo", "ytm"]]
            pA = 5 if par == 0 else 0
            for qi, (src, key) in enumerate([(dtf[0], "e1"), (dtf[3], "ncs"), (dtf[4], "tend"), (dtf[5], "fs")]):
                pg.mm(ps[pA][0:L, 256 + 8 * qi:256 + 8 * qi + 8], src[0:8, c0:c0 + L], ident[0:8, 0:8], True, True,
                      [key, "ident"], [pk(pA)])
            pg.act(tm[0:L, :], ps[pA][0:L, 256:288], AF.Identity, [pk(pA)], [TM])
            pg.op(V, "tensor_scalar", ["ident", "cs"], [DG], out=diag8[:, :], in0=ident[0:8, 0:8],
                  scalar1=dtf[2][:, c0 + L - 1:c0 + L], scalar2=None, op0=ALU.mult)
            pg.mm(ps[pA][:, 288:296], ones32[:, :], diag8[:, :], True, True, ["ones32", DG], [pk(pA)])
            pg.act(decb[:, :], ps[pA][:, 288:296], AF.Exp, [pk(pA)], [DC])
            for g in range(2):
                pg.mm(ps[pA][0:L, g * 128:g * 128 + L], xc[:, 4 + g, c0:c0 + L], xc[:, 6 + g, c0:c0 + L], True, True,
                      [("xc", 4 + g), ("xc", 6 + g)], [pk(pA)])
            for hg in range(2):
                bk = 3 if hg == 0 else 1
                for h in range(4 * hg, 4 * hg + 4):
                    col = (h % 4) * 128
                    pg.mm(ps[bk][:, col:col + L], sel[:, h * 128:(h + 1) * 128], dtf[2][:, c0:c0 + L], True, False, ["sel", "cs"], [pk(bk)])
                    pg.mm(ps[bk][:, col:col + L], identb[:, :], mnegb[:, 0:L], False, True, ["identb", "mnegb"], [pk(bk)])
                for h in range(4 * hg, 4 * hg + 4):
                    col = (h % 4) * 128
                    pg.act(E[h][0:L, 0:L], ps[bk][0:L, col:col + L], AF.Exp, [pk(bk), TM], [("E", h)], bias=tm[0:L, 8 + h:9 + h])
            for h in range(8):
                g = h // 4
                pg.op(V, "tensor_tensor", [("E", h), pk(pA)], [("G", h)], out=Gh[h][0:L, 0:L], in0=ps[pA][0:L, g * 128:g * 128 + L],
                      in1=E[h][0:L, 0:L], op=ALU.mult)
            for m in range(4):
                pg.op("tensor", "transpose", [("xc", m), "identb"], [("ps", "b")], psb[0:L, m * 128:(m + 1) * 128], xc[:, m, c0:c0 + L], identb[:, :])
            for g in range(2):
                pg.op("tensor", "transpose", [("xc", 4 + g), "identb"], [("ps", "b")], psb[0:L, 512 + g * 128:512 + (g + 1) * 128],
                      xc[:, 4 + g, c0:c0 + L], identb[:, :])
            psx = psb[0:L, 0:512].rearrange("p (h d) -> p h d", h=8)
            pg.op(V, "tensor_tensor", [("ps", "b"), TM], [XD], out=xdt[0:L, :, :], in0=psx,
                  in1=tm[0:L, 0:8].unsqueeze(2).broadcast_to([L, 8, 64]), op=ALU.mult)
            pg.op(V, "tensor_copy", [("ps", "b")], [BT], out=btm[0:L, :], in_=psb[0:L, 512:768])
            pg.op(G, "tensor_tensor", [XD, TM], [XW], out=xw[0:L, :, :], in0=xdt[0:L, :, :],
                  in1=tm[0:L, 16:24].unsqueeze(2).broadcast_to([L, 8, 64]), op=ALU.mult)
            for h in range(8):
                pg.mm(ps[6][0:L, h * 64:(h + 1) * 64], Gh[h][0:L, 0:L], xdt[0:L, h, :], True, True, [("G", h), XD], [pk(6)])
            for g in range(2):
                pg.mm(ps[7][0:L, g * 256:(g + 1) * 256], xc[:, 6 + g, c0:c0 + L], Hbf[:, g * 256:(g + 1) * 256], True, True,
                      [("xc", 6 + g), "Hbf"], [pk(7)])
            ps7v = ps[7][0:L, :].rearrange("p (h d) -> p h d", h=8)
            pg.op(V, "tensor_tensor", [pk(7), TM], [TY], out=tyo[0:L, :, :], in0=ps7v,
                  in1=tm[0:L, 24:32].unsqueeze(2).broadcast_to([L, 8, 64]), op=ALU.mult)
            pg.op(V, "tensor_tensor", [pk(6), TY], [YT], out=ytm[0:L, :], in0=ps[6][0:L, :],
                  in1=tyo[0:L, :, :].rearrange("p h d -> p (h d)"), op=ALU.add)
            for g in range(2):
                pg.mm(ps[2][:, g * 256:(g + 1) * 256], btm[0:L, g * 128:(g + 1) * 128],
                      xw[0:L, 4 * g:4 * g + 4, :].rearrange("p h d -> p (h d)"), True, True, [BT, XW], [pk(2)])
            pg.op(V, "tensor_tensor", ["H", DC], ["H"], out=Hs[:, :, :], in0=Hs[:, :, :],
                  in1=decb[:, :].unsqueeze(2).broadcast_to([128, 8, 64]), op=ALU.mult)
            pg.op(V, "tensor_tensor", ["H", pk(2)], ["H"], out=Hs[:, :, :], in0=Hs[:, :, :],
                  in1=ps[2][:, :].rearrange("p (h d) -> p h d", h=8), op=ALU.add)
            pg.op(G, "tensor_copy", ["H"], ["Hbf"], out=Hbf[:, :], in_=Hs[:, :, :].rearrange("p h d -> p (h d)"))
            for m in range(4):
                pg.op("tensor", "transpose", [YT, "identb"], [("ps", "b")], psb[:, m * 128:m * 128 + L], ytm[0:L, m * 128:(m + 1) * 128], identb[0:L, 0:L])
            for m in range(4):
                pg.act(yfm[:, m, c0:c0 + L], psb[:, m * 128:m * 128 + L], AF.Identity, [("ps", "b")], [("yfm", m)])
        for m in range(4):
            pg.op(V, "scalar_tensor_tensor", [("xc", m), "cvec", ("yfm", m)], [("yfm", m)], out=yfm[:, m, 0:nt], in0=xc[:, m, 0:nt],
                  scalar=cvec[:, m:m + 1], in1=yfm[:, m, 0:nt], op0=ALU.mult, op1=ALU.add)
            pg.op(G, "tensor_tensor", [("yfm", m), ("zs", m)], [("gg", m)], out=gg[:, m, 0:nt], in0=yfm[:, m, 0:nt], in1=zs[:, m, 0:nt], op=ALU.mult)
            pg.act(sq[:, m, 0:nt], gg[:, m, 0:nt], AF.Square, [("gg", m)], [("sq", m)])
        for g in range(2):
            pg.mm(P(0), ones[:, :], sq[:, 2 * g, 0:nt], True, False, [("sq", 2 * g), "ones"], [pk(0)])
            pg.mm(P(0), ones[:, :], sq[:, 2 * g + 1, 0:nt], False, True, [("sq", 2 * g + 1), "ones"], [pk(0)])
            pg.act(rs2[:, g, 0:nt], P(0), AF.Sqrt, [pk(0), "epsc"], [("rs2", g)], scale=1.0 / 256, bias=epsc[:, 0:1])
            pg.op(V, "reciprocal", [("rs2", g)], [("rs2", g)], out=rs2[:, g, 0:nt], in_=rs2[:, g, 0:nt])
        for m in range(4):
            pg.op(V, "scalar_tensor_tensor", [("gg", m), "cvec", ("rs2", m // 2)], [("GN", m)], out=GN[:, m, 0:nt], in0=gg[:, m, 0:nt],
                  scalar=cvec[:, 4 + m:5 + m], in1=rs2[:, m // 2, 0:nt], op0=ALU.mult, op1=ALU.mult)
        for d in range(8):
            p = 1 + (pr % 2); pr += 1
            for m in range(4):
                pg.mm(P(p), wout[:, m, d * 128:(d + 1) * 128], GN[:, m, 0:nt], m == 0, m == 3, [("GN", m), "wout"], [pk(p)])
            if d % 2 == 0:
                pg.act(ob[:, d, 0:nt], P(p), AF.Identity, [pk(p)], ["ob"])
            else:
                pg.op(V, "tensor_copy", [pk(p)], ["ob"], out=ob[:, d, 0:nt], in_=P(p))
        pg.dma("sync", part_v[:, :, t0:t0 + nt], ob[:, :, 0:nt], reads=["ob"], sem=("osem", 0))


M0_SHAPES = {'win': [1024, 384], 'nw': [128, 8], 'vecs': [128, 16], 'wa': [128, 128], 'wx': [128, 128], 'wglu': [128, 128], 'wout': [256, 1024], 'scol': [128, 3, 4], 'srow': [128, 3, 512], 'bT': [128, 2, 512], 'cT': [128, 2, 512], 'ident': [128, 128], 'kidx': [128, 512]}
M1_SHAPES = {'win': [1024, 1544], 'nw': [128, 8], 'cv': [128, 8, 5], 'hv': [8, 2], 'cvec': [128, 8], 'wout': [512, 1024], 'ident': [128, 128], 'mneg': [128, 512], 'seg': [8, 512], 'sel': [8, 1024]}


def build_fused(T, mlp_bs=410):
    nc = bass.Bass("TRN2", target_bir_lowering=False)
    din = lambda n, s: nc.dram_tensor(n, list(s), F32, kind="ExternalInput").ap()
    x_in = din("x_in", [D, T])
    out = nc.dram_tensor("out", [D, T], F32, kind="ExternalOutput").ap()
    parts = [nc.dram_tensor("part%d" % q, [D, T], F32).ap() for q in range(4)]
    h1 = nc.dram_tensor("h1", [D, T], F32).ap()
    PA = [{k: din("a%d_%s" % (q, k), s) for k, s in M0_SHAPES.items()} for q in range(4)]
    PB = [{k: din("b%d_%s" % (q, k), s) for k, s in M1_SHAPES.items()} for q in range(4)]
    mw = [{"w_up": din("m%d_w_up" % l, [D, DFF]), "w_dn": din("m%d_w_dn" % l, [DFF, D]), "nw": din("m%d_nw" % l, [128, 8])}
          for l in range(2)]
    nf = din("nf", [128, 8])
    pg = Prog(nc)
    ctx = Ctx(nc, pg)
    first = True
    for q in range(4):
        if not first:
            pg.phase(); ctx.reset()
        first = False
        phase_mix0(ctx, x_in, PA[q], parts[q], T)
    pg.phase(); ctx.reset()
    phase_mlp(ctx, x_in, parts, mw[0]["w_up"], mw[0]["w_dn"], mw[0]["nw"], nf, h1, T, False, mlp_bs)
    for q in range(4):
        pg.phase(); ctx.reset()
        phase_mix1(ctx, h1, PB[q], parts[q], T)
    pg.phase(); ctx.reset()
    phase_mlp(ctx, h1, parts, mw[1]["w_up"], mw[1]["w_dn"], mw[1]["nw"], nf, out, T, True, mlp_bs)
    pg.finish()
    stats = pg.emit()
    return nc, stats


import numpy as np

def lay8(v):
    return np.ascontiguousarray(v.reshape(-1, 128).T.astype(np.float32))

def prep_mix0(inp, q):
    f = np.float32
    w_in = inp["ev_w_in"][0]
    sl = slice(128 * q, 128 * q + 128)
    win = np.concatenate([w_in[:, 0:512][:, sl], w_in[:, 512:1024][:, sl], w_in[:, 1024:1536][:, sl]], axis=1)
    vecs = np.zeros((128, 16), f)
    vecs[:, 0:4] = inp["lru_conv_w"][0][:, sl].T
    vecs[:, 4] = inp["lru_conv_b"][0][sl]
    vecs[:, 5] = inp["lru_b_a"][0][sl]
    vecs[:, 6] = inp["lru_b_x"][0][sl]
    vecs[:, 7] = inp["lru_lambda"][0][sl]
    vecs[:, 8] = inp["s5_d"][0][sl]
    vecs[:, 9] = inp["s5_b_glu"][0][sl]
    def bd(blocks):
        n = blocks[0].shape[0]
        m = np.zeros((n * len(blocks), n * len(blocks)), f)
        for i, b in enumerate(blocks):
            m[i * n:(i + 1) * n, i * n:(i + 1) * n] = b
        return m
    wa = bd([inp["lru_w_a"][0][2 * q], inp["lru_w_a"][0][2 * q + 1]])
    wx = bd([inp["lru_w_x"][0][2 * q], inp["lru_w_x"][0][2 * q + 1]])
    wglu = bd([inp["s5_w_glu"][0][8 * q + g] for g in range(8)])
    w_out = inp["ev_w_out"][0]
    wout = np.concatenate([w_out[0:512][sl], w_out[512:1024][sl]], axis=0)
    scol = np.zeros((128, 3, 4), f)
    srow = np.zeros((128, 3, 512), f)
    bT = np.zeros((128, 2, 512), f)
    cT = np.zeros((128, 2, 512), f)
    for j in range(4):
        for slot in range(2):
            gl = 2 * j + slot
            g = 8 * q + gl
            sp = slice(slot * 64, slot * 64 + 64)
            scol[sp, 0, j] = inp["s5_a_re"][0][g]
            scol[sp, 1, j] = inp["s5_a_im"][0][g]
            scol[sp, 2, j] = inp["s5_log_dt"][0][g]
            cs = slice(j * 128 + slot * 64, j * 128 + slot * 64 + 64)
            srow[:, 0, cs] = inp["s5_a_re"][0][g][None, :]
            srow[:, 1, cs] = inp["s5_a_im"][0][g][None, :]
            srow[:, 2, cs] = inp["s5_log_dt"][0][g]
            ch = slice(gl * 16, gl * 16 + 16)
            bT[ch, 0, cs] = inp["s5_b_re"][0][g].T
            bT[ch, 1, cs] = inp["s5_b_im"][0][g].T
            cc = slice(j * 128 + gl * 16, j * 128 + gl * 16 + 16)
            cT[sp, 0, cc] = inp["s5_c_re"][0][g].T
            cT[sp, 1, cc] = inp["s5_c_im"][0][g].T
    return {
        "win": np.ascontiguousarray(win), "nw": lay8(inp["norm_mix"][0]), "vecs": vecs, "wa": wa, "wx": wx,
        "wglu": wglu, "wout": np.ascontiguousarray(wout), "scol": scol, "srow": srow, "bT": bT, "cT": cT,
        "ident": np.eye(128, dtype=f), "kidx": np.tile(np.arange(1, 513, dtype=f)[None, :], (128, 1)),
    }

def seq_fm(inp, b):
    return np.ascontiguousarray(np.concatenate([inp["meta_tokens"], inp["x"][b]], axis=0).T.astype(np.float32))


def prep_mix1(inp, q):
    f = np.float32
    w = inp["ssd_w_in"][0]
    cols = np.concatenate([np.arange(512 * q, 512 * q + 512), 2048 + np.arange(512 * q, 512 * q + 512),
                           4096 + np.arange(256 * q, 256 * q + 256), 5120 + np.arange(256 * q, 256 * q + 256),
                           6144 + np.arange(8 * q, 8 * q + 8)])
    win = np.ascontiguousarray(w[:, cols])
    cch = np.concatenate([np.arange(512 * q, 512 * q + 512), 2048 + np.arange(256 * q, 256 * q + 256),
                          3072 + np.arange(256 * q, 256 * q + 256)])
    cw = inp["ssd_conv_w"][0][:, cch]
    cb = inp["ssd_conv_b"][0][cch]
    cv = np.zeros((128, 8, 5), f)
    for m in range(8):
        cv[:, m, 0:4] = cw[:, m * 128:(m + 1) * 128].T
        cv[:, m, 4] = cb[m * 128:(m + 1) * 128]
    hv = np.stack([inp["ssd_dt_bias"][0][8 * q:8 * q + 8], inp["ssd_a_log"][0][8 * q:8 * q + 8]], axis=1).astype(f)
    cvec = np.zeros((128, 8), f)
    dd = np.repeat(inp["ssd_d"][0][8 * q:8 * q + 8], 64)
    nn = inp["ssd_norm"][0][512 * q:512 * q + 512]
    for m in range(4):
        cvec[:, m] = dd[m * 128:(m + 1) * 128]
        cvec[:, 4 + m] = nn[m * 128:(m + 1) * 128]
    wout = np.ascontiguousarray(inp["ssd_w_out"][0][512 * q:512 * q + 512])
    s_idx = np.arange(128)[:, None]
    l_idx = (np.arange(512) % 128)[None, :]
    mneg = np.where(l_idx >= s_idx, 0.0, -30000.0).astype(f)
    seg = np.ones((8, 512), f)
    seg[:, ::128] = 0.0
    sel = np.zeros((8, 8, 128), f)
    for h in range(8):
        sel[h, h, :] = 1.0
    return {"win": win, "nw": lay8(inp["norm_mix"][1]), "cv": cv, "hv": np.ascontiguousarray(hv), "cvec": cvec,
            "wout": wout, "ident": np.eye(128, dtype=f), "mneg": mneg, "seg": seg, "sel": sel.reshape(8, 1024)}

_CACHE = {}
T_SEQ = 16400


def kernel(**inputs):
    inp = {k: np.asarray(v) for k, v in inputs.items()}
    if "nc" not in _CACHE:
        _CACHE["nc"] = build_fused(T_SEQ, mlp_bs=410)[0]
    nc = _CACHE["nc"]
    base = {}
    for q in range(4):
        for k, v in prep_mix0(inp, q).items():
            base["a%d_%s" % (q, k)] = v
        for k, v in prep_mix1(inp, q).items():
            base["b%d_%s" % (q, k)] = v
    for l in range(2):
        base["m%d_w_up" % l] = np.ascontiguousarray(inp["mlp_w_up"][l])
        base["m%d_w_dn" % l] = np.ascontiguousarray(inp["mlp_w_down"][l])
        base["m%d_nw" % l] = lay8(inp["norm_mlp"][l])
    base["nf"] = lay8(inp["norm_final"])
    seqs = [seq_fm(inp, b) for b in range(2)]
    in_maps = []
    for c in range(8):
        m = dict(base)
        m["x_in"] = seqs[c % 2]
        in_maps.append(m)
    res = run_bass_kernel_spmd(nc, in_maps, core_ids=list(range(8))).results
    out = np.stack([np.ascontiguousarray(res[b]["out"][:, 16:].T) for b in range(2)], axis=0)
    return out.astype(np.float32)
```

```python
from concourse.bass_utils import run_bass_kernel_spmd


import numpy as np
import concourse.bass as bass
import concourse.mybir as mybir

F32 = mybir.dt.float32
BF16 = mybir.dt.bfloat16
AF = mybir.ActivationFunctionType
ALU = mybir.AluOpType
AX = mybir.AxisListType

ENGS = ["tensor", "vector", "scalar", "gpsimd", "sync"]


class Prog:
    def __init__(self, nc):
        self.nc = nc
        self.ins = {e: [] for e in ENGS}
        self.res = {}
        self.dma_sem = {}
        self.waited = {e: {} for e in ENGS}
        self.esems = [{e: nc.alloc_semaphore("prg0_" + e) for e in ENGS}]
        self.gen_start = {e: [0] for e in ENGS}
        self.pdi = 0

    def _deps(self, eng, reads, writes):
        deps = []
        for r in reads:
            st = self.res.get(r)
            if st and st["w"] is not None:
                deps.append(st["w"])
        for w in writes:
            st = self.res.get(w)
            if st:
                if st["w"] is not None:
                    deps.append(st["w"])
                deps.extend(st["r"])
        out = []
        wd = self.waited[eng]
        for d in deps:
            if d[0] == "e":
                if d[1] == eng and eng in ("tensor", "sync"):
                    continue
                key = ("e", d[1])
                if wd.get(key, -1) >= d[2]:
                    continue
                wd[key] = d[2]
                out.append(d)
            else:
                key = ("d", d[1])
                if wd.get(key, -1) >= d[2]:
                    continue
                wd[key] = d[2]
                out.append(d)
        return out

    def _commit(self, me, reads, writes):
        for r in reads:
            st = self.res.setdefault(r, {"w": None, "r": []})
            st["r"].append(me)
        for w in writes:
            self.res[w] = {"w": me, "r": []}

    def op(self, eng, name, reads, writes, *args, **kw):
        fn = (lambda e, name=name, args=args, kw=kw: getattr(e, name)(*args, **kw))
        ex = [r for r in reads if isinstance(r, tuple) and r[0] == "ps"]
        if ex:
            reads = [r for r in reads if r not in ex]
            writes = list(writes) + [r for r in ex if r not in writes]
        waits = self._deps(eng, reads, writes)
        idx = len(self.ins[eng])
        self.ins[eng].append({"fn": fn, "waits": waits, "dma": None})
        self._commit(("e", eng, idx), reads, writes)

    def mm(self, out, lhsT, rhs, start, stop, reads, writes, **kw):
        self.op("tensor", "matmul", reads, writes, out, lhsT=lhsT, rhs=rhs, start=start, stop=stop, **kw)

    def act(self, out, in_, func, reads, writes, **kw):
        self.op("scalar", "activation", reads, writes, out=out, in_=in_, func=func, **kw)

    def dma(self, eng, out, in_, reads=(), writes=(), sem=None, **dkw):
        if sem is None:
            sem = ("pd", self.pdi)
            self.pdi += 1
        if sem not in self.dma_sem:
            self.dma_sem[sem] = [self.nc.alloc_semaphore("d%d" % len(self.dma_sem)), 0]
        waits = self._deps(eng, reads, writes)
        self.dma_sem[sem][1] += 16
        val = self.dma_sem[sem][1]
        h = self.dma_sem[sem][0]
        self.ins[eng].append({"fn": (lambda e, out=out, in_=in_, dkw=dkw: e.dma_start(out=out, in_=in_, **dkw)),
                              "waits": waits, "dma": h})
        self._commit(("d", sem, val), reads, writes)

    def barrier(self):
        last = {}
        for e in ENGS:
            for idx in range(len(self.ins[e]) - 1, -1, -1):
                if self.ins[e][idx]["dma"] is None and self.ins[e][idx]["fn"] is not None:
                    last[e] = idx
                    break
        for eng in ENGS:
            waits = []
            for key, (h, tot) in self.dma_sem.items():
                if self.waited[eng].get(("d", key), -1) < tot:
                    waits.append(("d", key, tot))
                    self.waited[eng][("d", key)] = tot
            for e, idx in last.items():
                if e != eng and self.waited[eng].get(("e", e), -1) < idx:
                    waits.append(("e", e, idx))
                    self.waited[eng][("e", e)] = idx
            self.ins[eng].append({"fn": None, "waits": waits, "dma": None})

    def phase(self):
        self.barrier()
        g = len(self.esems)
        self.esems.append({e: self.nc.alloc_semaphore("prg%d_%s" % (g, e)) for e in ENGS})
        for e in ENGS:
            self.gen_start[e].append(len(self.ins[e]))
        self.pdi = 0

    def finish(self, eng="sync"):
        waits = []
        for key, (h, tot) in self.dma_sem.items():
            if self.waited[eng].get(("d", key), -1) < tot:
                waits.append(("d", key, tot))
        for e in ENGS:
            if e != eng and self.ins[e]:
                for idx in range(len(self.ins[e]) - 1, -1, -1):
                    if self.ins[e][idx]["dma"] is None and self.ins[e][idx]["fn"] is not None:
                        waits.append(("e", e, idx))
                        break
        self.ins[eng].append({"fn": None, "waits": waits, "dma": None})

    def emit(self):
        nc = self.nc
        needed = {e: set() for e in ENGS}
        for e in ENGS:
            for ins in self.ins[e]:
                for d in ins["waits"]:
                    if d[0] == "e":
                        needed[d[1]].add(d[2])
        import bisect
        val = {}
        gen_of = {}
        for e in ENGS:
            run = 0
            g = 0
            starts = self.gen_start[e]
            for idx in range(len(self.ins[e])):
                while g + 1 < len(starts) and idx >= starts[g + 1]:
                    g += 1
                    run = 0
                if idx in needed[e]:
                    run += 1
                    val[(e, idx)] = run
                    gen_of[(e, idx)] = g
        stats = {e: (len(self.ins[e]), len(needed[e])) for e in ENGS}
        ecount = {}
        dval = {k: 0 for k in self.dma_sem}
        hmap = {id(v[0]): k for k, v in self.dma_sem.items()}
        ptr = {e: 0 for e in ENGS}
        progress = True
        while progress:
            progress = False
            for e in ENGS:
                while ptr[e] < len(self.ins[e]):
                    ins = self.ins[e][ptr[e]]
                    ok = True
                    for d in ins["waits"]:
                        if d[0] == "e":
                            if ecount.get((d[1], gen_of[(d[1], d[2])]), 0) < val[(d[1], d[2])]:
                                ok = False
                                break
                        else:
                            if dval[d[1]] < d[2]:
                                ok = False
                                break
                    if not ok:
                        break
                    if ins["dma"] is not None:
                        dval[hmap[id(ins["dma"])]] += 16
                    elif ins["fn"] is not None and ptr[e] in needed[e]:
                        kk = (e, gen_of[(e, ptr[e])])
                        ecount[kk] = ecount.get(kk, 0) + 1
                    ptr[e] += 1
                    progress = True
        for e in ENGS:
            if ptr[e] < len(self.ins[e]):
                raise RuntimeError("DEADLOCK in dry run: engine %s stuck at %d/%d waits=%s" % (
                    e, ptr[e], len(self.ins[e]), self.ins[e][ptr[e]]["waits"]))

        def mk(e):
            def body(engobj):
                for idx, ins in enumerate(self.ins[e]):
                    for d in ins["waits"]:
                        if d[0] == "e":
                            engobj.wait_ge(self.esems[gen_of[(d[1], d[2])]][d[1]], val[(d[1], d[2])])
                        else:
                            engobj.wait_ge(self.dma_sem[d[1]][0], d[2])
                    if ins["fn"] is None:
                        continue
                    r = ins["fn"](engobj)
                    if ins["dma"] is not None:
                        r.then_inc(ins["dma"], 16)
                    elif idx in needed[e]:
                        r.then_inc(self.esems[gen_of[(e, idx)]][e], 1)
            return body

        with nc.Block() as block:
            block.tensor(mk("tensor"))
            block.vector(mk("vector"))
            block.scalar(mk("scalar"))
            block.gpsimd(mk("gpsimd"))
            block.sync(mk("sync"))
        return stats


import math
import numpy as np
import concourse.bass as bass
import concourse.mybir as mybir

D = 1024
DFF = 4096
EPS = 1e-5
BS = 512
PI = math.pi
NCOL = 1544


class Ctx:
    def __init__(self, nc, pg):
        self.nc = nc
        self.pg = pg
        self.words = 53000
        self.arena = nc.alloc_sbuf_tensor("arena", [128, self.words], F32)
        self.off = 0
        self.ps = [nc.alloc_psum_tensor("ps%d" % i, [128, 512], F32) for i in range(8)]
        self.psb = self.ps[4][:, :].bitcast(BF16)

    def reset(self):
        self.off = 0

    def sb(self, name, shape, dt=F32):
        n = 1
        for d_ in shape[1:]:
            n *= d_
        nw = n if dt == F32 else (n + 1) // 2
        nw = (nw + 7) // 8 * 8
        assert self.off + nw <= self.words, ("arena overflow", name, self.off, nw)
        base = self.arena[0:shape[0], self.off:self.off + nw]
        self.off += nw
        if dt != F32:
            base = base.bitcast(dt)
        ap = base[:, 0:n]
        if len(shape) == 3:
            ap = ap.rearrange("p (a b) -> p a b", a=shape[1])
        elif len(shape) == 4:
            ap = ap.rearrange("p (a b c) -> p a b c", a=shape[1], b=shape[2])
        return ap


def blocks_of(nt_total, pre=16, bs=512):
    bl = [(0, pre)] if pre else []
    t = pre
    while t < nt_total:
        n = min(bs, nt_total - t)
        bl.append((t, n))
        t += n
    return bl

def rms_rstd(pg, x_ap, sq_ap_fn, ones_bf, ps_ap, rstd_ap, D_, rkeys, sqkey, pskey, rstdkey, kt, eps_ap):
    for k in range(kt):
        pg.act(sq_ap_fn(k), x_ap[:, k, :], AF.Square, rkeys, [sqkey])
    for k in range(kt):
        pg.mm(ps_ap, ones_bf, sq_ap_fn(k), k == 0, k == kt - 1, [sqkey, "ones"], [pskey])
    pg.act(rstd_ap, ps_ap, AF.Sqrt, [pskey, "epsc"], [rstdkey], scale=1.0 / D_, bias=eps_ap)
    pg.op("vector", "reciprocal", [rstdkey], [rstdkey], out=rstd_ap, in_=rstd_ap)


def gelu_tanh(pg, eng, out_ap, x_ap, t1, t2, rk, wk, tk):
    pg.op(eng, "tensor_tensor", rk, [tk + "1"], out=t1, in0=x_ap, in1=x_ap, op=ALU.mult)
    pg.op(eng, "tensor_scalar", [tk + "1"], [tk + "1"], out=t1, in0=t1, scalar1=0.044715, scalar2=1.0,
          op0=ALU.mult, op1=ALU.add)
    pg.op(eng, "tensor_tensor", rk + [tk + "1"], [tk + "2"], out=t2, in0=t1, in1=x_ap, op=ALU.mult)
    pg.act(t1, t2, AF.Sigmoid, [tk + "2"], [tk + "1"], scale=1.5957691216057308)
    pg.op(eng, "tensor_tensor", rk + [tk + "1"], wk, out=out_ap, in0=x_ap, in1=t1, op=ALU.mult)


def sincos_tables(pg, nc, arg_ap, sin_out, cos_out, tmp_ap, key, negpi, tmp2_ap=None):
    MAGIC = 12582912.0
    SC = 2 * PI * (1.0 - 1e-6)
    if tmp2_ap is None:
        tmp2_ap = cos_out
    for off, outp, nm in ((0.0, sin_out, "sin"), (0.25, cos_out, "cos")):
        pg.op("vector", "tensor_scalar", [key + "arg", key + "sin"], [key + "tmp"], out=tmp_ap, in0=arg_ap, scalar1=1.0 / (2 * PI),
              scalar2=off, op0=ALU.mult, op1=ALU.add)
        pg.op("vector", "tensor_scalar", [key + "tmp"], [key + nm], out=outp, in0=tmp_ap, scalar1=MAGIC, scalar2=None, op0=ALU.add)
        pg.op("vector", "tensor_scalar", [key + nm], [key + nm], out=outp, in0=outp, scalar1=-MAGIC, scalar2=None, op0=ALU.add)
        pg.op("vector", "tensor_tensor", [key + "tmp", key + nm], [key + "tmp"], out=tmp_ap, in0=tmp_ap, in1=outp, op=ALU.subtract)
        pg.act(outp, tmp_ap, AF.Sin, [key + "tmp"], [key + nm], scale=SC)


def lam_parts(pg, nc, are, aim, ldt, shape, key, negpi, pool=None, sbf=None):
    al = lambda n: sbf(key + n, shape, F32)
    names = ["dt", "x", "mag", "th", "sn", "cs", "tmp", "ar1", "ai", "den", "fr", "fi", "t2"]
    if pool is None:
        tiles = [al(n) for n in names]
    else:
        tiles = pool[:len(names)]
    dt, x, mag, th, sn, cs_, tmp, ar1, ai, den, fr, fi, t2 = tiles
    sl = tuple(slice(None) for _ in shape)
    V = "vector"
    pg.act(dt[sl], ldt[sl], AF.Exp, [key + "in"], [key + "dt"])
    pg.op(V, "tensor_tensor", [key + "in", key + "dt"], [key + "x"], out=x[sl], in0=are[sl], in1=dt[sl], op=ALU.mult)
    pg.op(V, "tensor_scalar", [key + "x"], [key + "mag"], out=mag[sl], in0=x[sl], scalar1=1.0 / 5, scalar2=1.0, op0=ALU.mult, op1=ALU.add)
    for dv in (4.0, 3.0, 2.0, 1.0):
        pg.op(V, "tensor_tensor", [key + "x", key + "mag"], [key + "mag"], out=mag[sl], in0=mag[sl], in1=x[sl], op=ALU.mult)
        pg.op(V, "tensor_scalar", [key + "mag"], [key + "mag"], out=mag[sl], in0=mag[sl], scalar1=1.0 / dv, scalar2=1.0, op0=ALU.mult, op1=ALU.add)
    pg.op(V, "tensor_tensor", [key + "in", key + "dt"], [key + "arg"], out=th[sl], in0=aim[sl], in1=dt[sl], op=ALU.mult)
    sincos_tables(pg, nc, th[sl], sn[sl], cs_[sl], tmp[sl], key, negpi)
    pg.op(V, "tensor_tensor", [key + "mag", key + "cos"], [key + "ar1"], out=ar1[sl], in0=mag[sl], in1=cs_[sl], op=ALU.mult)
    pg.op(V, "tensor_scalar", [key + "ar1"], [key + "ar1"], out=ar1[sl], in0=ar1[sl], scalar1=-1.0, scalar2=None, op0=ALU.add)
    pg.op(V, "tensor_tensor", [key + "mag", key + "sin"], [key + "ai"], out=ai[sl], in0=mag[sl], in1=sn[sl], op=ALU.mult)
    pg.op(V, "tensor_tensor", [key + "in"], [key + "den"], out=den[sl], in0=are[sl], in1=are[sl], op=ALU.mult)
    pg.op(V, "tensor_tensor", [key + "in", key + "tmp"], [key + "tmp"], out=tmp[sl], in0=aim[sl], in1=aim[sl], op=ALU.mult)
    pg.op(V, "tensor_tensor", [key + "den", key + "tmp"], [key + "den"], out=den[sl], in0=den[sl], in1=tmp[sl], op=ALU.add)
    pg.op(V, "reciprocal", [key + "den"], [key + "den"], out=den[sl], in_=den[sl])
    pg.op(V, "tensor_tensor", [key + "ar1", key + "in"], [key + "fr"], out=fr[sl], in0=ar1[sl], in1=are[sl], op=ALU.mult)
    pg.op(V, "tensor_tensor", [key + "ai", key + "in", key + "tmp"], [key + "tmp"], out=tmp[sl], in0=ai[sl], in1=aim[sl], op=ALU.mult)
    pg.op(V, "tensor_tensor", [key + "fr", key + "tmp"], [key + "fr"], out=fr[sl], in0=fr[sl], in1=tmp[sl], op=ALU.add)
    pg.op(V, "tensor_tensor", [key + "fr", key + "den"], [key + "fr"], out=fr[sl], in0=fr[sl], in1=den[sl], op=ALU.mult)
    pg.op(V, "tensor_tensor", [key + "ai", key + "in"], [key + "fi"], out=fi[sl], in0=ai[sl], in1=are[sl], op=ALU.mult)
    pg.op(V, "tensor_tensor", [key + "ar1", key + "in"], [key + "t2"], out=t2[sl], in0=ar1[sl], in1=aim[sl], op=ALU.mult)
    pg.op(V, "tensor_tensor", [key + "fi", key + "t2"], [key + "fi"], out=fi[sl], in0=fi[sl], in1=t2[sl], op=ALU.subtract)
    pg.op(V, "tensor_tensor", [key + "fi", key + "den"], [key + "fi"], out=fi[sl], in0=fi[sl], in1=den[sl], op=ALU.mult)
    return mag, th, fr, fi


def phase_mlp(ctx, h_in, parts, w_up, w_dn, nw, nf, h_out, NT, final_norm, BS):
    nc, pg, sb = ctx.nc, ctx.pg, ctx.sb
    nparts = len(parts)
    blocks = None
    hin_v = h_in.rearrange("(k p) t -> p k t", p=128)
    hout_v = h_out.rearrange("(k p) t -> p k t", p=128)
    parts_v = [p_.rearrange("(k p) t -> p k t", p=128) for p_ in parts]
    wup = sb("wup", [128, 8, DFF], BF16)
    wdn = sb("wdn", [128, 32, D], BF16)
    nw_t = sb("nw_t", [128, 8], F32)
    nf_t = sb("nf_t", [128, 8], F32)
    ones = sb("ones", [128, 128], BF16)
    epsc = sb("epsc", [128, 1], F32)
    xb = [sb("xb%d" % i, [128, 8, BS], F32) for i in range(2)]
    hn = sb("hn", [128, 8, BS], BF16)
    hid = sb("hid", [128, 32, BS], BF16)
    rstd = sb("rstd", [128, BS], F32)
    tmp = [sb("tmp%d" % i, [128, BS], BF16) for i in range(4)]
    ps = ctx.ps

    pg.op("vector", "memset", [], ["ones"], ones[:, :], 1.0)
    pg.op("vector", "memset", [], ["epsc"], epsc[:, :], EPS)
    pg.dma("sync", nw_t[:, :], nw, writes=["nw"])
    pg.dma("sync", nf_t[:, :], nf, writes=["nf"])
    wup_v = w_up.rearrange("(k p) n -> p k n", p=128)
    wdn_v = w_dn.rearrange("(m p) d -> p m d", p=128)
    for k in range(8):
        pg.dma("gpsimd", wup[:, k, :], wup_v[:, k, :], writes=[("wup", k)])
    for j in range(8):
        pg.dma("gpsimd", wdn[:, 4 * j:4 * j + 4, :], wdn_v[:, 4 * j:4 * j + 4, :], writes=[("wdn", j)])

    if blocks is None:
        blocks = blocks_of(NT, pre=0, bs=BS)
    psi = 0
    def issue_loads(bi):
        t0_, nt_ = blocks[bi]
        s_ = bi % 2
        pg.dma("sync", xb[s_][:, :, 0:nt_], hin_v[:, :, t0_:t0_ + nt_], writes=[("xb", s_)], sem=("xsem", s_))
        for i in range(nparts):
            pg.dma("gpsimd", xb[s_][:, :, 0:nt_], parts_v[i][:, :, t0_:t0_ + nt_], writes=[("xb", s_)], sem=("xsem", s_),
                   accum_op=ALU.add)

    issue_loads(0)
    for bi, (t0, nt) in enumerate(blocks):
        s = bi % 2
        X = xb[s]
        xk = ("xb", s)
        if bi + 1 < len(blocks):
            issue_loads(bi + 1)
        rms_rstd(pg, X[:, :, 0:nt], lambda k: hn[:, k, 0:nt], ones[:, :], ps[7][:, 0:nt], rstd[:, 0:nt], D,
                 [xk], "hn", ("ps", 7), "rstd", 8, epsc[:, 0:1])
        for k in range(8):
            pg.op("vector", "scalar_tensor_tensor", [xk, "rstd", "nw"], ["hn"],
                  out=hn[:, k, 0:nt], in0=X[:, k, 0:nt], scalar=nw_t[:, k:k + 1], in1=rstd[:, 0:nt],
                  op0=ALU.mult, op1=ALU.mult)
        for m in range(32):
            p = psi % 6
            psi += 1
            for k in range(8):
                pg.mm(ps[p][:, 0:nt], wup[:, k, m * 128:(m + 1) * 128], hn[:, k, 0:nt], k == 0, k == 7,
                      ["hn", ("wup", k)], [("ps", p)])
            tq = m % 4
            pg.act(tmp[tq][:, 0:nt], ps[p][:, 0:nt], AF.Relu, [("ps", p)], [("tmp", tq)])
            pg.op("vector", "tensor_tensor", [("tmp", tq)], [("hid", m)],
                  out=hid[:, m, 0:nt], in0=tmp[tq][:, 0:nt], in1=tmp[tq][:, 0:nt], op=ALU.mult)
        for d in range(8):
            p = psi % 6
            psi += 1
            for m in range(32):
                pg.mm(ps[p][:, 0:nt], wdn[:, m, d * 128:(d + 1) * 128], hid[:, m, 0:nt], m == 0, m == 31,
                      [("hid", m), ("wdn", m // 4)], [("ps", p)])
            pg.op("vector", "tensor_tensor", [("ps", p), xk], [xk],
                  out=X[:, d, 0:nt], in0=X[:, d, 0:nt], in1=ps[p][:, 0:nt], op=ALU.add)
        if final_norm:
            rms_rstd(pg, X[:, :, 0:nt], lambda k: hn[:, k, 0:nt], ones[:, :], ps[7][:, 0:nt], rstd[:, 0:nt], D,
                     [xk], "hn", ("ps", 7), "rstd", 8, epsc[:, 0:1])
            for k in range(8):
                pg.op("vector", "scalar_tensor_tensor", [xk, "rstd", "nf"], [xk],
                      out=X[:, k, 0:nt], in0=X[:, k, 0:nt], scalar=nf_t[:, k:k + 1], in1=rstd[:, 0:nt],
                      op0=ALU.mult, op1=ALU.mult)
        pg.dma("sync", hout_v[:, :, t0:t0 + nt], X[:, :, 0:nt], reads=[xk], sem=("osem", s))


def phase_mix0(ctx, x_in, P_, part, T, blocks=None, dbg=9):
    nc, pg, sb = ctx.nc, ctx.pg, ctx.sb
    win_d, nw_d, vecs_d, wa_d, wx_d, wglu_d, wout_d, scol_d, srow_d, bT_d, cT_d, ident_d, kidx_d = [
        P_[k] for k in ["win", "nw", "vecs", "wa", "wx", "wglu", "wout", "scol", "srow", "bT", "cT", "ident", "kidx"]]
    xin_v = x_in.rearrange("(k p) t -> p k t", p=128)
    part_v = part.rearrange("(k p) t -> p k t", p=128)
    win = sb("win_s", [128, 8, 384], BF16)
    wout = sb("wout_s", [128, 2, D], BF16)
    nw_t = sb("nw_s", [128, 8]); vecs = sb("vecs_s", [128, 16])
    wa32 = sb("wa32", [128, 128]); wx32 = sb("wx32", [128, 128]); wglu32 = sb("wglu32", [128, 128])
    wa = sb("wa_s", [128, 128], BF16); wx = sb("wx_s", [128, 128], BF16); wglu = sb("wglu_s", [128, 128], BF16)
    scol = sb("scol_s", [128, 3, 4]); srow = sb("srow_s", [128, 3, 512])
    bT = sb("bT_s", [128, 2, 512]); cT = sb("cT_s", [128, 2, 512])
    ident = sb("ident_s", [128, 128]); kidx = sb("kidx_s", [128, BS])
    ones = sb("ones", [128, 128], BF16); epsc = sb("epsc", [128, 1]); negpi = sb("negpi", [128, 1]); onec = sb("onec", [128, 1])
    V = "vector"
    pg.op(V, "memset", [], ["ones"], ones[:, :], 1.0)
    pg.op(V, "memset", [], ["epsc"], epsc[:, :], EPS)
    pg.op(V, "memset", [], ["negpi"], negpi[:, :], -PI)
    pg.op(V, "memset", [], ["onec"], onec[:, :], 1.0)
    for t_, d_, k_ in [(nw_t, nw_d, "nw"), (vecs, vecs_d, "vecs"), (wa32, wa_d, "wa32"), (wx32, wx_d, "wx32"),
                       (wglu32, wglu_d, "wglu32"), (scol, scol_d, "colin"), (srow, srow_d, "rowin"),
                       (bT, bT_d, "bT"), (cT, cT_d, "cT"), (ident, ident_d, "ident"), (kidx, kidx_d, "kidx")]:
        sl = tuple(slice(None) for _ in t_.shape)
        pg.dma("sync", t_[sl], d_, writes=[k_])
    pg.dma("gpsimd", win[:, :, :], win_d.rearrange("(k p) n -> p k n", p=128), writes=["win"])
    pg.dma("gpsimd", wout[:, :, :], wout_d.rearrange("(k p) n -> p k n", p=128), writes=["wout"])
    pg.op(V, "tensor_copy", ["wa32"], ["wa"], out=wa[:, :], in_=wa32[:, :])
    pg.op(V, "tensor_copy", ["wx32"], ["wx"], out=wx[:, :], in_=wx32[:, :])
    pg.op(V, "tensor_copy", ["wglu32"], ["wglu"], out=wglu[:, :], in_=wglu32[:, :])
    cdiag = sb("cdiag", [128, 4, 128], BF16)
    for k in range(4):
        pg.op(V, "tensor_scalar", ["ident", "vecs"], ["cdiag"], out=cdiag[:, k, :], in0=ident[:, :],
              scalar1=vecs[:, k:k + 1], scalar2=None, op0=ALU.mult)
    lc = sb("lc", [128, 1])
    pg.act(lc[:, :], vecs[:, 7:8], AF.Exp, ["vecs"], ["lc"], scale=-1.0)
    pg.act(lc[:, :], lc[:, :], AF.Ln, ["lc", "onec"], ["lc"], bias=onec[:, 0:1])
    pg.op(V, "tensor_scalar", ["lc"], ["lc"], out=lc[:, :], in0=lc[:, :], scalar1=-8.0, scalar2=None, op0=ALU.mult)
    NB = 32
    bufs = [sb("w%d" % i, [128, BS]) for i in range(NB)]
    c_are = sb("c_are", [128, 4]); c_aim = sb("c_aim", [128, 4]); c_ldt = sb("c_ldt", [128, 4])
    for i, t_ in enumerate([c_are, c_aim, c_ldt]):
        pg.op(V, "tensor_copy", ["colin"], ["colin2"], out=t_[:, :], in_=scol[:, i, :])
    pg.op(V, "tensor_copy", ["colin2"], ["colin"], out=c_ldt[:, :], in_=c_ldt[:, :])
    magc, thc, _, _ = lam_parts(pg, nc, c_are, c_aim, c_ldt, [128, 4], "col", negpi[:, 0:1], sbf=sb)
    r_are, r_aim, r_ldt = bufs[13], bufs[14], bufs[15]
    for i, t_ in enumerate([r_are, r_aim, r_ldt]):
        pg.op(V, "tensor_copy", ["rowin"], ["rowin2"], out=t_[:, :], in_=srow[:, i, :])
    pg.op(V, "tensor_copy", ["rowin2"], ["rowin"], out=r_ldt[:, :], in_=r_ldt[:, :])
    _, _, frr, fir = lam_parts(pg, nc, r_are, r_aim, r_ldt, [128, 512], "row", negpi[:, 0:1], pool=bufs)
    bbT = sb("bbT", [128, 2, 512], BF16)
    tA, tB = bufs[16], bufs[17]
    pg.op(V, "tensor_tensor", ["bT", "rowfr"], ["tA"], out=tA[:, :], in0=bT[:, 0, :], in1=frr[:, :], op=ALU.mult)
    pg.op(V, "tensor_tensor", ["bT", "rowfi"], ["tB"], out=tB[:, :], in0=bT[:, 1, :], in1=fir[:, :], op=ALU.mult)
    pg.op(V, "tensor_tensor", ["tA", "tB"], ["bbT"], out=bbT[:, 0, :], in0=tA[:, :], in1=tB[:, :], op=ALU.subtract)
    pg.op(V, "tensor_tensor", ["bT", "rowfi", "tA"], ["tA"], out=tA[:, :], in0=bT[:, 0, :], in1=fir[:, :], op=ALU.mult)
    pg.op(V, "tensor_tensor", ["bT", "rowfr", "tB"], ["tB"], out=tB[:, :], in0=bT[:, 1, :], in1=frr[:, :], op=ALU.mult)
    pg.op(V, "tensor_tensor", ["tA", "tB"], ["bbT"], out=bbT[:, 1, :], in0=tA[:, :], in1=tB[:, :], op=ALU.add)
    ccT = sb("ccT", [128, 2, 512], BF16)
    pg.op(V, "tensor_copy", ["cT"], ["ccT"], out=ccT[:, 0, :], in_=cT[:, 0, :])
    pg.op(V, "tensor_scalar", ["cT", "ccT"], ["ccT"], out=ccT[:, 1, :], in0=cT[:, 1, :], scalar1=-1.0, scalar2=None, op0=ALU.mult)
    cosT = sb("cosT", [128, 4, BS]); sinT = sb("sinT", [128, 4, BS]); decT = sb("decT", [128, 4, BS])
    for j in range(4):
        key = "tb%d" % j
        ta = bufs[18 + 2 * j]
        pg.op(V, "tensor_scalar", ["kidx", "colarg"], [key + "arg"], out=ta[:, :], in0=kidx[:, :],
              scalar1=thc[:, j:j + 1], scalar2=None, op0=ALU.mult)
        tt = bufs[19 + 2 * j]
        sincos_tables(pg, nc, ta[:, :], sinT[:, j, :], cosT[:, j, :], tt[:, :], key, negpi[:, 0:1])
        pg.op(V, "tensor_scalar", ["kidx", "colmag"], ["decT"], out=decT[:, j, :], in0=kidx[:, :], scalar1=0.0,
              scalar2=magc[:, j:j + 1], op0=ALU.mult, op1=ALU.add)
    tabk = ["decT"] + ["tb%dsin" % j for j in range(4)] + ["tb%dcos" % j for j in range(4)]

    xb = [sb("xb%d" % i, [128, 8, BS]) for i in range(2)]
    hn = sb("hn", [128, 8, BS], BF16)
    rstd = sb("rstd", [128, BS])
    xlp = sb("xlp", [128, 4 + BS], BF16)
    bb16 = [sb("h%d" % i, [128, BS], BF16) for i in range(12)]
    Y = sb("Y", [128, 2, BS], BF16)
    ob = [sb("ob0", [128, 8, BS])] * 2
    hst = sb("hst", [128, 2])
    sst = sb("sst", [128, 4, 2, 2])
    stmp = sb("stmp", [128, 4, 4])
    ps = ctx.ps
    pg.op(V, "memset", [], ["xlp"], xlp[:, :], 0.0)
    pg.op(V, "memset", [], ["hst"], hst[:, :], 0.0)
    pg.op(V, "memset", [], ["sst"], sst[:, :, :, :], 0.0)
    pg.op(V, "memset", [], ["Y0"], Y[:, 0, :], 0.0)
    pg.op(V, "memset", [], ["Y1"], Y[:, 1, :], 0.0)

    if blocks is None:
        blocks = blocks_of(T)
    pg.barrier()
    G = "gpsimd"
    for bi, (t0, nt) in enumerate(blocks):
        s = bi % 2
        X = xb[s]; xk = ("xb", s)
        W = lambda i: bufs[i][:, 0:nt]
        Hh = lambda i: bb16[i][:, 0:nt]
        wk = lambda i: ("w", i)
        hk = lambda i: ("h", i)
        P = lambda i: ps[i][:, 0:nt]
        pk = lambda i: ("ps", i)
        if bi == 0:
            pg.dma("sync", X[:, :, 0:nt], xin_v[:, :, t0:t0 + nt], writes=[xk], sem=("xsem", s))
        if bi + 1 < len(blocks):
            t1_, n1_ = blocks[bi + 1]
            pg.dma("sync", xb[(bi + 1) % 2][:, :, 0:n1_], xin_v[:, :, t1_:t1_ + n1_], writes=[("xb", (bi + 1) % 2)], sem=("xsem", (bi + 1) % 2))
        FL = "norm,hn,inproj,ev1,ev2,ev3,ev4".split(",")
        if "norm" in FL:
            rms_rstd(pg, X[:, :, 0:nt], lambda k: hn[:, k, 0:nt], ones[:, :], P(0), rstd[:, 0:nt], D,
                     [xk], "hn", pk(0), "rstd", 8, epsc[:, 0:1])
        if "hn" in FL:
            for k in range(8):
                pg.op(V, "scalar_tensor_tensor", [xk, "rstd", "nw"], ["hn"], out=hn[:, k, 0:nt], in0=X[:, k, 0:nt],
                      scalar=nw_t[:, k:k + 1], in1=rstd[:, 0:nt], op0=ALU.mult, op1=ALU.mult)
        if "inproj" in FL:
            for m in range(3):
                for k in range(8):
                    pg.mm(P(1 + m), win[:, k, m * 128:(m + 1) * 128], hn[:, k, 0:nt], k == 0, k == 7, ["hn", "win"], [pk(1 + m)])
        if "ev1" in FL:
            pg.act(xlp[:, 4:4 + nt], P(1), AF.Identity, [pk(1)], ["xlp"])
        if "ev2" in FL:
            pg.act(W(0), P(2), AF.Identity, [pk(2)], [wk(0)])
        if "ev3" in FL:
            pg.act(W(1), P(3), AF.Identity, [pk(3)], [wk(1)])
        if "ev4" in FL:
            pg.op(V, "tensor_copy", [pk(3)], [hk(0)], out=Hh(0), in_=P(3))
        if dbg >= 2:
            for k in range(4):
                pg.mm(P(4), cdiag[:, k, :], xlp[:, 1 + k:1 + k + nt], k == 0, k == 3, ["xlp", "cdiag"], [pk(4)])
            pg.act(W(2), P(4), AF.Identity, [pk(4), "vecs"], [wk(2)], bias=vecs[:, 4:5])
            pg.op(V, "tensor_copy", [wk(2)], [hk(1)], out=Hh(1), in_=W(2))
            pg.op(V, "tensor_copy", ["xlp"], ["xlp"], out=xlp[:, 0:4], in_=xlp[:, nt:nt + 4])
            pg.mm(P(5), wa[:, :], Hh(1), True, True, [hk(1), "wa"], [pk(5)])
            pg.mm(P(6), wx[:, :], Hh(1), True, True, [hk(1), "wx"], [pk(6)])
            pg.act(W(3), P(5), AF.Sigmoid, [pk(5), "vecs"], [wk(3)], bias=vecs[:, 5:6])
            pg.act(W(4), P(6), AF.Sigmoid, [pk(6), "vecs"], [wk(4)], bias=vecs[:, 6:7])
            pg.act(W(5), W(3), AF.Exp, [wk(3), "lc"], [wk(5)], scale=lc[:, 0:1])
            pg.op(G, "tensor_tensor", [wk(5)], [wk(6)], out=W(6), in0=W(5), in1=W(5), op=ALU.mult)
            pg.act(W(6), W(6), AF.Sqrt, [wk(6), "onec"], [wk(6)], scale=-1.0, bias=onec[:, 0:1])
            pg.op(G, "tensor_tensor", [wk(4), wk(2)], [wk(7)], out=W(7), in0=W(4), in1=W(2), op=ALU.mult)
            pg.op(G, "tensor_tensor", [wk(7), wk(6)], [wk(7)], out=W(7), in0=W(7), in1=W(6), op=ALU.mult)
            hin = hst[:, (bi % 2):(bi % 2) + 1]; hout = hst[:, ((bi + 1) % 2):((bi + 1) % 2) + 1]
            pg.op(V, "tensor_tensor_scan", [wk(5), wk(7), "hst"], [wk(8)], out=W(8), data0=W(5), data1=W(7),
                  initial=hin, op0=ALU.mult, op1=ALU.add)
            pg.op(V, "tensor_copy", [wk(8)], ["hst"], out=hout, in_=bufs[8][:, nt - 1:nt])
            gelu_tanh(pg, G, W(9), W(0), W(10), W(11), [wk(0)], [wk(9)], "ga")
            pg.op(V, "tensor_tensor", [wk(8), wk(9)], ["Y0"], out=Y[:, 0, 0:nt], in0=W(8), in1=W(9), op=ALU.mult)
        if dbg >= 3:
            def stageA(j):
                pa, pb = (4, 5) if j % 2 == 0 else (6, 7)
                pg.mm(P(pa), bbT[:, 0, j * 128:(j + 1) * 128], Hh(0), True, True, [hk(0), "bbT"], [pk(pa)])
                pg.mm(P(pb), bbT[:, 1, j * 128:(j + 1) * 128], Hh(0), True, True, [hk(0), "bbT"], [pk(pb)])
                o = 12 + (j % 2) * 8
                cs_ = cosT[:, j, 0:nt]; sn_ = sinT[:, j, 0:nt]
                pg.op(V, "tensor_tensor", [pk(pa)] + tabk, [wk(o)], out=W(o), in0=P(pa), in1=cs_, op=ALU.mult)
                pg.op(V, "tensor_tensor", [pk(pb)] + tabk, [wk(o + 1)], out=W(o + 1), in0=P(pb), in1=sn_, op=ALU.mult)
                pg.op(G, "tensor_tensor", [wk(o), wk(o + 1)], [wk(o)], out=W(o), in0=W(o), in1=W(o + 1), op=ALU.add)
                pg.op(V, "tensor_tensor", [pk(pb)] + tabk, [wk(o + 2)], out=W(o + 2), in0=P(pb), in1=cs_, op=ALU.mult)
                pg.op(V, "tensor_tensor", [pk(pa)] + tabk, [wk(o + 3)], out=W(o + 3), in0=P(pa), in1=sn_, op=ALU.mult)
                pg.op(G, "tensor_tensor", [wk(o + 2), wk(o + 3)], [wk(o + 2)], out=W(o + 2), in0=W(o + 2), in1=W(o + 3), op=ALU.subtract)
            def stageB(j):
                pa, pb = (4, 5) if j % 2 == 0 else (6, 7)
                o = 12 + (j % 2) * 8
                cs_ = cosT[:, j, 0:nt]; sn_ = sinT[:, j, 0:nt]
                pi_, po_ = bi % 2, (bi + 1) % 2
                pg.op(V, "tensor_tensor_scan", [wk(o), "sst"] + tabk, [wk(o + 4)], out=W(o + 4), data0=decT[:, j, 0:nt], data1=W(o),
                      initial=sst[:, j, 0, pi_:pi_ + 1], op0=ALU.mult, op1=ALU.add)
                pg.op(V, "tensor_tensor_scan", [wk(o + 2), "sst"] + tabk, [wk(o + 5)], out=W(o + 5), data0=decT[:, j, 0:nt], data1=W(o + 2),
                      initial=sst[:, j, 1, pi_:pi_ + 1], op0=ALU.mult, op1=ALU.add)
                cN = cosT[:, j, nt - 1:nt]; sN = sinT[:, j, nt - 1:nt]
                lre = bufs[o + 4][:, nt - 1:nt]; lim = bufs[o + 5][:, nt - 1:nt]
                pg.op(V, "tensor_tensor", [wk(o + 5)] + tabk, ["stmp"], out=stmp[:, j, 0:1], in0=lim, in1=sN, op=ALU.mult)
                pg.op(V, "scalar_tensor_tensor", [wk(o + 4), "stmp", "sst"] + tabk, ["sst"], out=sst[:, j, 0, po_:po_ + 1], in0=lre, scalar=cN,
                      in1=stmp[:, j, 0:1], op0=ALU.mult, op1=ALU.subtract)
                pg.op(V, "tensor_tensor", [wk(o + 4)] + tabk, ["stmp"], out=stmp[:, j, 1:2], in0=lre, in1=sN, op=ALU.mult)
                pg.op(V, "scalar_tensor_tensor", [wk(o + 5), "stmp", "sst"] + tabk, ["sst"], out=sst[:, j, 1, po_:po_ + 1], in0=lim, scalar=cN,
                      in1=stmp[:, j, 1:2], op0=ALU.mult, op1=ALU.add)
                pg.op(G, "tensor_tensor", [wk(o + 4)] + tabk, [wk(o + 6)], out=W(o + 6), in0=W(o + 4), in1=cs_, op=ALU.mult)
                pg.op(G, "tensor_tensor", [wk(o + 5)] + tabk, [wk(o + 7)], out=W(o + 7), in0=W(o + 5), in1=sn_, op=ALU.mult)
                pg.op(G, "tensor_tensor", [wk(o + 6), wk(o + 7)], [hk(2 + 2 * j)], out=Hh(2 + 2 * j), in0=W(o + 6), in1=W(o + 7), op=ALU.subtract)
                pg.op(G, "tensor_tensor", [wk(o + 4)] + tabk, [wk(o + 6)], out=W(o + 6), in0=W(o + 4), in1=sn_, op=ALU.mult)
                pg.op(G, "tensor_tensor", [wk(o + 5)] + tabk, [wk(o + 7)], out=W(o + 7), in0=W(o + 5), in1=cs_, op=ALU.mult)
                pg.op(G, "tensor_tensor", [wk(o + 6), wk(o + 7)], [hk(3 + 2 * j)], out=Hh(3 + 2 * j), in0=W(o + 6), in1=W(o + 7), op=ALU.add)
            stageA(0)
            for j in range(4):
                if j + 1 < 4:
                    stageA(j + 1)
                stageB(j)
            for j in range(4):
                pg.mm(P(1), ccT[:, 0, j * 128:(j + 1) * 128], Hh(2 + 2 * j), j == 0, False, [hk(2 + 2 * j), "ccT"], [pk(1)])
                pg.mm(P(1), ccT[:, 1, j * 128:(j + 1) * 128], Hh(3 + 2 * j), False, j == 3, [hk(3 + 2 * j), "ccT"], [pk(1)])
            pg.op(V, "scalar_tensor_tensor", [wk(1), "vecs", pk(1)], [wk(28)], out=W(28), in0=W(1), scalar=vecs[:, 8:9],
                  in1=P(1), op0=ALU.mult, op1=ALU.add)
            gelu_tanh(pg, V, W(29), W(28), W(30), W(31), [wk(28)], [wk(29)], "gb")
            pg.op(V, "tensor_copy", [wk(29)], [hk(10)], out=Hh(10), in_=W(29))
            pg.mm(P(2), wglu[:, :], Hh(10), True, True, [hk(10), "wglu"], [pk(2)])
            pg.act(W(30), P(2), AF.Sigmoid, [pk(2), "vecs"], [wk(30)], bias=vecs[:, 9:10])
            pg.op(V, "tensor_tensor", [wk(29), wk(30)], ["Y1"], out=Y[:, 1, 0:nt], in0=W(29), in1=W(30), op=ALU.mult)
        O = ob[0]; okk = ("ob", 0)
        for d in range(8):
            p = 3 if d % 2 == 0 else 0
            pg.mm(P(p), wout[:, 0, d * 128:(d + 1) * 128], Y[:, 0, 0:nt], True, False, ["Y0", "wout"], [pk(p)])
            pg.mm(P(p), wout[:, 1, d * 128:(d + 1) * 128], Y[:, 1, 0:nt], False, True, ["Y1", "wout"], [pk(p)])
            if d % 2 == 0:
                pg.act(O[:, d, 0:nt], P(p), AF.Identity, [pk(p)], [okk])
            else:
                pg.op(V, "tensor_copy", [pk(p)], [okk], out=O[:, d, 0:nt], in_=P(p))
        pg.dma("sync", part_v[:, :, t0:t0 + nt], O[:, :, 0:nt], reads=[okk], sem=("osem", 0))


def phase_mix1(ctx, x_in, P_, part, T, blocks=None):
    nc, pg, sb = ctx.nc, ctx.pg, ctx.sb
    win_d, nw_d, cv_d, hv_d, cvec_d, wout_d, ident_d, mneg_d, seg_d, sel_d = [
        P_[k] for k in ["win", "nw", "cv", "hv", "cvec", "wout", "ident", "mneg", "seg", "sel"]]
    xin_v = x_in.rearrange("(k p) t -> p k t", p=128)
    part_v = part.rearrange("(k p) t -> p k t", p=128)
    V, G = "vector", "gpsimd"
    win = sb("win_s", [128, 8, NCOL], BF16)
    wout = sb("wout_s", [128, 4, D], BF16)
    nw_t = sb("nw_s", [128, 8]); cv = sb("cv_s", [128, 8, 5]); hv = sb("hv_s", [8, 2]); cvec = sb("cvec_s", [128, 8])
    ident = sb("ident_s", [128, 128]); identb = sb("identb", [128, 128], BF16)
    mneg = sb("mneg_s", [128, BS]); mnegb = sb("mnegb", [128, BS], BF16)
    seg = sb("seg_s", [8, BS]); sel = sb("sel_s", [8, 8 * 128])
    ones = sb("ones", [128, 128], BF16); ones32 = sb("ones32", [8, 128])
    epsc = sb("epsc", [128, 1]); onec = sb("onec", [128, 1])
    pg.op(V, "memset", [], ["ones"], ones[:, :], 1.0)
    pg.op(V, "memset", [], ["ones32"], ones32[:, :], 1.0)
    pg.op(V, "memset", [], ["epsc"], epsc[:, :], EPS)
    pg.op(V, "memset", [], ["onec"], onec[:, :], 1.0)
    for t_, d_, k_ in [(nw_t, nw_d, "nw"), (cv, cv_d, "cv"), (hv, hv_d, "hv"), (cvec, cvec_d, "cvec"),
                       (ident, ident_d, "ident"), (mneg, mneg_d, "mneg"), (seg, seg_d, "seg"), (sel, sel_d, "sel")]:
        sl = tuple(slice(None) for _ in t_.shape)
        pg.dma("sync", t_[sl], d_, writes=[k_])
    win_v = win_d.rearrange("(k p) n -> p k n", p=128)
    for k in range(8):
        pg.dma("gpsimd", win[:, k, :], win_v[:, k, :], writes=[("win", k)])
    pg.dma("gpsimd", wout[:, :, :], wout_d.rearrange("(k p) n -> p k n", p=128), writes=["wout"])
    pg.op(V, "tensor_copy", ["ident"], ["identb"], out=identb[:, :], in_=ident[:, :])
    pg.op(V, "tensor_copy", ["mneg"], ["mnegb"], out=mnegb[:, :], in_=mneg[:, :])
    cdiag = sb("cdiag", [128, 8, 4, 128], BF16)
    for m in range(8):
        for k in range(4):
            pg.op(V, "tensor_scalar", ["ident", "cv"], ["cdiag"], out=cdiag[:, m, k, :], in0=ident[:, :],
                  scalar1=cv[:, m, k:k + 1], scalar2=None, op0=ALU.mult)
    ah = sb("ah", [8, 1])
    pg.act(ah[:, :], hv[:, 1:2], AF.Exp, ["hv"], ["ah"])
    pg.op(V, "tensor_scalar", ["ah"], ["ah"], out=ah[:, :], in0=ah[:, :], scalar1=-1.0, scalar2=None, op0=ALU.mult)

    xb = [sb("xb%d" % i, [128, 8, BS]) for i in range(2)]
    hn = sb("hn", [128, 8, BS], BF16)
    rstd = sb("rstd", [128, BS])
    zs = sb("zs", [128, 4, BS])
    xbc = sb("xbc", [128, 8, 4 + BS], BF16)
    xc = sb("xc", [128, 8, BS], BF16)
    dtf = [sb("dtf%d" % i, [8, BS]) for i in range(6)]
    tm2 = [sb("tm%d" % i, [128, 32]) for i in range(2)]
    diag82 = [sb("diag8%d" % i, [8, 8]) for i in range(2)]
    decb2 = [sb("decb%d" % i, [128, 8]) for i in range(2)]
    Hs = sb("Hs", [128, 8, 64]); Hbf = sb("Hbf", [128, 512], BF16)
    E = [sb("E%d" % i, [128, 128], BF16) for i in range(8)]
    Gh = [sb("Gh%d" % i, [128, 128], BF16) for i in range(8)]
    xdt2 = [sb("xdt%d" % i, [128, 8, 64], BF16) for i in range(2)]; xw2 = [sb("xw%d" % i, [128, 8, 64], BF16) for i in range(2)]
    btm2 = [sb("btm%d" % i, [128, 256], BF16) for i in range(2)]
    tyo2 = [sb("tyo%d" % i, [128, 8, 64]) for i in range(2)]; ytm2 = [sb("ytm%d" % i, [128, 512], BF16) for i in range(2)]
    yfm = sb("yfm", [128, 4, BS])
    gg = sb("gg", [128, 4, BS]); sq = sb("sq", [128, 4, BS], BF16); rs2 = sb("rs2", [128, 2, BS])
    GN = sb("GN", [128, 4, BS], BF16)
    ob = sb("ob", [128, 8, BS])
    ps = ctx.ps
    psb = ctx.psb
    pg.op(V, "memset", [], ["xbc"], xbc[:, :, :], 0.0)
    pg.op(V, "memset", [], ["H"], Hs[:, :, :], 0.0)
    pg.op(V, "memset", [], ["Hbf"], Hbf[:, :], 0.0)
    pg.barrier()

    if blocks is None:
        blocks = blocks_of(T)
    pr = 0
    cpar = 0
    for bi, (t0, nt) in enumerate(blocks):
        s = bi % 2
        X = xb[s]; xk = ("xb", s)
        P = lambda i: ps[i][:, 0:nt]
        pk = lambda i: ("ps", i)
        if bi == 0:
            pg.dma("sync", X[:, :, 0:nt], xin_v[:, :, t0:t0 + nt], writes=[xk], sem=("xsem", s))
        if bi + 1 < len(blocks):
            t1_, n1_ = blocks[bi + 1]
            pg.dma("sync", xb[(bi + 1) % 2][:, :, 0:n1_], xin_v[:, :, t1_:t1_ + n1_], writes=[("xb", (bi + 1) % 2)], sem=("xsem", (bi + 1) % 2))
        rms_rstd(pg, X[:, :, 0:nt], lambda k: hn[:, k, 0:nt], ones[:, :], P(0), rstd[:, 0:nt], D,
                 [xk], "hn", pk(0), "rstd", 8, epsc[:, 0:1])
        for k in range(8):
            pg.op(V, "scalar_tensor_tensor", [xk, "rstd", "nw"], ["hn"], out=hn[:, k, 0:nt], in0=X[:, k, 0:nt],
                  scalar=nw_t[:, k:k + 1], in1=rstd[:, 0:nt], op0=ALU.mult, op1=ALU.mult)
        wk_all = [("win", k) for k in range(8)]
        for m in range(12):
            p = 1 + (pr % 2); pr += 1
            for k in range(8):
                pg.mm(P(p), win[:, k, m * 128:(m + 1) * 128], hn[:, k, 0:nt], k == 0, k == 7, ["hn"] + wk_all, [pk(p)])
            if m < 4:
                pg.act(zs[:, m, 0:nt], P(p), AF.Silu, [pk(p)], [("zs", m)])
            else:
                pg.act(xbc[:, m - 4, 4:4 + nt], P(p), AF.Identity, [pk(p)], [("xbc", m - 4)])
        p = 1 + (pr % 2); pr += 1
        for k in range(8):
            pg.mm(ps[p][0:8, 0:nt], win[:, k, 1536:1544], hn[:, k, 0:nt], k == 0, k == 7, ["hn"] + wk_all, [pk(p)])
        e1, dtA, cs, ncs, tend, fs = [t_[:, 0:nt] for t_ in dtf]
        pg.act(e1, ps[p][0:8, 0:nt], AF.Exp, [pk(p), "hv"], ["e1"], bias=hv[:, 0:1])
        pg.act(e1, e1, AF.Ln, ["e1", "onec"], ["e1"], bias=onec[0:8, 0:1])
        pg.op(V, "tensor_scalar", ["e1", "ah"], ["dtA"], out=dtA, in0=e1, scalar1=ah[:, 0:1], scalar2=None, op0=ALU.mult)
        pg.op(V, "tensor_tensor_scan", ["dtA", "seg"], ["cs"], out=cs, data0=seg[:, 0:nt], data1=dtA, initial=0.0,
              op0=ALU.mult, op1=ALU.add)
        pg.op(V, "tensor_scalar", ["cs"], ["ncs"], out=ncs, in0=cs, scalar1=-1.0, scalar2=None, op0=ALU.mult)
        chunks = [(c0, min(128, nt - c0)) for c0 in range(0, nt, 128)]
        for (c0, L) in chunks:
            pg.op(V, "tensor_scalar", ["cs"], ["tend"], out=dtf[4][:, c0:c0 + L], in0=dtf[2][:, c0:c0 + L], scalar1=-1.0,
                  scalar2=dtf[2][:, c0 + L - 1:c0 + L], op0=ALU.mult, op1=ALU.add)
        pg.act(tend, tend, AF.Exp, ["tend"], ["tend"])
        pg.act(fs, cs, AF.Exp, ["cs"], ["fs"])
        for m in range(8):
            p = 1 + (pr % 2); pr += 1
            for k in range(4):
                pg.mm(P(p), cdiag[:, m, k, :], xbc[:, m, 1 + k:1 + k + nt], k == 0, k == 3, [("xbc", m), "cdiag"], [pk(p)])
            pg.act(xc[:, m, 0:nt], P(p), AF.Silu, [pk(p), "cv"], [("xc", m)], bias=cv[:, m, 4:5])
            pg.op(V, "tensor_copy", [("xbc", m)], [("xbc", m)], out=xbc[:, m, 0:4], in_=xbc[:, m, nt:nt + 4])
        for (c0, L) in chunks:
            par = cpar % 2; cpar += 1
            tm, diag8, decb, xdt, xw, btm, tyo, ytm = tm2[par], diag82[par], decb2[par], xdt2[par], xw2[par], btm2[par], tyo2[par], ytm2[par]
            TM, DG, DC, XD, XW, BT, TY, YT = [(n_, par) for n_ in ["tm", "diag8", "decb", "xdt", "xw", "btm", "tyo", "ytm"]]
            pA = 5 if par == 0 else 0
            for qi, (src, key) in enumerate([(dtf[0], "e1"), (dtf[3], "ncs"), (dtf[4], "tend"), (dtf[5], "fs")]):
                pg.mm(ps[pA][0:L, 256 + 8 * qi:256 + 8 * qi + 8], src[0:8, c0:c0 + L], ident[0:8, 0:8], True, True,
                      [key, "ident"], [pk(pA)])
            pg.act(tm[0:L, :], ps[pA][0:L, 256:288], AF.Identity, [pk(pA)], [TM])
            pg.op(V, "tensor_scalar", ["ident", "cs"], [DG], out=diag8[:, :], in0=ident[0:8, 0:8],
                  scalar1=dtf[2][:, c0 + L - 1:c0 + L], scalar2=None, op0=ALU.mult)
            pg.mm(ps[pA][:, 288:296], ones32[:, :], diag8[:, :], True, True, ["ones32", DG], [pk(pA)])
            pg.act(decb[:, :], ps[pA][:, 288:296], AF.Exp, [pk(pA)], [DC])
            for g in range(2):
                pg.mm(ps[pA][0:L, g * 128:g * 128 + L], xc[:, 4 + g, c0:c0 + L], xc[:, 6 + g, c0:c0 + L], True, True,
                      [("xc", 4 + g), ("xc", 6 + g)], [pk(pA)])
            for hg in range(2):
                bk = 3 if hg == 0 else 1
                for h in range(4 * hg, 4 * hg + 4):
                    col = (h % 4) * 128
                    pg.mm(ps[bk][:, col:col + L], sel[:, h * 128:(h + 1) * 128], dtf[2][:, c0:c0 + L], True, False, ["sel", "cs"], [pk(bk)])
                    pg.mm(ps[bk][:, col:col + L], identb[:, :], mnegb[:, 0:L], False, True, ["identb", "mnegb"], [pk(bk)])
                for h in range(4 * hg, 4 * hg + 4):
                    col = (h % 4) * 128
                    pg.act(E[h][0:L, 0:L], ps[bk][0:L, col:col + L], AF.Exp, [pk(bk), TM], [("E", h)], bias=tm[0:L, 8 + h:9 + h])
            for h in range(8):
                g = h // 4
                pg.op(V, "tensor_tensor", [("E", h), pk(pA)], [("G", h)], out=Gh[h][0:L, 0:L], in0=ps[pA][0:L, g * 128:g * 128 + L],
                      in1=E[h][0:L, 0:L], op=ALU.mult)
            for m in range(4):
                pg.op("tensor", "transpose", [("xc", m), "identb"], [("ps", "b")], psb[0:L, m * 128:(m + 1) * 128], xc[:, m, c0:c0 + L], identb[:, :])
            for g in range(2):
                pg.op("tensor", "transpose", [("xc", 4 + g), "identb"], [("ps", "b")], psb[0:L, 512 + g * 128:512 + (g + 1) * 128],
                      xc[:, 4 + g, c0:c0 + L], identb[:, :])
            psx = psb[0:L, 0:512].rearrange("p (h d) -> p h d", h=8)
            pg.op(V, "tensor_tensor", [("ps", "b"), TM], [XD], out=xdt[0:L, :, :], in0=psx,
                  in1=tm[0:L, 0:8].unsqueeze(2).broadcast_to([L, 8, 64]), op=ALU.mult)
            pg.op(V, "tensor_copy", [("ps", "b")], [BT], out=btm[0:L, :], in_=psb[0:L, 512:768])
            pg.op(G, "tensor_tensor", [XD, TM], [XW], out=xw[0:L, :, :], in0=xdt[0:L, :, :],
                  in1=tm[0:L, 16:24].unsqueeze(2).broadcast_to([L, 8, 64]), op=ALU.mult)
            for h in range(8):
                pg.mm(ps[6][0:L, h * 64:(h + 1) * 64], Gh[h][0:L, 0:L], xdt[0:L, h, :], True, True, [("G", h), XD], [pk(6)])
            for g in range(2):
                pg.mm(ps[7][0:L, g * 256:(g + 1) * 256], xc[:, 6 + g, c0:c0 + L], Hbf[:, g * 256:(g + 1) * 256], True, True,
                      [("xc", 6 + g), "Hbf"], [pk(7)])
            ps7v = ps[7][0:L, :].rearrange("p (h d) -> p h d", h=8)
            pg.op(V, "tensor_tensor", [pk(7), TM], [TY], out=tyo[0:L, :, :], in0=ps7v,
                  in1=tm[0:L, 24:32].unsqueeze(2).broadcast_to([L, 8, 64]), op=ALU.mult)
            pg.op(V, "tensor_tensor", [pk(6), TY], [YT], out=ytm[0:L, :], in0=ps[6][0:L, :],
                  in1=tyo[0:L, :, :].rearrange("p h d -> p (h d)"), op=ALU.add)
            for g in range(2):
                pg.mm(ps[2][:, g * 256:(g + 1) * 256], btm[0:L, g * 128:(g + 1) * 128],
                      xw[0:L, 4 * g:4 * g + 4, :].rearrange("p h d -> p (h d)"), True, True, [BT, XW], [pk(2)])
            pg.op(V, "tensor_tensor", ["H", DC], ["H"], out=Hs[:, :, :], in0=Hs[:, :, :],
                  in1=decb[:, :].unsqueeze(2).broadcast_to([128, 8, 64]), op=ALU.mult)
            pg.op(V, "tensor_tensor", ["H", pk(2)], ["H"], out=Hs[:, :, :], in0=Hs[:, :, :],
                  in1=ps[2][:, :].rearrange("p (h d) -> p h d", h=8), op=ALU.add)
            pg.op(G, "tensor_copy", ["H"], ["Hbf"], out=Hbf[:, :], in_=Hs[:, :, :].rearrange("p h d -> p (h d)"))
            for m in range(4):
                pg.op("tensor", "transpose", [YT, "identb"], [("ps", "b")], psb[:, m * 128:m * 128 + L], ytm[0:L, m * 128:(m + 1) * 128], identb[0:L, 0:L])
            for m in range(4):
                pg.act(yfm[:, m, c0:c0 + L], psb[:, m * 128:m * 128 + L], AF.Identity, [("ps", "b")], [("yfm", m)])
        for m in range(4):
            pg.op(V, "scalar_tensor_tensor", [("xc", m), "cvec", ("yfm", m)], [("yfm", m)], out=yfm[:, m, 0:nt], in0=xc[:, m, 0:nt],
                  scalar=cvec[:, m:m + 1], in1=yfm[:, m, 0:nt], op0=ALU.mult, op1=ALU.add)
            pg.op(G, "tensor_tensor", [("yfm", m), ("zs", m)], [("gg", m)], out=gg[:, m, 0:nt], in0=yfm[:, m, 0:nt], in1=zs[:, m, 0:nt], op=ALU.mult)
            pg.act(sq[:, m, 0:nt], gg[:, m, 0:nt], AF.Square, [("gg", m)], [("sq", m)])
        for g in range(2):
            pg.mm(P(0), ones[:, :], sq[:, 2 * g, 0:nt], True, False, [("sq", 2 * g), "ones"], [pk(0)])
            pg.mm(P(0), ones[:, :], sq[:, 2 * g + 1, 0:nt], False, True, [("sq", 2 * g + 1), "ones"], [pk(0)])
            pg.act(rs2[:, g, 0:nt], P(0), AF.Sqrt, [pk(0), "epsc"], [("rs2", g)], scale=1.0 / 256, bias=epsc[:, 0:1])
            pg.op(V, "reciprocal", [("rs2", g)], [("rs2", g)], out=rs2[:, g, 0:nt], in_=rs2[:, g, 0:nt])
        for m in range(4):
            pg.op(V, "scalar_tensor_tensor", [("gg", m), "cvec", ("rs2", m // 2)], [("GN", m)], out=GN[:, m, 0:nt], in0=gg[:, m, 0:nt],
                  scalar=cvec[:, 4 + m:5 + m], in1=rs2[:, m // 2, 0:nt], op0=ALU.mult, op1=ALU.mult)
        for d in range(8):
            p = 1 + (pr % 2); pr += 1
            for m in range(4):
                pg.mm(P(p), wout[:, m, d * 128:(d + 1) * 128], GN[:, m, 0:nt], m == 0, m == 3, [("GN", m), "wout"], [pk(p)])
            if d % 2 == 0:
                pg.act(ob[:, d, 0:nt], P(p), AF.Identity, [pk(p)], ["ob"])
            else:
                pg.op(V, "tensor_copy", [pk(p)], ["ob"], out=ob[:, d, 0:nt], in_=P(p))
        pg.dma("sync", part_v[:, :, t0:t0 + nt], ob[:, :, 0:nt], reads=["ob"], sem=("osem", 0))


M0_SHAPES = {'win': [1024, 384], 'nw': [128, 8], 'vecs': [128, 16], 'wa': [128, 128], 'wx': [128, 128], 'wglu': [128, 128], 'wout': [256, 1024], 'scol': [128, 3, 4], 'srow': [128, 3, 512], 'bT': [128, 2, 512], 'cT': [128, 2, 512], 'ident': [128, 128], 'kidx': [128, 512]}
M1_SHAPES = {'win': [1024, 1544], 'nw': [128, 8], 'cv': [128, 8, 5], 'hv': [8, 2], 'cvec': [128, 8], 'wout': [512, 1024], 'ident': [128, 128], 'mneg': [128, 512], 'seg': [8, 512], 'sel': [8, 1024]}


def build_fused(T, mlp_bs=410):
    nc = bass.Bass("TRN2", target_bir_lowering=False)
    din = lambda n, s: nc.dram_tensor(n, list(s), F32, kind="ExternalInput").ap()
    x_in = din("x_in", [D, T])
    out = nc.dram_tensor("out", [D, T], F32, kind="ExternalOutput").ap()
    parts = [nc.dram_tensor("part%d" % q, [D, T], F32).ap() for q in range(4)]
    h1 = nc.dram_tensor("h1", [D, T], F32).ap()
    PA = [{k: din("a%d_%s" % (q, k), s) for k, s in M0_SHAPES.items()} for q in range(4)]
    PB = [{k: din("b%d_%s" % (q, k), s) for k, s in M1_SHAPES.items()} for q in range(4)]
    mw = [{"w_up": din("m%d_w_up" % l, [D, DFF]), "w_dn": din("m%d_w_dn" % l, [DFF, D]), "nw": din("m%d_nw" % l, [128, 8])}
          for l in range(2)]
    nf = din("nf", [128, 8])
    pg = Prog(nc)
    ctx = Ctx(nc, pg)
    first = True
    for q in range(4):
        if not first:
            pg.phase(); ctx.reset()
        first = False
        phase_mix0(ctx, x_in, PA[q], parts[q], T)
    pg.phase(); ctx.reset()
    phase_mlp(ctx, x_in, parts, mw[0]["w_up"], mw[0]["w_dn"], mw[0]["nw"], nf, h1, T, False, mlp_bs)
    for q in range(4):
        pg.phase(); ctx.reset()
        phase_mix1(ctx, h1, PB[q], parts[q], T)
    pg.phase(); ctx.reset()
    phase_mlp(ctx, h1, parts, mw[1]["w_up"], mw[1]["w_dn"], mw[1]["nw"], nf, out, T, True, mlp_bs)
    pg.finish()
    stats = pg.emit()
    return nc, stats


import numpy as np

def lay8(v):
    return np.ascontiguousarray(v.reshape(-1, 128).T.astype(np.float32))

def prep_mix0(inp, q):
    f = np.float32
    w_in = inp["ev_w_in"][0]
    sl = slice(128 * q, 128 * q + 128)
    win = np.concatenate([w_in[:, 0:512][:, sl], w_in[:, 512:1024][:, sl], w_in[:, 1024:1536][:, sl]], axis=1)
    vecs = np.zeros((128, 16), f)
    vecs[:, 0:4] = inp["lru_conv_w"][0][:, sl].T
    vecs[:, 4] = inp["lru_conv_b"][0][sl]
    vecs[:, 5] = inp["lru_b_a"][0][sl]
    vecs[:, 6] = inp["lru_b_x"][0][sl]
    vecs[:, 7] = inp["lru_lambda"][0][sl]
    vecs[:, 8] = inp["s5_d"][0][sl]
    vecs[:, 9] = inp["s5_b_glu"][0][sl]
    def bd(blocks):
        n = blocks[0].shape[0]
        m = np.zeros((n * len(blocks), n * len(blocks)), f)
        for i, b in enumerate(blocks):
            m[i * n:(i + 1) * n, i * n:(i + 1) * n] = b
        return m
    wa = bd([inp["lru_w_a"][0][2 * q], inp["lru_w_a"][0][2 * q + 1]])
    wx = bd([inp["lru_w_x"][0][2 * q], inp["lru_w_x"][0][2 * q + 1]])
    wglu = bd([inp["s5_w_glu"][0][8 * q + g] for g in range(8)])
    w_out = inp["ev_w_out"][0]
    wout = np.concatenate([w_out[0:512][sl], w_out[512:1024][sl]], axis=0)
    scol = np.zeros((128, 3, 4), f)
    srow = np.zeros((128, 3, 512), f)
    bT = np.zeros((128, 2, 512), f)
    cT = np.zeros((128, 2, 512), f)
    for j in range(4):
        for slot in range(2):
            gl = 2 * j + slot
            g = 8 * q + gl
            sp = slice(slot * 64, slot * 64 + 64)
            scol[sp, 0, j] = inp["s5_a_re"][0][g]
            scol[sp, 1, j] = inp["s5_a_im"][0][g]
            scol[sp, 2, j] = inp["s5_log_dt"][0][g]
            cs = slice(j * 128 + slot * 64, j * 128 + slot * 64 + 64)
            srow[:, 0, cs] = inp["s5_a_re"][0][g][None, :]
            srow[:, 1, cs] = inp["s5_a_im"][0][g][None, :]
            srow[:, 2, cs] = inp["s5_log_dt"][0][g]
            ch = slice(gl * 16, gl * 16 + 16)
            bT[ch, 0, cs] = inp["s5_b_re"][0][g].T
            bT[ch, 1, cs] = inp["s5_b_im"][0][g].T
            cc = slice(j * 128 + gl * 16, j * 128 + gl * 16 + 16)
            cT[sp, 0, cc] = inp["s5_c_re"][0][g].T
            cT[sp, 1, cc] = inp["s5_c_im"][0][g].T
    return {
        "win": np.ascontiguousarray(win), "nw": lay8(inp["norm_mix"][0]), "vecs": vecs, "wa": wa, "wx": wx,
        "wglu": wglu, "wout": np.ascontiguousarray(wout), "scol": scol, "srow": srow, "bT": bT, "cT": cT,
        "ident": np.eye(128, dtype=f), "kidx": np.tile(np.arange(1, 513, dtype=f)[None, :], (128, 1)),
    }

def seq_fm(inp, b):
    return np.ascontiguousarray(np.concatenate([inp["meta_tokens"], inp["x"][b]], axis=0).T.astype(np.float32))


def prep_mix1(inp, q):
    f = np.float32
    w = inp["ssd_w_in"][0]
    cols = np.concatenate([np.arange(512 * q, 512 * q + 512), 2048 + np.arange(512 * q, 512 * q + 512),
                           4096 + np.arange(256 * q, 256 * q + 256), 5120 + np.arange(256 * q, 256 * q + 256),
                           6144 + np.arange(8 * q, 8 * q + 8)])
    win = np.ascontiguousarray(w[:, cols])
    cch = np.concatenate([np.arange(512 * q, 512 * q + 512), 2048 + np.arange(256 * q, 256 * q + 256),
                          3072 + np.arange(256 * q, 256 * q + 256)])
    cw = inp["ssd_conv_w"][0][:, cch]
    cb = inp["ssd_conv_b"][0][cch]
    cv = np.zeros((128, 8, 5), f)
    for m in range(8):
        cv[:, m, 0:4] = cw[:, m * 128:(m + 1) * 128].T
        cv[:, m, 4] = cb[m * 128:(m + 1) * 128]
    hv = np.stack([inp["ssd_dt_bias"][0][8 * q:8 * q + 8], inp["ssd_a_log"][0][8 * q:8 * q + 8]], axis=1).astype(f)
    cvec = np.zeros((128, 8), f)
    dd = np.repeat(inp["ssd_d"][0][8 * q:8 * q + 8], 64)
    nn = inp["ssd_norm"][0][512 * q:512 * q + 512]
    for m in range(4):
        cvec[:, m] = dd[m * 128:(m + 1) * 128]
        cvec[:, 4 + m] = nn[m * 128:(m + 1) * 128]
    wout = np.ascontiguousarray(inp["ssd_w_out"][0][512 * q:512 * q + 512])
    s_idx = np.arange(128)[:, None]
    l_idx = (np.arange(512) % 128)[None, :]
    mneg = np.where(l_idx >= s_idx, 0.0, -30000.0).astype(f)
    seg = np.ones((8, 512), f)
    seg[:, ::128] = 0.0
    sel = np.zeros((8, 8, 128), f)
    for h in range(8):
        sel[h, h, :] = 1.0
    return {"win": win, "nw": lay8(inp["norm_mix"][1]), "cv": cv, "hv": np.ascontiguousarray(hv), "cvec": cvec,
            "wout": wout, "ident": np.eye(128, dtype=f), "mneg": mneg, "seg": seg, "sel": sel.reshape(8, 1024)}

_CACHE = {}
T_SEQ = 16400


def kernel(**inputs):
    inp = {k: np.asarray(v) for k, v in inputs.items()}
    if "nc" not in _CACHE:
        _CACHE["nc"] = build_fused(T_SEQ, mlp_bs=410)[0]
    nc = _CACHE["nc"]
    base = {}
    for q in range(4):
        for k, v in prep_mix0(inp, q).items():
            base["a%d_%s" % (q, k)] = v
        for k, v in prep_mix1(inp, q).items():
            base["b%d_%s" % (q, k)] = v
    for l in range(2):
        base["m%d_w_up" % l] = np.ascontiguousarray(inp["mlp_w_up"][l])
        base["m%d_w_dn" % l] = np.ascontiguousarray(inp["mlp_w_down"][l])
        base["m%d_nw" % l] = lay8(inp["norm_mlp"][l])
    base["nf"] = lay8(inp["norm_final"])
    seqs = [seq_fm(inp, b) for b in range(2)]
    in_maps = []
    for c in range(8):
        m = dict(base)
        m["x_in"] = seqs[c % 2]
        in_maps.append(m)
    res = run_bass_kernel_spmd(nc, in_maps, core_ids=list(range(8))).results
    out = np.stack([np.ascontiguousarray(res[b]["out"][:, 16:].T) for b in range(2)], axis=0)
    return out.astype(np.float32)
```
